# Optimizing a Trainium2 kernel written in Bass

```python
import math
import jax, jax.numpy as jnp
from jax import lax
import numpy as np

D_MODEL = 2048
BATCH = 2
SEQ = 4096
DEPTH = 2

GRID_W = 64
CTX_LEN = 256
N_MIXERS = 2
MIXER_S5 = 0
MIXER_POOL = 1
S5_GROUP = 16
S5_GROUPS = D_MODEL // S5_GROUP
S5_STATE = 64
S5_DT_MIN = 0.001
S5_DT_MAX = 0.1
POOL_WINDOWS = (2, 4, 8, 16)
POOL_GROUPS = len(POOL_WINDOWS)
POOL_CH = D_MODEL // POOL_GROUPS
D_FF = ((8 * D_MODEL // 3 + 255) // 256) * 256
N_MOD = 6
RMS_EPS = 1e-6
POS_BASE = 10000.0
N_S5_LAYERS = (DEPTH + 1) // 2
N_POOL_LAYERS = DEPTH // 2

kernel_name = "hybrid_s5_pool_convffn_dit"


def _rmsnorm(x, g):
    xf = x.astype(jnp.float32)
    y = xf * lax.rsqrt(jnp.mean(xf * xf, axis=-1, keepdims=True) + RMS_EPS)
    return (y * g.astype(jnp.float32)).astype(x.dtype)


def _modulate(xn, shift, scale):
    return xn * (1 + scale) + shift


def _grid_pos_emb(n_tokens, dim):
    rows = n_tokens // GRID_W
    r, col = jnp.meshgrid(jnp.arange(rows, dtype=jnp.float32),
                          jnp.arange(GRID_W, dtype=jnp.float32), indexing="ij")
    quarter = dim // 4
    omega = 1.0 / (POS_BASE ** (jnp.arange(quarter, dtype=jnp.float32) / quarter))

    def enc(p):
        ang = p.reshape(-1, 1) * omega[None, :]
        return jnp.concatenate([jnp.sin(ang), jnp.cos(ang)], axis=-1)

    return jnp.concatenate([enc(r), enc(col)], axis=-1)


def _ssm_combine(e1, e2):
    a1r, a1i, b1r, b1i = e1
    a2r, a2i, b2r, b2i = e2
    return (a2r * a1r - a2i * a1i,
            a2r * a1i + a2i * a1r,
            a2r * b1r - a2i * b1i + b2r,
            a2r * b1i + a2i * b1r + b2i)


def _s5_discretize(lam_re, lam_im, log_step, b_re, b_im):
    f32 = jnp.float32
    lam_re, lam_im = lam_re.astype(f32), lam_im.astype(f32)
    b_re, b_im = b_re.astype(f32), b_im.astype(f32)
    dt = jnp.exp(log_step.astype(f32))[:, None]
    mag = jnp.exp(lam_re * dt)
    abar_re = mag * jnp.cos(lam_im * dt)
    abar_im = mag * jnp.sin(lam_im * dt)
    nr, ni = abar_re - 1.0, abar_im
    den = lam_re * lam_re + lam_im * lam_im
    fr = (nr * lam_re + ni * lam_im) / den
    fi = (ni * lam_re - nr * lam_im) / den
    bbar_re = fr[..., None] * b_re - fi[..., None] * b_im
    bbar_im = fr[..., None] * b_im + fi[..., None] * b_re
    return abar_re, abar_im, bbar_re, bbar_im


def _s5_states(u_g, abar_re, abar_im, bbar_re, bbar_im, reverse, h0=None):
    b_re = jnp.einsum("blgc,gpc->blgp", u_g, bbar_re)
    b_im = jnp.einsum("blgc,gpc->blgp", u_g, bbar_im)
    if h0 is not None:
        h0_re, h0_im = h0
        first = -1 if reverse else 0
        b_re = b_re.at[:, first].add(abar_re * h0_re - abar_im * h0_im)
        b_im = b_im.at[:, first].add(abar_re * h0_im + abar_im * h0_re)
    a_re = jnp.broadcast_to(abar_re, b_re.shape)
    a_im = jnp.broadcast_to(abar_im, b_im.shape)
    _, _, h_re, h_im = lax.associative_scan(_ssm_combine, (a_re, a_im, b_re, b_im),
                                            reverse=reverse, axis=1)
    return h_re, h_im


def _s5_readout(h_re, h_im, c_re, c_im):
    return (jnp.einsum("blgp,gcp->blgc", h_re, c_re.astype(jnp.float32))
            - jnp.einsum("blgp,gcp->blgc", h_im, c_im.astype(jnp.float32)))


def _s5_glu(y, glu_w, dtype):
    z = jax.nn.gelu(y).astype(dtype)
    val, gate = jnp.split(z @ glu_w, 2, axis=-1)
    return val * jax.nn.sigmoid(gate)


def _s5_mixer(u, u_c, lam_re, lam_im, log_step, b_re, b_im, c_re, c_im, d_skip, glu_w,
              with_ctx_out):
    bsz, n_lat, dim = u.shape
    n_ctx = u_c.shape[1]
    u_g = u.astype(jnp.float32).reshape(bsz, n_lat, S5_GROUPS, S5_GROUP)
    uc_g = u_c.astype(jnp.float32).reshape(bsz, n_ctx, S5_GROUPS, S5_GROUP)
    dsk = d_skip.astype(jnp.float32).reshape(S5_GROUPS, S5_GROUP)
    y = u_g * dsk
    y_c = uc_g * dsk if with_ctx_out else None
    for direction, rev in ((0, False), (1, True)):
        disc = _s5_discretize(lam_re[direction], lam_im[direction], log_step[direction],
                              b_re[direction], b_im[direction])
        hc_re, hc_im = _s5_states(uc_g, *disc, reverse=rev)
        end = 0 if rev else -1
        h_re, h_im = _s5_states(u_g, *disc, reverse=rev, h0=(hc_re[:, end], hc_im[:, end]))
        y = y + _s5_readout(h_re, h_im, c_re[direction], c_im[direction])
        if with_ctx_out:
            y_c = y_c + _s5_readout(hc_re, hc_im, c_re[direction], c_im[direction])
    out = _s5_glu(y.reshape(bsz, n_lat, dim), glu_w, u.dtype)
    out_c = _s5_glu(y_c.reshape(bsz, n_ctx, dim), glu_w, u.dtype) if with_ctx_out else None
    return out, out_c


def _pool_mixer(u, w, scale):
    bsz, n_tok, dim = u.shape
    uf = u.astype(jnp.float32)
    csum = jnp.concatenate([jnp.zeros((bsz, 1, dim), jnp.float32),
                            jnp.cumsum(uf, axis=1)], axis=1)
    t = jnp.arange(n_tok)
    groups = []
    for g, win in enumerate(POOL_WINDOWS):
        lo = jnp.clip(t - win // 2, 0, n_tok - 1)
        hi = jnp.clip(t + win // 2 - 1, 0, n_tok - 1)
        ch = slice(g * POOL_CH, (g + 1) * POOL_CH)
        cs = csum[:, :, ch]
        cnt = (hi - lo + 1).astype(jnp.float32)[None, :, None]
        groups.append((cs[:, hi + 1] - cs[:, lo]) / cnt - uf[:, :, ch])
    p = jnp.stack(groups, axis=2).astype(u.dtype)
    y = jnp.einsum("blgc,gcd->blgd", p, w).reshape(bsz, n_tok, dim)
    return y * scale


def _conv_ffn(u, up, conv, conv_b, down):
    h = u @ up
    hp = jnp.pad(h, ((0, 0), (1, 1), (0, 0)))
    h = hp[:, :-2] * conv[0] + hp[:, 1:-1] * conv[1] + hp[:, 2:] * conv[2] + conv_b
    val, gate = jnp.split(h, 2, axis=-1)
    return (jax.nn.silu(gate) * val) @ down


def setup_inputs(seed: int = 0) -> dict:
    key = jax.random.key(seed)
    ks = jax.random.split(key, 24)
    f32 = jnp.float32
    nrm = lambda k, shp: jax.random.normal(k, shp, f32)
    G, P, C = S5_GROUPS, S5_STATE, S5_GROUP
    s5_lam_re = -0.5 + 0.01 * nrm(ks[5], (N_S5_LAYERS, 2, G, P))
    s5_lam_im = math.pi * jnp.arange(P, dtype=f32) + 0.01 * nrm(ks[6], (N_S5_LAYERS, 2, G, P))
    s5_log_step = jax.random.uniform(ks[7], (N_S5_LAYERS, 2, G), f32,
                                     math.log(S5_DT_MIN), math.log(S5_DT_MAX))
    ffn_conv = 0.3 * nrm(ks[18], (DEPTH, 3, 2 * D_FF))
    ffn_conv = ffn_conv.at[:, 1].add(1.0)
    return {
        "x": nrm(ks[0], (BATCH, SEQ, D_MODEL)),
        "c": nrm(ks[1], (BATCH, D_MODEL)),
        "ctx": nrm(ks[2], (BATCH, CTX_LEN, D_MODEL)),
        "c_ctx": nrm(ks[3], (D_MODEL,)),
        "ada_w": 0.5 * D_MODEL ** -0.5 * nrm(ks[4], (DEPTH, D_MODEL, N_MOD * D_MODEL)),
        "ada_b": 0.01 * nrm(ks[8], (DEPTH, N_MOD * D_MODEL)),
        "norm_g": 1.0 + 0.05 * nrm(ks[9], (DEPTH, 4, D_MODEL)),
        "s5_lam_re": s5_lam_re,
        "s5_lam_im": s5_lam_im,
        "s5_log_step": s5_log_step,
        "s5_b_re": (2 * C) ** -0.5 * nrm(ks[10], (N_S5_LAYERS, 2, G, P, C)),
        "s5_b_im": (2 * C) ** -0.5 * nrm(ks[11], (N_S5_LAYERS, 2, G, P, C)),
        "s5_c_re": P ** -0.5 * nrm(ks[12], (N_S5_LAYERS, 2, G, C, P)),
        "s5_c_im": P ** -0.5 * nrm(ks[13], (N_S5_LAYERS, 2, G, C, P)),
        "s5_d": nrm(ks[14], (N_S5_LAYERS, D_MODEL)),
        "s5_glu_w": D_MODEL ** -0.5 * nrm(ks[15], (N_S5_LAYERS, D_MODEL, 2 * D_MODEL)),
        "pool_w": POOL_CH ** -0.5 * nrm(ks[16], (N_POOL_LAYERS, POOL_GROUPS, POOL_CH, POOL_CH)),
        "pool_scale": 1.0 + 0.1 * nrm(ks[17], (N_POOL_LAYERS, D_MODEL)),
        "ffn_up": D_MODEL ** -0.5 * nrm(ks[19], (DEPTH, D_MODEL, 2 * D_FF)),
        "ffn_conv": ffn_conv,
        "ffn_conv_b": 0.01 * nrm(ks[20], (DEPTH, 2 * D_FF)),
        "ffn_down": D_FF ** -0.5 * nrm(ks[21], (DEPTH, D_FF, D_MODEL)),
    }


def reference(x, c, ctx, c_ctx, ada_w, ada_b, norm_g, s5_lam_re, s5_lam_im, s5_log_step,
              s5_b_re, s5_b_im, s5_c_re, s5_c_im, s5_d, s5_glu_w, pool_w, pool_scale,
              ffn_up, ffn_conv, ffn_conv_b, ffn_down):
    n_lat = x.shape[1]
    x = x + _grid_pos_emb(n_lat, D_MODEL).astype(x.dtype)[None]
    cond = jax.nn.silu(c)
    cond_ctx = jax.nn.silu(c_ctx)
    h_ctx = ctx
    for i in range(DEPTH):
        kind = i % N_MIXERS
        k = i // N_MIXERS
        ctx_read = kind == MIXER_S5
        ctx_later = any(j % N_MIXERS == MIXER_S5 for j in range(i + 1, DEPTH))
        mod = jnp.split((cond @ ada_w[i] + ada_b[i])[:, None, :], N_MOD, axis=-1)
        u = _modulate(_rmsnorm(x, norm_g[i, 0]), mod[0], mod[1])
        if ctx_read or ctx_later:
            mod_c = jnp.split(cond_ctx @ ada_w[i] + ada_b[i], N_MOD)
            u_c = _modulate(_rmsnorm(h_ctx, norm_g[i, 0]), mod_c[0], mod_c[1])
        if kind == MIXER_S5:
            y, y_c = _s5_mixer(u, u_c, s5_lam_re[k], s5_lam_im[k], s5_log_step[k],
                               s5_b_re[k], s5_b_im[k], s5_c_re[k], s5_c_im[k],
                               s5_d[k], s5_glu_w[k], ctx_later)
        else:
            y = _pool_mixer(u, pool_w[k], pool_scale[k])
            y_c = _pool_mixer(u_c, pool_w[k], pool_scale[k]) if ctx_later else None
        x = x + mod[2] * _rmsnorm(y, norm_g[i, 1])
        f = _conv_ffn(_modulate(_rmsnorm(x, norm_g[i, 2]), mod[3], mod[4]),
                      ffn_up[i], ffn_conv[i], ffn_conv_b[i], ffn_down[i])
        x = x + mod[5] * _rmsnorm(f, norm_g[i, 3])
        if ctx_later:
            h_ctx = h_ctx + mod_c[2] * _rmsnorm(y_c, norm_g[i, 1])
            fc = _conv_ffn(_modulate(_rmsnorm(h_ctx, norm_g[i, 2]), mod_c[3], mod_c[4]),
                           ffn_up[i], ffn_conv[i], ffn_conv_b[i], ffn_down[i])
            h_ctx = h_ctx + mod_c[5] * _rmsnorm(fc, norm_g[i, 3])
    return x
```

```python
import math
from contextlib import ExitStack
import numpy as np
import ml_dtypes
import concourse.bass as bass
import concourse.mybir as mybir
from concourse.bass_utils import run_bass_kernel_spmd

F32 = mybir.dt.float32
BF16 = mybir.dt.bfloat16
I32 = mybir.dt.int32
AF = mybir.ActivationFunctionType
ALU = mybir.AluOpType

D = 2048
NB = 16
L = 4096
NCTX = 256
DFF = 5632
NJ = 44
MAGIC = 12582912.0
TWO_PI = 2.0 * math.pi
EPS = 1e-6
EPOCH = 3000
NDMA_CH = 12


class KB:
    ENGS = ("tensor", "vector", "scalar", "gpsimd", "sync")

    def __init__(self, nc):
        self.nc = nc
        self.prog = {e: [] for e in self.ENGS}
        self.cnt = {e: 0 for e in self.ENGS}
        self.sems = {e: [] for e in self.ENGS}
        self.waited = {e: {} for e in self.ENGS}
        self.last_w = {}
        self.readers = {}
        self.dma_ch = []
        self.dma_rr = 0
        self._stack = []
        for i in range(NDMA_CH):
            self.dma_ch.append({"sem": self._sem(f"s_dma_{i}"), "n": 0})

    def _sem(self, name):
        cm = self.nc.semaphore(name)
        s = cm.__enter__()
        self._stack.append(cm)
        return s

    def _eng_sem(self, e, epoch):
        while len(self.sems[e]) <= epoch:
            self.sems[e].append(self._sem(f"s_{e}_{len(self.sems[e])}"))
        return self.sems[e][epoch]

    def _deps(self, reads, writes):
        deps = []
        for b in reads:
            t = self.last_w.get(b)
            if t is not None:
                deps.append(t)
        for b in writes:
            t = self.last_w.get(b)
            if t is not None:
                deps.append(t)
            deps.extend(self.readers.get(b, []))
        return deps

    def _record(self, tok, reads, writes):
        for b in reads:
            self.readers.setdefault(b, []).append(tok)
        for b in writes:
            self.last_w[b] = tok
            self.readers[b] = []

    def _emit_waits(self, e, deps):
        need = {}
        for t in deps:
            if t[0] == "eng":
                key = ("eng", t[1], t[2]); v = t[3]
            else:
                key = ("dma", t[1]); v = t[2]
            if need.get(key, 0) < v:
                need[key] = v
        for key, v in need.items():
            if self.waited[e].get(key, 0) >= v:
                continue
            self.waited[e][key] = v
            sem = self._eng_sem(key[1], key[2]) if key[0] == "eng" else self.dma_ch[key[1]]["sem"]
            self.prog[e].append(lambda eng, sem=sem, v=v: eng.wait_ge(sem, v))

    def op(self, e, fn, reads=(), writes=()):
        self._emit_waits(e, self._deps(reads, writes))
        n = self.cnt[e]
        epoch, val = divmod(n, EPOCH)
        sem = self._eng_sem(e, epoch)
        self.cnt[e] = n + 1
        self.prog[e].append(lambda eng, fn=fn, sem=sem: fn(eng).then_inc(sem, 1))
        tok = ("eng", e, epoch, val + 1)
        self._record(tok, reads, writes)
        return tok

    def ops(self, e, fns, reads=(), writes=()):
        self._emit_waits(e, self._deps(reads, writes))
        for fn in fns[:-1]:
            self.prog[e].append(lambda eng, fn=fn: fn(eng))
        n = self.cnt[e]
        epoch, val = divmod(n, EPOCH)
        sem = self._eng_sem(e, epoch)
        self.cnt[e] = n + 1
        self.prog[e].append(lambda eng, fn=fns[-1], sem=sem: fn(eng).then_inc(sem, 1))
        tok = ("eng", e, epoch, val + 1)
        self._record(tok, reads, writes)
        return tok

    def dma(self, q, out, in_, reads=(), writes=(), **kw):
        deps = self._deps(reads, writes)
        ch = self.dma_rr
        self.dma_rr = (self.dma_rr + 1) % len(self.dma_ch)
        c = self.dma_ch[ch]
        if c["n"] > 0:
            deps.append(("dma", ch, 16 * c["n"]))
        self._emit_waits(q, deps)
        c["n"] += 1
        sem = c["sem"]
        self.prog[q].append(
            lambda eng, out=out, in_=in_, sem=sem, kw=kw: eng.dma_start(out=out, in_=in_, **kw).then_inc(sem, 16))
        tok = ("dma", ch, 16 * c["n"])
        self._record(tok, reads, writes)
        return tok

    def barrier(self):
        toks = []
        for e in self.ENGS:
            n = self.cnt[e]
            if n > 0:
                epoch, val = divmod(n - 1, EPOCH)
                toks.append(("eng", e, epoch, val + 1))
        for ch, c in enumerate(self.dma_ch):
            if c["n"] > 0:
                toks.append(("dma", ch, 16 * c["n"]))
        for e in self.ENGS:
            self._emit_waits(e, toks)
        self.last_w = {}
        self.readers = {}

    def finish(self):
        toks = []
        for e in self.ENGS:
            n = self.cnt[e]
            if n > 0:
                epoch, val = divmod(n - 1, EPOCH)
                toks.append(("eng", e, epoch, val + 1))
        for ch, c in enumerate(self.dma_ch):
            if c["n"] > 0:
                toks.append(("dma", ch, 16 * c["n"]))
        self._emit_waits("sync", toks)

    def run(self):
        with self.nc.Block() as block:
            for e in self.ENGS:
                prog = self.prog[e]
                if not prog:
                    continue

                def body(eng, prog=prog):
                    for f in prog:
                        f(eng)
                getattr(block, e)(body)

    def close(self):
        while self._stack:
            self._stack.pop().__exit__(None, None, None)


class Ctx:
    def __init__(self):
        self.nc = bass.Bass("TRN2", target_bir_lowering=False)
        self.es = ExitStack()
        self.k = KB(self.nc)
        self.uid = 0

    def din(self, name, shape, dt=F32):
        return self.nc.dram_tensor(name, list(shape), dt, kind="ExternalInput").ap()

    def dout(self, name, shape, dt=F32):
        return self.nc.dram_tensor(name, list(shape), dt, kind="ExternalOutput").ap()

    def sb(self, name, shape, dt=F32, es=None):
        return (es or self.es).enter_context(self.nc.sbuf_tensor(name, list(shape), dt))

    def ps(self, name, shape, dt=F32, es=None):
        return (es or self.es).enter_context(self.nc.psum_tensor(name, list(shape), dt))

    def done(self):
        self.k.finish()
        self.k.run()
        self.k.close()
        self.es.close()
        return self.nc

    def mm(self, out, lhsT, rhs, start, stop, reads, writes):
        return self.k.op("tensor", lambda e: e.matmul(out, lhsT=lhsT, rhs=rhs, start=start, stop=stop), reads, writes)

    def mmg(self, out, pairs, reads, writes):
        n = len(pairs)
        fns = []
        for i, (lh, rh) in enumerate(pairs):
            fns.append(lambda e, lh=lh, rh=rh, i=i: e.matmul(out, lhsT=lh, rhs=rh, start=(i == 0), stop=(i == n - 1)))
        return self.k.ops("tensor", fns, reads, writes)

    def tt(self, eng, out, in0, in1, op, reads, writes):
        return self.k.op(eng, lambda e: e.tensor_tensor(out=out, in0=in0, in1=in1, op=op), reads, writes)

    def ts(self, eng, out, in0, s1, s2, op0, op1, reads, writes):
        if op1 is None:
            return self.k.op(eng, lambda e: e.tensor_scalar(out=out, in0=in0, scalar1=s1, scalar2=None, op0=op0), reads, writes)
        return self.k.op(eng, lambda e: e.tensor_scalar(out=out, in0=in0, scalar1=s1, scalar2=s2, op0=op0, op1=op1), reads, writes)

    def stt(self, out, in0, scalar, in1, op0, op1, reads, writes):
        return self.k.op("vector", lambda e: e.scalar_tensor_tensor(out=out, in0=in0, scalar=scalar, in1=in1, op0=op0, op1=op1), reads, writes)

    def act(self, out, in_, func, reads, writes, scale=None, bias=None):
        kw = {}
        if scale is not None:
            kw["scale"] = scale
        if bias is not None:
            kw["bias"] = bias
        return self.k.op("scalar", lambda e: e.activation(out=out, in_=in_, func=func, **kw), reads, writes)

    def cp(self, eng, out, in_, reads, writes):
        if eng == "scalar":
            return self.k.op(eng, lambda e: e.copy(out=out, in_=in_), reads, writes)
        return self.k.op(eng, lambda e: e.tensor_copy(out=out, in_=in_), reads, writes)

    def memset(self, eng, out, val, writes):
        return self.k.op(eng, lambda e: e.memset(out, val), (), writes)

    def recip(self, out, in_, reads, writes):
        return self.k.op("vector", lambda e: e.reciprocal(out=out, in_=in_), reads, writes)

    def dma(self, q, out, in_, reads=(), writes=()):
        return self.k.dma(q, out, in_, reads, writes)


def ap_of(t, off, dims):
    a = t[:]
    return bass.AP(a.tensor, a.offset + off, [list(a.ap[0])] + [list(d) for d in dims])


NK = 548
GL = 16


def build_s5(debug=False):
    c = Ctx(); k = c.k; nc = c.nc
    dbg = []
    XF = c.din("XF", [2, GL, 128, NK], BF16)
    XB = c.din("XB", [2, GL, 128, NK], BF16)
    LAM = c.din("LAM", [128, 2, 2, GL])
    LST = c.din("LST", [128, 2, GL])
    B1 = c.din("B1", [128, 2, GL, 16]); B2 = c.din("B2", [128, 2, GL, 16])
    C1 = c.din("C1", [128, 2, GL, 16]); C2 = c.din("C2", [128, 2, GL, 16])
    DCOL = c.din("DCOL", [128, GL])
    CONS = c.din("CONS", [128, 4, 128])
    SGN = c.din("SGN", [128, 2])
    JFAC = c.din("JFAC", [128, 16, GL])
    IOTA = c.din("IOTA", [128, NK])
    YC = c.dout("YC", [2, GL, 128, 512])

    sb = c.sb
    lam = sb("lam", [128, 2, 2, GL]); lst = sb("lst", [128, 2, GL])
    b1 = sb("b1", [128, 2, GL, 16]); b2 = sb("b2", [128, 2, GL, 16])
    c1 = sb("c1", [128, 2, GL, 16]); c2 = sb("c2", [128, 2, GL, 16])
    dcol = sb("dcol", [128, GL]); cons = sb("cons", [128, 4, 128]); consb = sb("consb", [128, 2, 128], BF16)
    sgn = sb("sgn", [128, 2]); jfac = sb("jfac", [128, 16, GL]); iota = sb("iota", [128, NK])
    for dst, src, nm in ((lam, LAM, "lam"), (lst, LST, "lst"), (b1, B1, "b1"), (b2, B2, "b2"), (c1, C1, "c1"),
                         (c2, C2, "c2"), (dcol, DCOL, "dcol"), (cons, CONS, "cons"), (sgn, SGN, "sgn"),
                         (jfac, JFAC, "jfac"), (iota, IOTA, "iota")):
        k.dma("sync", dst[:], src[:], writes=[nm])
    k.op("vector", lambda e: e.tensor_copy(out=consb[:], in_=cons[:, 0:2, :]), reads=["cons"], writes=["consb"])
    identb = consb[:, 0, :]; pswapb = consb[:, 1, :]

    V = "vector"
    dt = sb("dt", [128, 2, GL]); ee = sb("ee", [128, 2, GL]); th = sb("th", [128, 2, GL])
    k.op("scalar", lambda e: e.activation(out=dt[:], in_=lst[:], func=AF.Exp), reads=["lst"], writes=["dt"])
    k.op(V, lambda e: e.tensor_tensor(out=ee[:], in0=lam[:, 0], in1=dt[:], op=ALU.mult), reads=["lam", "dt"], writes=["ee"])
    k.op(V, lambda e: e.tensor_tensor(out=th[:], in0=lam[:, 1], in1=dt[:], op=ALU.mult), reads=["lam", "dt"], writes=["th"])
    shp = [128, 2, 16, GL]
    ej = sb("ej", shp); yj = sb("yj", shp); tmp = sb("tmp", shp); tmp2 = sb("tmp2", shp)
    emag = sb("emag", shp); sn = sb("sn", shp); cs = sb("cs", shp); pr = sb("pr", shp); pi = sb("pi", shp)

    def bc_dg(t):
        return ap_of(t, 0, [[GL, 2], [0, 16], [1, GL]])

    def bc_j(t):
        return ap_of(t, 0, [[0, 2], [GL, 16], [1, GL]])
    k.op(V, lambda e: e.tensor_tensor(out=ej[:], in0=bc_dg(ee), in1=bc_j(jfac), op=ALU.mult), reads=["ee", "jfac"], writes=["ej"])
    k.op("scalar", lambda e: e.activation(out=emag[:], in_=ej[:], func=AF.Exp), reads=["ej"], writes=["emag"])
    k.op(V, lambda e: e.tensor_tensor(out=yj[:], in0=bc_dg(th), in1=bc_j(jfac), op=ALU.mult), reads=["th", "jfac"], writes=["yj"])

    def sin_of(dst, src, shift, nm_dst, nm_src):
        k.op(V, lambda e: e.tensor_scalar(out=tmp[:], in0=src[:], scalar1=1.0 / TWO_PI, scalar2=shift, op0=ALU.mult, op1=ALU.add),
             reads=[nm_src], writes=["tmp"])
        k.op(V, lambda e: e.tensor_scalar(out=tmp2[:], in0=tmp[:], scalar1=MAGIC, scalar2=MAGIC, op0=ALU.add, op1=ALU.subtract),
             reads=["tmp"], writes=["tmp2"])
        k.op(V, lambda e: e.tensor_tensor(out=tmp[:], in0=tmp[:], in1=tmp2[:], op=ALU.subtract), reads=["tmp", "tmp2"], writes=["tmp"])
        k.op("scalar", lambda e: e.activation(out=dst[:], in_=tmp[:], func=AF.Sin, scale=TWO_PI), reads=["tmp"], writes=[nm_dst])
    sin_of(sn, yj, 0.0, "sn", "yj")
    sin_of(cs, yj, 0.25, "cs", "yj")
    k.op(V, lambda e: e.tensor_tensor(out=pr[:], in0=emag[:], in1=cs[:], op=ALU.mult), reads=["emag", "cs"], writes=["pr"])
    k.op(V, lambda e: e.tensor_tensor(out=pi[:], in0=emag[:], in1=sn[:], op=ALU.mult), reads=["emag", "sn"], writes=["pi"])
    s3 = [128, 2, GL]
    nr = sb("nr", s3); den = sb("den", s3); t3 = sb("t3", s3); t4 = sb("t4", s3); fr = sb("fr", s3); fi = sb("fi", s3)
    ar = pr[:, :, 8, :]; ai = pi[:, :, 8, :]
    lr = lam[:, 0]; li = lam[:, 1]
    k.op(V, lambda e: e.tensor_scalar(out=nr[:], in0=ar, scalar1=-1.0, scalar2=None, op0=ALU.add), reads=["pr"], writes=["nr"])
    k.op(V, lambda e: e.tensor_tensor(out=den[:], in0=lr, in1=lr, op=ALU.mult), reads=["lam"], writes=["den"])
    k.op(V, lambda e: e.tensor_tensor(out=t3[:], in0=li, in1=li, op=ALU.mult), reads=["lam"], writes=["t3"])
    k.op(V, lambda e: e.tensor_tensor(out=den[:], in0=den[:], in1=t3[:], op=ALU.add), reads=["den", "t3"], writes=["den"])
    k.op(V, lambda e: e.reciprocal(out=den[:], in_=den[:]), reads=["den"], writes=["den"])
    k.op(V, lambda e: e.tensor_tensor(out=t3[:], in0=nr[:], in1=lr, op=ALU.mult), reads=["nr", "lam"], writes=["t3"])
    k.op(V, lambda e: e.tensor_tensor(out=t4[:], in0=ai, in1=li, op=ALU.mult), reads=["pi", "lam"], writes=["t4"])
    k.op(V, lambda e: e.tensor_tensor(out=t3[:], in0=t3[:], in1=t4[:], op=ALU.add), reads=["t3", "t4"], writes=["t3"])
    k.op(V, lambda e: e.tensor_tensor(out=fr[:], in0=t3[:], in1=den[:], op=ALU.mult), reads=["t3", "den"], writes=["fr"])
    k.op(V, lambda e: e.tensor_tensor(out=t3[:], in0=ai, in1=lr, op=ALU.mult), reads=["pi", "lam"], writes=["t3"])
    k.op(V, lambda e: e.tensor_tensor(out=t4[:], in0=nr[:], in1=li, op=ALU.mult), reads=["nr", "lam"], writes=["t4"])
    k.op(V, lambda e: e.tensor_tensor(out=t3[:], in0=t3[:], in1=t4[:], op=ALU.subtract), reads=["t3", "t4"], writes=["t3"])
    k.op(V, lambda e: e.tensor_tensor(out=fi[:], in0=t3[:], in1=den[:], op=ALU.mult), reads=["t3", "den"], writes=["fi"])
    qa = sb("qa", shp); qb = sb("qb", shp); t1t = sb("t1t", shp); t2t = sb("t2t", shp)
    k.op(V, lambda e: e.tensor_tensor(out=tmp[:], in0=pr[:], in1=bc_dg(fr), op=ALU.mult), reads=["pr", "fr"], writes=["tmp"])
    k.op(V, lambda e: e.tensor_tensor(out=tmp2[:], in0=pi[:], in1=bc_dg(fi), op=ALU.mult), reads=["pi", "fi"], writes=["tmp2"])
    k.op(V, lambda e: e.tensor_tensor(out=qa[:], in0=tmp[:], in1=tmp2[:], op=ALU.subtract), reads=["tmp", "tmp2"], writes=["qa"])
    k.op(V, lambda e: e.tensor_tensor(out=tmp[:], in0=pr[:], in1=bc_dg(fi), op=ALU.mult), reads=["pr", "fi"], writes=["tmp"])
    k.op(V, lambda e: e.tensor_tensor(out=tmp2[:], in0=pi[:], in1=bc_dg(fr), op=ALU.mult), reads=["pi", "fr"], writes=["tmp2"])
    k.op(V, lambda e: e.tensor_tensor(out=tmp[:], in0=tmp[:], in1=tmp2[:], op=ALU.add), reads=["tmp", "tmp2"], writes=["tmp"])
    k.op(V, lambda e: e.tensor_scalar(out=qb[:], in0=tmp[:], scalar1=sgn[:, 1:2], scalar2=None, op0=ALU.mult), reads=["tmp", "sgn"], writes=["qb"])
    k.op(V, lambda e: e.tensor_scalar(out=t1t[:], in0=pr[:], scalar1=sgn[:, 0:1], scalar2=None, op0=ALU.mult), reads=["pr", "sgn"], writes=["t1t"])
    k.op(V, lambda e: e.tensor_scalar(out=t2t[:], in0=pi[:], scalar1=-1.0, scalar2=None, op0=ALU.mult), reads=["pi"], writes=["t2t"])
    ph = sb("ph", s3); phs = sb("phs", s3)
    k.op(V, lambda e: e.tensor_scalar(out=ph[:], in0=th[:], scalar1=8.0 / TWO_PI, scalar2=None, op0=ALU.mult), reads=["th"], writes=["ph"])
    k.op(V, lambda e: e.tensor_scalar(out=phs[:], in0=ph[:], scalar1=sgn[:, 0:1], scalar2=None, op0=ALU.mult), reads=["ph", "sgn"], writes=["phs"])
    rho = emag

    msh = [128, GL, 8, 16]
    mt1 = sb("mt1", msh); mt2 = sb("mt2", msh)
    Lm = [sb(f"Lm{d}", msh, BF16) for d in range(2)]
    Rm = [sb(f"Rm{d}", msh, BF16) for d in range(2)]
    Wo = [sb(f"Wo{d}", msh, BF16) for d in range(2)]
    Wos = [sb(f"Wos{d}", [128, GL, 128], BF16) for d in range(2)]
    Wi = [sb(f"Wi{d}", [128, GL, 128], BF16) for d in range(2)]
    Wis = [sb(f"Wis{d}", [128, GL, 128], BF16) for d in range(2)]
    Mg = sb("Mg", [128, GL, 128], BF16)

    def tab_ap(t, d, jj0, jstep):
        return ap_of(t, d * 16 * GL + jj0 * GL, [[1, GL], [jstep * GL, 8], [0, 16]])

    def par_ap(t, d):
        return ap_of(t, d * GL * 16, [[16, GL], [0, 8], [1, 16]])

    def gen(dst, nm, ta, tb, na, nb_, pa, pb, npa, npb, d, jj0, jstep):
        k.op(V, lambda e: e.tensor_tensor(out=mt1[:], in0=tab_ap(ta, d, jj0, jstep), in1=par_ap(pa, d), op=ALU.mult),
             reads=[na, npa], writes=["mt1"])
        k.op(V, lambda e: e.tensor_tensor(out=mt2[:], in0=tab_ap(tb, d, jj0, jstep), in1=par_ap(pb, d), op=ALU.mult),
             reads=[nb_, npb], writes=["mt2"])
        k.op(V, lambda e: e.tensor_tensor(out=dst[:], in0=mt1[:], in1=mt2[:], op=ALU.add), reads=["mt1", "mt2"], writes=[nm])
    gen(Lm[0], "Lm0", qa, qb, "qa", "qb", b1, b2, "b1", "b2", 0, 14, -1)
    gen(Rm[0], "Rm0", t1t, t2t, "t1t", "t2t", c1, c2, "c1", "c2", 0, 0, 1)
    gen(Wo[0], "Wo0", t1t, t2t, "t1t", "t2t", c1, c2, "c1", "c2", 0, 8, 1)
    gen(Lm[1], "Lm1", qa, qb, "qa", "qb", b1, b2, "b1", "b2", 1, 7, 1)
    gen(Rm[1], "Rm1", t1t, t2t, "t1t", "t2t", c1, c2, "c1", "c2", 1, 7, -1)
    gen(Wo[1], "Wo1", t1t, t2t, "t1t", "t2t", c1, c2, "c1", "c2", 1, 15, -1)

    pg = [c.ps(f"pg{i}", [128, 512]) for i in range(2)]
    gi = 0
    mtmp = sb("mtmp", [128, 128]); mtmp2 = sb("mtmp2", [128, 128])
    for g in range(GL):
        for d in range(2):
            Lg = Lm[d][:, g].rearrange("p s c -> p (s c)")
            Wog = Wo[d][:, g].rearrange("p s c -> p (s c)")
            p = pg[gi % 2]; pn = f"pg{gi % 2}"; gi += 1
            k.op("tensor", lambda e, p=p, Lg=Lg: e.matmul(p[:, 0:128], lhsT=Lg, rhs=identb, start=True, stop=True),
                 reads=[f"Lm{d}", "consb"], writes=[pn])
            k.op("tensor", lambda e, p=p, Lg=Lg: e.matmul(p[:, 128:256], lhsT=Lg, rhs=pswapb, start=True, stop=True),
                 reads=[f"Lm{d}", "consb"], writes=[pn])
            k.op("tensor", lambda e, p=p, Wog=Wog: e.matmul(p[:, 256:384], lhsT=pswapb, rhs=Wog, start=True, stop=True),
                 reads=[f"Wo{d}", "consb"], writes=[pn])
            k.op("scalar", lambda e, p=p, d=d, g=g: e.copy(out=Wi[d][:, g, :], in_=p[:, 0:128]), reads=[pn], writes=[f"Wi{d}"])
            k.op("scalar", lambda e, p=p, d=d, g=g: e.copy(out=Wis[d][:, g, :], in_=p[:, 128:256]), reads=[pn], writes=[f"Wis{d}"])
            k.op("scalar", lambda e, p=p, d=d, g=g: e.copy(out=Wos[d][:, g, :], in_=p[:, 256:384]), reads=[pn], writes=[f"Wos{d}"])
        p = pg[gi % 2]; pn = f"pg{gi % 2}"; gi += 1
        for d in range(2):
            Lg = Lm[d][:, g].rearrange("p s c -> p (s c)")
            Rg = Rm[d][:, g].rearrange("p s c -> p (s c)")
            k.op("tensor", lambda e, p=p, Lg=Lg, Rg=Rg, d=d: e.matmul(p[:, 128 * d:128 * d + 128], lhsT=Lg, rhs=Rg, start=True, stop=True),
                 reads=[f"Lm{d}", f"Rm{d}"], writes=[pn])
        k.op(V, lambda e, p=p: e.tensor_tensor(out=mtmp[:], in0=p[:, 0:128], in1=cons[:, 2, :], op=ALU.mult), reads=[pn, "cons"], writes=["mtmp"])
        k.op(V, lambda e, p=p: e.tensor_tensor(out=mtmp2[:], in0=p[:, 128:256], in1=cons[:, 3, :], op=ALU.mult), reads=[pn, "cons"], writes=["mtmp2"])
        k.op(V, lambda e: e.tensor_tensor(out=mtmp[:], in0=mtmp[:], in1=mtmp2[:], op=ALU.add), reads=["mtmp", "mtmp2"], writes=["mtmp"])
        k.op(V, lambda e, g=g: e.scalar_tensor_tensor(out=Mg[:, g, :], in0=cons[:, 0, :], scalar=dcol[:, g:g + 1], in1=mtmp[:],
                                                      op0=ALU.mult, op1=ALU.add), reads=["cons", "dcol", "mtmp"], writes=["Mg"])

    pS = c.ps("pS", [128, 1024]); pSw = c.ps("pSw", [128, 1024]); pY = c.ps("pY", [128, 512])
    xf = [sb(f"xf{i}", [128, NK], BF16) for i in range(2)]
    xb = [sb(f"xb{i}", [128, NK], BF16) for i in range(2)]
    ctabs = [sb(f"ctab{i}", [128, NK]) for i in range(2)]; stabs = [sb(f"stab{i}", [128, NK]) for i in range(2)]; ty = sb("ty", [128, NK]); tr = sb("tr", [128, NK])
    sp = sb("sp", [128, NK]); sp2 = sb("sp2", [128, NK]); ggs = [sb(f"gg{i}", [128, NK]) for i in range(2)]
    GC = [[sb(f"GC{b}{d}", [128, NK], BF16) for d in range(2)] for b in range(2)]
    GS = [[sb(f"GS{b}{d}", [128, NK], BF16) for d in range(2)] for b in range(2)]
    yo = [sb(f"yo{i}", [128, 512]) for i in range(2)]
    for g in range(GL):
        for b in range(2):
            k.dma("sync", xf[b][:], XF[b, g], writes=[f"xf{b}"])
            k.dma("sync", xb[b][:], XB[b, g], writes=[f"xb{b}"])
        for d in range(2):
            ctab = ctabs[d]; stab = stabs[d]; cn = f"ctab{d}"; sn_ = f"stab{d}"
            def table(dst, nm, phcol, shift):
                k.op(V, lambda e: e.tensor_scalar(out=ty[:], in0=iota[:], scalar1=phcol, scalar2=shift, op0=ALU.mult, op1=ALU.add),
                     reads=["iota", "ph", "phs"], writes=["ty"])
                k.op(V, lambda e: e.tensor_scalar(out=tr[:], in0=ty[:], scalar1=MAGIC, scalar2=MAGIC, op0=ALU.add, op1=ALU.subtract),
                     reads=["ty"], writes=["tr"])
                k.op(V, lambda e: e.tensor_tensor(out=ty[:], in0=ty[:], in1=tr[:], op=ALU.subtract), reads=["ty", "tr"], writes=["ty"])
                k.op("scalar", lambda e: e.activation(out=dst[:], in_=ty[:], func=AF.Sin, scale=TWO_PI), reads=["ty"], writes=[nm])
            table(stab, sn_, phs[:, d, g:g + 1], 0.0)
            table(ctab, cn, ph[:, d, g:g + 1], 0.25)
            rho_bc = ap_of(emag, d * 16 * GL + 15 * GL + g, [[0, NK]])
            for b in range(2):
                gg = ggs[b]; gn = f"gg{b}"
                X = xf[b] if d == 0 else xb[b]
                xn = f"xf{b}" if d == 0 else f"xb{b}"
                for (P, pn, W, wn) in ((pS, "pS", Wi[d], f"Wi{d}"), (pSw, "pSw", Wis[d], f"Wis{d}")):
                    k.op("tensor", lambda e, P=P, W=W, X=X, g=g: e.matmul(P[:, 0:512], lhsT=W[:, g, :], rhs=X[:, 0:512], start=True, stop=True),
                         reads=[wn, xn], writes=[pn])
                    k.op("tensor", lambda e, P=P, W=W, X=X, g=g: e.matmul(P[:, 512:NK], lhsT=W[:, g, :], rhs=X[:, 512:NK], start=True, stop=True),
                         reads=[wn, xn], writes=[pn])
                k.op(V, lambda e, ctab=ctab: e.tensor_tensor(out=sp[:], in0=pS[:, 0:NK], in1=ctab[:], op=ALU.mult), reads=["pS", cn], writes=["sp"])
                k.op(V, lambda e, stab=stab: e.tensor_tensor(out=sp2[:], in0=pSw[:, 0:NK], in1=stab[:], op=ALU.mult), reads=["pSw", sn_], writes=["sp2"])
                k.op(V, lambda e: e.tensor_tensor(out=sp[:], in0=sp[:], in1=sp2[:], op=ALU.add), reads=["sp", "sp2"], writes=["sp"])
                k.op(V, lambda e, rho_bc=rho_bc, gg=gg: e.tensor_tensor_scan(out=gg[:], data0=rho_bc, data1=sp[:], initial=0.0, op0=ALU.mult, op1=ALU.add),
                     reads=["emag", "sp"], writes=[gn])
                k.op(V, lambda e, b=b, d=d, gg=gg, ctab=ctab: e.tensor_tensor(out=GC[b][d][:], in0=gg[:], in1=ctab[:], op=ALU.mult), reads=[gn, cn], writes=[f"GC{b}{d}"])
                k.op("gpsimd", lambda e, b=b, d=d, gg=gg, stab=stab: e.tensor_tensor(out=GS[b][d][:], in0=gg[:], in1=stab[:], op=ALU.mult), reads=[gn, sn_], writes=[f"GS{b}{d}"])
        for b in range(2):
            def rev(t):
                return ap_of(t, 542, [[-1, 512]])
            mm = [(Mg[:, g, :], xf[b][:, 32:544], ["Mg", f"xf{b}"]),
                  (Wo[0][:, g].rearrange("p s c -> p (s c)"), GC[b][0][:, 31:543], ["Wo0", f"GC{b}0"]),
                  (Wos[0][:, g, :], GS[b][0][:, 31:543], ["Wos0", f"GS{b}0"]),
                  (Wo[1][:, g].rearrange("p s c -> p (s c)"), rev(GC[b][1]), ["Wo1", f"GC{b}1"]),
                  (Wos[1][:, g, :], rev(GS[b][1]), ["Wos1", f"GS{b}1"])]
            for i, (lh, rh, rd) in enumerate(mm):
                k.op("tensor", lambda e, lh=lh, rh=rh, i=i: e.matmul(pY[:], lhsT=lh, rhs=rh, start=(i == 0), stop=(i == 4)),
                     reads=rd, writes=["pY"])
            y = yo[b]
            k.op("scalar", lambda e, y=y: e.copy(out=y[:], in_=pY[:]), reads=["pY"], writes=[f"yo{b}"])
            k.dma("sync", YC[b, g], y[:], reads=[f"yo{b}"], writes=[f"YC{b}{g}"])
    if debug:
        for nm, t, shp, dt_ in (("pr", pr, shp, F32), ("pi", pi, shp, F32), ("qa", qa, shp, F32), ("qb", qb, shp, F32),
                                ("fr", fr, s3, F32), ("fi", fi, s3, F32), ("Lm0", Lm[0], msh, BF16), ("Rm0", Rm[0], msh, BF16),
                                ("Wo0", Wo[0], msh, BF16), ("Mg", Mg, [128, GL, 128], BF16), ("Wi0", Wi[0], [128, GL, 128], BF16),
                                ("Wis0", Wis[0], [128, GL, 128], BF16), ("Wos0", Wos[0], [128, GL, 128], BF16),
                                ("ctab1", ctabs[1], [128, NK], F32), ("stab1", stabs[1], [128, NK], F32), ("gg1", ggs[1], [128, NK], F32),
                                ("sp", sp, [128, NK], F32), ("GC00", GC[0][0], [128, NK], BF16)):
            o = c.dout("dbg_" + nm, shp, dt_)
            k.dma("sync", o[:], t[:], reads=[nm], writes=["dbg_" + nm])
    return c.done()


def s5_consts():
    idx = np.arange(128)
    ident = np.eye(128, dtype=np.float32)
    pswap = np.zeros((128, 128), np.float32); pswap[idx, (idx + 64) % 128] = 1
    s_of = idx // 16
    maskF = (s_of[None, :] >= s_of[:, None]).astype(np.float32)
    maskB = (s_of[None, :] <= s_of[:, None]).astype(np.float32)
    cons = np.stack([ident, pswap, maskF, maskB], 1)
    sg = np.where(idx < 64, 1.0, -1.0).astype(np.float32)
    sgn = np.stack([sg, -sg], 1)
    jfac = np.broadcast_to((np.arange(16, dtype=np.float32) - 7)[None, :, None], (128, 16, GL)).copy()
    iota = np.broadcast_to(np.arange(NK, dtype=np.float32)[None], (128, NK)).copy()
    return cons, sgn, jfac, iota


def s5_in_maps(u, uc, inp):
    cons, sgn, jfac, iota = s5_consts()
    bf = u.dtype
    XF = np.zeros((2, 128, 128, NK), bf); XB = np.zeros((2, 128, 128, NK), bf)
    for b in range(2):
        seq = np.concatenate([uc[b], u[b]], 0).reshape(544, 8, 128, 16)
        chb = np.concatenate([seq[:32][::-1], seq[32:][::-1]], 0)
        XF[b, :, :, :544] = seq.transpose(2, 1, 3, 0).reshape(128, 128, 544)
        XB[b, :, :, :544] = chb.transpose(2, 1, 3, 0).reshape(128, 128, 544)
    lre = inp["s5_lam_re"][0]; lim = inp["s5_lam_im"][0]
    def rep(a):
        t = a.transpose(2, 0, 1)
        return np.concatenate([t, t], 0)
    LAM = np.stack([rep(lre), rep(lim)], 1)
    LST = np.broadcast_to(inp["s5_log_step"][0][None], (128, 2, 128)).copy()
    bre = inp["s5_b_re"][0].transpose(2, 0, 1, 3); bim = inp["s5_b_im"][0].transpose(2, 0, 1, 3)
    cre = inp["s5_c_re"][0].transpose(3, 0, 1, 2); cim = inp["s5_c_im"][0].transpose(3, 0, 1, 2)
    B1 = np.concatenate([bre, bim], 0); B2 = np.concatenate([bim, bre], 0)
    C1 = np.concatenate([cre, cim], 0); C2 = np.concatenate([cim, cre], 0)
    dd = inp["s5_d"][0].reshape(128, 16)
    DCOL = np.tile(dd.T, (8, 1))
    maps = []
    for core in range(8):
        gs = slice(16 * core, 16 * core + 16)
        maps.append({
            "XF": np.ascontiguousarray(XF[:, gs]), "XB": np.ascontiguousarray(XB[:, gs]),
            "LAM": np.ascontiguousarray(LAM[..., gs]).astype(np.float32), "LST": np.ascontiguousarray(LST[..., gs]).astype(np.float32),
            "B1": np.ascontiguousarray(B1[:, :, gs]), "B2": np.ascontiguousarray(B2[:, :, gs]),
            "C1": np.ascontiguousarray(C1[:, :, gs]), "C2": np.ascontiguousarray(C2[:, :, gs]),
            "DCOL": np.ascontiguousarray(DCOL[:, gs]).astype(np.float32),
            "CONS": cons, "SGN": sgn, "JFAC": jfac, "IOTA": iota})
    return maps


def s5_gather(res):
    y = np.zeros((2, L, D), np.float32)
    for core in range(8):
        yc = res[core]["YC"].reshape(2, 16, 8, 16, 512)
        y[:, :, 256 * core:256 * core + 256] = yc.transpose(0, 4, 2, 1, 3).reshape(2, L, 256)
    return y


def build_mod():
    c = Ctx(); k = c.k
    CND = c.din("CND", [128, 16, 4])
    AW = c.din("AW", [2, 2048, 1536])
    AB = c.din("AB", [2, 128, 12])
    MOD = c.dout("MOD", [2, 128, 12, 4])
    cnd = c.sb("cnd", [128, 16, 4]); ab = c.sb("ab", [2, 128, 12]) if False else None
    abt = [c.sb(f"abt{l}", [128, 12]) for l in range(2)]
    c.dma("sync", cnd[:], CND[:], writes=["cnd"])
    for l in range(2):
        c.dma("sync", abt[l][:], AB[l], writes=[f"abt{l}"])
    cond = c.sb("cond", [128, 16, 4])
    c.act(cond[:], cnd[:], AF.Silu, ["cnd"], ["cond"])
    wb = [c.sb(f"wb{i}", [128, 16, 512]) for i in range(4)]
    pm = [c.ps(f"pm{i}", [128, 512]) for i in range(2)]
    mo = c.sb("mo", [128, 2, 12, 4])
    n = 0
    for l in range(2):
        for q in range(3):
            w = wb[n % 4]; wn = f"wb{n % 4}"; p = pm[n % 2]; pn = f"pm{n % 2}"
            c.dma("sync" if n % 2 == 0 else "gpsimd", w[:], AW[l, :, 512 * q:512 * q + 512].rearrange("(kc p) n -> p kc n", p=128), writes=[wn])
            n += 1
            for j in range(4):
                c.mmg(p[:, 4 * j:4 * j + 4], [(w[:, kc, 128 * j:128 * j + 128], cond[:, kc, :]) for kc in range(16)], [wn, "cond"], [pn])
            for j in range(4):
                blk = 4 * q + j
                c.ts("vector", mo[:, l, blk, :], p[:, 4 * j:4 * j + 4], abt[l][:, blk:blk + 1], None, ALU.add, None, [pn, f"abt{l}"], ["mo"])
    c.dma("sync", MOD.rearrange("l p b r -> p l b r"), mo[:], reads=["mo"], writes=["MOD"])
    return c.done()


def col_tiles(nt):
    n = (nt + 511) // 512
    sz = (nt + n - 1) // n
    return [(i * sz, min(nt, (i + 1) * sz)) for i in range(n)]


def rms_rstd(c, X, xname, nt, rstd, rname, ones, sq, psq):
    cts = col_tiles(nt)
    for blk in range(NB):
        s_ = sq[blk % 2]; sn_ = f"sq{blk % 2}"
        c.act(s_[:, 0:nt], X[:, blk, :], AF.Square, [xname], [sn_])
        for i, (c0, c1) in enumerate(cts):
            c.mm(psq[i][:, 0:c1 - c0], ones[:], s_[:, c0:c1], blk == 0, blk == NB - 1, ["ones", sn_], [f"psq{i}"])
    for i, (c0, c1) in enumerate(cts):
        c.ts("vector", rstd[:, c0:c1], psq[i][:, 0:c1 - c0], 1.0 / D, EPS, ALU.mult, ALU.add, [f"psq{i}"], [rname])
    c.act(rstd[:, 0:nt], rstd[:, 0:nt], AF.Sqrt, [rname], [rname])
    c.recip(rstd[:, 0:nt], rstd[:, 0:nt], [rname], [rname])


def build_prep():
    c = Ctx(); k = c.k
    NT = 1024; NC = 64
    XT = c.din("XT", [NB, 128, NT]); CT = c.din("CT", [NB, 128, NC])
    MV = c.din("MV", [128, NB, 4])
    G0 = c.din("G0", [128, NB])
    RIDX = c.din("RIDX", [128, NT]); CIDX = c.din("CIDX", [128, NT]); JIDX = c.din("JIDX", [128, 4])
    XPT = c.dout("XPT", [NB, 128, NT]); UT = c.dout("UT", [NB, 128, NT], BF16); UCT = c.dout("UCT", [NB, 128, NC], BF16)
    xp = c.sb("xp", [128, NB, NT]); xc = c.sb("xc", [128, NB, NC])
    mv = c.sb("mv", [128, NB, 4]); g0 = c.sb("g0", [128, NB]); ridx = c.sb("ridx", [128, NT]); cidx = c.sb("cidx", [128, NT])
    jidx = c.sb("jidx", [128, 4]); om = c.sb("om", [128, 4]); ones = c.sb("ones", [128, 128])
    c.memset("vector", ones[:], 1.0, ["ones"])
    for blk in range(NB):
        c.dma("sync", xp[:, blk, :], XT[blk], writes=[f"xp{blk}"])
    c.dma("sync", xc[:], CT.rearrange("b p t -> p b t"), writes=["xc"])
    for dst, src, nm in ((mv, MV, "mv"), (g0, G0, "g0"), (ridx, RIDX, "ridx"), (cidx, CIDX, "cidx"), (jidx, JIDX, "jidx")):
        c.dma("sync", dst[:], src[:], writes=[nm])
    c.act(om[:], jidx[:], AF.Exp, ["jidx"], ["om"], scale=-math.log(10000.0) / 512.0)
    c.ts("vector", om[:], om[:], 1.0 / TWO_PI, None, ALU.mult, None, ["om"], ["om"])
    ty = [c.sb(f"ty{i}", [128, NT]) for i in range(2)]; tr = [c.sb(f"tr{i}", [128, NT]) for i in range(2)]
    for blk in range(NB):
        idx, inm = (ridx, "ridx") if blk < 8 else (cidx, "cidx")
        shift = 0.25 if (blk // 4) % 2 == 1 else 0.0
        y = ty[blk % 2]; yn = f"ty{blk % 2}"; r = tr[blk % 2]; rn = f"tr{blk % 2}"
        c.ts("vector", y[:], idx[:], om[:, blk % 4:blk % 4 + 1], shift, ALU.mult, ALU.add, [inm, "om"], [yn])
        c.ts("vector", r[:], y[:], MAGIC, MAGIC, ALU.add, ALU.subtract, [yn], [rn])
        c.tt("vector", y[:], y[:], r[:], ALU.subtract, [yn, rn], [yn])
        c.act(r[:], y[:], AF.Sin, [yn], [rn], scale=TWO_PI)
        c.tt("vector", xp[:, blk, :], xp[:, blk, :], r[:], ALU.add, [f"xp{blk}", rn], [f"xp{blk}"])
        c.dma("sync", XPT[blk], xp[:, blk, :], reads=[f"xp{blk}"], writes=[f"XPT{blk}"])
    sq = [c.sb(f"sq{i}", [128, NT]) for i in range(2)]
    psq = [c.ps(f"psq{i}", [128, 512]) for i in range(2)]
    rstd = c.sb("rstd", [128, NT]); rstc = c.sb("rstc", [128, NC])
    allx = [f"xp{b}" for b in range(NB)]
    cts = col_tiles(NT)
    for blk in range(NB):
        s_ = sq[blk % 2]; sn_ = f"sq{blk % 2}"
        c.act(s_[:], xp[:, blk, :], AF.Square, [f"xp{blk}"], [sn_])
        for i, (c0, c1) in enumerate(cts):
            c.mm(psq[i][:, 0:c1 - c0], ones[:], s_[:, c0:c1], blk == 0, blk == NB - 1, ["ones", sn_], [f"psq{i}"])
    for i, (c0, c1) in enumerate(cts):
        c.ts("vector", rstd[:, c0:c1], psq[i][:, 0:c1 - c0], 1.0 / D, EPS, ALU.mult, ALU.add, [f"psq{i}"], ["rstd"])
    c.act(rstd[:], rstd[:], AF.Sqrt, ["rstd"], ["rstd"])
    c.recip(rstd[:], rstd[:], ["rstd"], ["rstd"])
    pc = c.ps("pc", [128, 512])
    for blk in range(NB):
        s_ = sq[blk % 2]; sn_ = f"sq{blk % 2}"
        c.act(s_[:, 0:NC], xc[:, blk, :], AF.Square, ["xc"], [sn_])
        c.mm(pc[:, 0:NC], ones[:], s_[:, 0:NC], blk == 0, blk == NB - 1, ["ones", sn_], ["pc"])
    c.ts("vector", rstc[:], pc[:, 0:NC], 1.0 / D, EPS, ALU.mult, ALU.add, ["pc"], ["rstc"])
    c.act(rstc[:], rstc[:], AF.Sqrt, ["rstc"], ["rstc"])
    c.recip(rstc[:], rstc[:], ["rstc"], ["rstc"])
    gm = c.sb("gm", [128, NB, 2])
    for r_ in range(2):
        c.ts("vector", gm[:, :, r_], mv[:, :, 2 * r_ + 1], 1.0, None, ALU.add, None, ["mv"], ["gm"])
        c.tt("vector", gm[:, :, r_], gm[:, :, r_], g0[:], ALU.mult, ["gm", "g0"], ["gm"])
    ub = [c.sb(f"ub{i}", [128, NT], BF16) for i in range(2)]
    ucb = c.sb("ucb", [128, NB, NC], BF16); tcx = c.sb("tcx", [128, NC])
    for blk in range(NB):
        y = ty[blk % 2]; yn = f"ty{blk % 2}"; u = ub[blk % 2]; un = f"ub{blk % 2}"
        c.tt("vector", y[:], xp[:, blk, :], rstd[:], ALU.mult, [f"xp{blk}", "rstd"], [yn])
        c.ts("vector", u[:], y[:], gm[:, blk, 0:1], mv[:, blk, 0:1], ALU.mult, ALU.add, [yn, "gm", "mv"], [un])
        c.dma("sync", UT[blk], u[:], reads=[un], writes=[f"UT{blk}"])
        c.tt("vector", tcx[:], xc[:, blk, :], rstc[:], ALU.mult, ["xc", "rstc"], ["tcx"])
        c.ts("vector", ucb[:, blk, :], tcx[:], gm[:, blk, 1:2], mv[:, blk, 2:3], ALU.mult, ALU.add, ["tcx", "gm", "mv"], ["ucb"])
    c.dma("sync", UCT.rearrange("b p t -> p b t"), ucb[:], reads=["ucb"], writes=["UCT"])
    return c.done()


def build_layer(kind, stop=0):
    c = Ctx(); k = c.k
    H = 1 if kind == 0 else 9
    NT = 1024 + 2 * H
    cts = col_tiles(NT)
    XIN = c.din("XIN", [NB, 128, NT])
    MV = c.din("MV", [128, NB, 6]); NG = c.din("NG", [128, NB, 4])
    VM = c.din("VM", [128, NT])
    UP = c.din("UP", [D, 2 * DFF]); DOWN = c.din("DOWN", [DFF, D]); CONV = c.din("CONV", [128, 2 * NJ, 4])
    if kind == 0:
        YT = c.din("YT", [NB, 128, NT]); GLUW = c.din("GLUW", [D, 2 * D])
    else:
        POOLW = c.din("POOLW", [4, 512, 512]); PSC = c.din("PSC", [128, NB]); INVC = c.din("INVC", [128, 4, NT])
    XOT = c.dout("XOT", [NB, 128, 1024])

    XS = c.nc.dram_tensor("XS", [NB, 128, NT], F32, kind="Internal").ap()
    mv = c.sb("mv", [128, NB, 6]); ng = c.sb("ng", [128, NB, 4]); vm = c.sb("vm", [128, NT])
    conv = c.sb("conv", [128, 2 * NJ, 4]); ones = c.sb("ones", [128, 128])
    zu = c.sb("zu", [128, NB, NT], BF16)
    rstd = c.sb("rstd", [128, NT]); coef = c.sb("coef", [128, NB, 4])
    sq = [c.sb(f"sq{i}", [128, NT]) for i in range(2)]
    es1 = ExitStack()
    xp = c.sb("xp", [128, NB, NT], es=es1)
    c.memset("vector", ones[:], 1.0, ["ones"])
    for blk in range(NB):
        c.dma("sync", xp[:, blk, :], XIN[blk], writes=["xp"])
    for dst, src, nm in ((mv, MV, "mv"), (ng, NG, "ng"), (vm, VM, "vm"), (conv, CONV, "conv")):
        c.dma("sync", dst[:], src[:], writes=[nm])
    c.tt("vector", coef[:, :, 0], mv[:, :, 2], ng[:, :, 1], ALU.mult, ["mv", "ng"], ["coef"])
    c.ts("vector", coef[:, :, 1], mv[:, :, 4], 1.0, None, ALU.add, None, ["mv"], ["coef"])
    c.tt("vector", coef[:, :, 1], coef[:, :, 1], ng[:, :, 2], ALU.mult, ["coef", "ng"], ["coef"])
    c.tt("vector", coef[:, :, 2], mv[:, :, 5], ng[:, :, 3], ALU.mult, ["mv", "ng"], ["coef"])
    c.ts("vector", coef[:, :, 3], mv[:, :, 1], 1.0, None, ALU.add, None, ["mv"], ["coef"])
    c.tt("vector", coef[:, :, 3], coef[:, :, 3], ng[:, :, 0], ALU.mult, ["coef", "ng"], ["coef"])

    psq = [c.ps(f"psq{i}", [128, 512]) for i in range(3)]
    pa = [c.ps(f"pa{i}", [128, 512]) for i in range(4)]
    yv = c.sb("yv", [128, NB, NT], BF16, es=es1)
    if kind == 0:
        zf = zu
        yst = [c.sb(f"yst{i}", [128, NT], es=es1) for i in range(2)]
        for blk in range(NB):
            y = yst[blk % 2]; yn = f"yst{blk % 2}"
            c.dma("sync", y[:], YT[blk], writes=[yn])
            c.act(zf[:, blk, :], y[:], AF.Gelu_apprx_tanh, [yn], ["zu"])
        wg = [c.sb(f"wg{i}", [128, 16, 2, 256], BF16, es=es1) for i in range(2)]
        sg = [c.sb(f"sg{i}", [128, 512], es=es1) for i in range(2)]
        n = 0
        for i in range(NB):
            w = wg[(i // 2) % 2]; wn = f"wg{(i // 2) % 2}"; wc0 = 128 * (i % 2)
            if i % 2 == 0:
                for h in range(2):
                    c.dma("gpsimd", w[:, :, h, :], GLUW[:, D * h + 128 * i:D * h + 128 * i + 256].rearrange("(kc p) n -> p kc n", p=128), writes=[wn])
            for (c0, c1) in cts:
                pv = pa[(2 * n) % 4]; pvn = f"pa{(2 * n) % 4}"; pg_ = pa[(2 * n + 1) % 4]; pgn = f"pa{(2 * n + 1) % 4}"
                s_ = sg[n % 2]; sn_ = f"sg{n % 2}"; n += 1
                c.mmg(pv[:, 0:c1 - c0], [(w[:, kc, 0, wc0:wc0 + 128], zf[:, kc, c0:c1]) for kc in range(16)], [wn, "zu"], [pvn])
                c.mmg(pg_[:, 0:c1 - c0], [(w[:, kc, 1, wc0:wc0 + 128], zf[:, kc, c0:c1]) for kc in range(16)], [wn, "zu"], [pgn])
                c.act(s_[:, 0:c1 - c0], pg_[:, 0:c1 - c0], AF.Sigmoid, [pgn], [sn_])
                c.tt("vector", yv[:, i, c0:c1], pv[:, 0:c1 - c0], s_[:, 0:c1 - c0], ALU.mult, [pvn, sn_], ["yv"])
    else:
        pp = zu
        invc = c.sb("invc", [128, NT], es=es1); psc = c.sb("psc", [128, NB], es=es1)
        c.dma("sync", psc[:], PSC[:], writes=["psc"])
        rms_rstd(c, xp, "xp", NT, rstd, "rstd", ones, sq, psq)
        W = NT + 32
        ua = [c.sb(f"ua{i}", [128, W], es=es1) for i in range(3)]
        for i in range(3):
            c.memset("vector", ua[i][:], 0.0, [f"ua{i}"])
        ut = c.sb("ut", [128, NT], es=es1)
        for blk in range(NB):
            m = 1 + blk // 4
            if blk % 4 == 0:
                c.dma("sync", invc[:], INVC[:, m - 1, :], writes=["invc"])
            c.tt("vector", ut[:], xp[:, blk, :], rstd[:, 0:NT], ALU.mult, ["xp", "rstd"], ["ut"])
            c.ts("vector", ut[:], ut[:], coef[:, blk, 3:4], mv[:, blk, 0:1], ALU.mult, ALU.add, ["ut", "coef", "mv"], ["ut"])
            c.tt("vector", ua[0][:, 16:16 + NT], ut[:], vm[:], ALU.mult, ["ut", "vm"], ["ua0"])
            cur = 0
            for lvl in range(m):
                sh = 1 << lvl
                nxt = 1 + (lvl % 2)
                c.tt("vector", ua[nxt][:, 16:W], ua[cur][:, 16:W], ua[cur][:, 16 - sh:W - sh], ALU.add, [f"ua{cur}"], [f"ua{nxt}"])
                cur = nxt
            w2 = (1 << m) // 2
            off = 16 + w2 - 1
            c.tt("vector", ut[:], ua[cur][:, off:off + NT], invc[:], ALU.mult, [f"ua{cur}", "invc"], ["ut"])
            c.tt("vector", pp[:, blk, :], ut[:], ua[0][:, 16:16 + NT], ALU.subtract, ["ut", "ua0"], ["zu"])
        wp = [c.sb(f"wp{i}", [128, 4, 128], BF16, es=es1) for i in range(2)]
        n = 0
        for gi in range(4):
            for bo in range(4):
                w = wp[(4 * gi + bo) % 2]; wn = f"wp{(4 * gi + bo) % 2}"
                c.dma("gpsimd", w[:], POOLW[gi, :, 128 * bo:128 * bo + 128].rearrange("(kc p) n -> p kc n", p=128), writes=[wn])
                for (c0, c1) in cts:
                    p = pa[n % 4]; pn = f"pa{n % 4}"; n += 1
                    c.mmg(p[:, 0:c1 - c0], [(w[:, kc, :], pp[:, 4 * gi + kc, c0:c1]) for kc in range(4)], [wn, "zu"], [pn])
                    c.ts("vector", yv[:, 4 * gi + bo, c0:c1], p[:, 0:c1 - c0], psc[:, 4 * gi + bo:4 * gi + bo + 1], None, ALU.mult, None, [pn, "psc"], ["yv"])
    rms_rstd(c, yv, "yv", NT, rstd, "rstd", ones, sq, psq)
    for blk in range(NB):
        s_ = sq[blk % 2]; sn_ = f"sq{blk % 2}"
        c.stt(s_[:, 0:NT], yv[:, blk, :], coef[:, blk, 0:1], rstd[:, 0:NT], ALU.mult, ALU.mult, ["yv", "coef", "rstd"], [sn_])
        c.tt("vector", xp[:, blk, :], xp[:, blk, :], s_[:, 0:NT], ALU.add, ["xp", sn_], ["xp"])

    if stop == 1:
        for blk in range(NB):
            c.dma("sync", XOT[blk], xp[:, blk, H:H + 1024], reads=["xp"], writes=[f"XOT{blk}"])
        k.barrier()
        es1.close()
        return c.done()
    u2 = zu
    rms_rstd(c, xp, "xp", NT, rstd, "rstd", ones, sq, psq)
    for blk in range(NB):
        s_ = sq[blk % 2]; sn_ = f"sq{blk % 2}"
        c.tt("vector", s_[:, 0:NT], xp[:, blk, :], rstd[:, 0:NT], ALU.mult, ["xp", "rstd"], [sn_])
        c.ts("vector", s_[:, 0:NT], s_[:, 0:NT], coef[:, blk, 1:2], mv[:, blk, 3:4], ALU.mult, ALU.add, [sn_, "coef", "mv"], [sn_])
        c.tt("vector", u2[:, blk, :], s_[:, 0:NT], vm[:], ALU.mult, [sn_, "vm"], ["zu"])
        c.dma("sync", XS[blk], xp[:, blk, :], reads=["xp"], writes=[f"XS{blk}"])
    k.barrier()
    es1.close()
    es2 = ExitStack()
    A = c.sb("A", [128, NJ, NT], BF16, es=es2)
    es3 = ExitStack()
    wu = [c.sb(f"wu{i}", [128, 16, 2, 256], BF16, es=es3) for i in range(2)]
    hs = [[c.sb(f"hs{i}{h}", [128, NT + 2], es=es3) for h in range(2)] for i in range(2)]
    hc = [[c.sb(f"hc{i}{h}", [128, NT], es=es3) for h in range(2)] for i in range(2)]
    for i in range(2):
        for h in range(2):
            c.memset("vector", hs[i][h][:], 0.0, [f"hs{i}{h}"])
    n = 0
    for j in range(NJ):
        w = wu[(j // 2) % 2]; wn = f"wu{(j // 2) % 2}"; wc0 = 128 * (j % 2)
        if j % 2 == 0:
            for h in range(2):
                c.dma("gpsimd", w[:, :, h, :], UP[:, DFF * h + 128 * j:DFF * h + 128 * j + 256].rearrange("(kc p) n -> p kc n", p=128), writes=[wn])
        for h in range(2):
            hsb = hs[j % 2][h]; hsn = f"hs{j % 2}{h}"; hcb = hc[j % 2][h]; hcn = f"hc{j % 2}{h}"
            cb = NJ * h + j
            for (c0, c1) in cts:
                p = pa[n % 4]; pn = f"pa{n % 4}"; n += 1
                c.mmg(p[:, 0:c1 - c0], [(w[:, kc, h, wc0:wc0 + 128], u2[:, kc, c0:c1]) for kc in range(16)], [wn, "zu"], [pn])
                c.cp("scalar", hsb[:, 1 + c0:1 + c1], p[:, 0:c1 - c0], [pn], [hsn])
            c.act(hcb[:], hsb[:, 1:NT + 1], AF.Identity, [hsn, "conv"], [hcn], scale=conv[:, cb, 1:2], bias=conv[:, cb, 3:4])
            c.stt(hcb[:], hsb[:, 0:NT], conv[:, cb, 0:1], hcb[:], ALU.mult, ALU.add, [hsn, "conv", hcn], [hcn])
            c.stt(hcb[:], hsb[:, 2:NT + 2], conv[:, cb, 2:3], hcb[:], ALU.mult, ALU.add, [hsn, "conv", hcn], [hcn])
        hv = hc[j % 2][0]; hg = hc[j % 2][1]
        c.act(hg[:], hg[:], AF.Silu, [f"hc{j % 2}1"], [f"hc{j % 2}1"])
        c.tt("vector", A[:, j, :], hv[:], hg[:], ALU.mult, [f"hc{j % 2}0", f"hc{j % 2}1"], ["A"])
    k.barrier()
    es3.close()
    if stop == 3:
        xl = [c.sb(f"xl{i}", [128, NT]) for i in range(2)]
        for blk in range(NB):
            c.cp("vector", xl[blk % 2][:], A[:, blk, :], ["A"], [f"xl{blk % 2}"])
            c.dma("sync", XOT[blk], xl[blk % 2][:, H:H + 1024], reads=[f"xl{blk % 2}"], writes=[f"XOT{blk}"])
        k.barrier()
        c.es.pop_all().close() if False else None
        nc_ = c.k
        c.k.finish(); c.k.run(); c.k.close()
        return c.nc
    es4 = ExitStack()
    wd = [c.sb(f"wd{i}", [128, NJ, 256], BF16, es=es4) for i in range(2)]
    F = zu
    xl = [c.sb(f"xl{i}", [128, NT], es=es4) for i in range(2)]
    n = 0
    ctd = [(H, H + 512), (H + 512, H + 1024)]
    for blk in range(NB):
        w = wd[(blk // 2) % 2]; wn = f"wd{(blk // 2) % 2}"; wc0 = 128 * (blk % 2)
        if blk % 2 == 0:
            for jq in range(4):
                c.dma("gpsimd", w[:, 11 * jq:11 * jq + 11, :],
                      DOWN[1408 * jq:1408 * jq + 1408, 128 * blk:128 * blk + 256].rearrange("(j p) n -> p j n", p=128), writes=[wn])
        s_ = sq[blk % 2]; sn_ = f"sq{blk % 2}"
        for i, (c0, c1) in enumerate(ctd):
            p = pa[n % 4]; pn = f"pa{n % 4}"; n += 1
            c.mmg(p[:, 0:c1 - c0], [(w[:, j, wc0:wc0 + 128], A[:, j, c0:c1]) for j in range(NJ)], [wn, "A"], [pn])
            c.cp("vector", F[:, blk, c0:c1], p[:, 0:c1 - c0], [pn], ["zu"])
            c.act(s_[:, c0:c1], F[:, blk, c0:c1], AF.Square, ["zu"], [sn_])
        for i, (c0, c1) in enumerate(ctd):
            c.mm(psq[i][:, 0:c1 - c0], ones[:], s_[:, c0:c1], blk == 0, blk == NB - 1, ["ones", sn_], [f"psq{i}"])
    for i, (c0, c1) in enumerate(ctd):
        c.ts("vector", rstd[:, c0:c1], psq[i][:, 0:c1 - c0], 1.0 / D, EPS, ALU.mult, ALU.add, [f"psq{i}"], ["rstd"])
    c.act(rstd[:, H:H + 1024], rstd[:, H:H + 1024], AF.Sqrt, ["rstd"], ["rstd"])
    c.recip(rstd[:, H:H + 1024], rstd[:, H:H + 1024], ["rstd"], ["rstd"])
    for blk in range(NB):
        s_ = sq[blk % 2]; sn_ = f"sq{blk % 2}"
        x_ = xl[blk % 2]; xn_ = f"xl{blk % 2}"
        c.dma("sync", x_[:], XS[blk], reads=[f"XS{blk}"], writes=[xn_])
        c.stt(s_[:, H:H + 1024], F[:, blk, H:H + 1024], coef[:, blk, 2:3], rstd[:, H:H + 1024], ALU.mult, ALU.mult, ["zu", "coef", "rstd"], [sn_])
        c.tt("vector", s_[:, H:H + 1024], x_[:, H:H + 1024], s_[:, H:H + 1024], ALU.add, [xn_, sn_], [sn_])
        c.dma("sync", XOT[blk], s_[:, H:H + 1024], reads=[sn_], writes=[f"XOT{blk}"])
    k.barrier()
    es4.close(); es2.close()
    return c.done()


_PROGS = {}


def _prog(name, fn, *a):
    key = (name,) + a
    if key not in _PROGS:
        _PROGS[key] = fn(*a)
    return _PROGS[key]


def _fm(a):
    return np.ascontiguousarray(a.T.reshape(NB, 128, a.shape[0]))


def _unfm(a):
    return a.reshape(D, a.shape[2]).T


def _pervec(v):
    return np.ascontiguousarray(v.reshape(NB, 128).T)


def _halo(a, q, h):
    out = np.zeros((1024 + 2 * h, a.shape[1]), a.dtype)
    lo = 1024 * q - h; hi = 1024 * q + 1024 + h
    s0 = max(lo, 0); s1 = min(hi, L)
    out[s0 - lo:s1 - lo] = a[s0:s1]
    return out


def kernel(x, c, ctx, c_ctx, ada_w, ada_b, norm_g, s5_lam_re, s5_lam_im, s5_log_step,
           s5_b_re, s5_b_im, s5_c_re, s5_c_im, s5_d, s5_glu_w, pool_w, pool_scale,
           ffn_up, ffn_conv, ffn_conv_b, ffn_down, _dbg=None):
    f32 = np.float32
    inp = dict(s5_lam_re=np.asarray(s5_lam_re, f32), s5_lam_im=np.asarray(s5_lam_im, f32), s5_log_step=np.asarray(s5_log_step, f32),
               s5_b_re=np.asarray(s5_b_re, f32), s5_b_im=np.asarray(s5_b_im, f32), s5_c_re=np.asarray(s5_c_re, f32),
               s5_c_im=np.asarray(s5_c_im, f32), s5_d=np.asarray(s5_d, f32))
    x = np.asarray(x, f32); c = np.asarray(c, f32); ctx = np.asarray(ctx, f32); c_ctx = np.asarray(c_ctx, f32)
    ada_w = np.asarray(ada_w, f32); ada_b = np.asarray(ada_b, f32); norm_g = np.asarray(norm_g, f32)
    cores = list(range(8))
    cnd = np.zeros((128, 16, 4), f32)
    for r, v in enumerate((c[0], c[1], c_ctx)):
        cnd[:, :, r] = v.reshape(16, 128).T
    maps = []
    for i in cores:
        cs = slice(1536 * i, 1536 * i + 1536)
        maps.append({"CND": cnd, "AW": np.ascontiguousarray(ada_w[:, :, cs]),
                     "AB": np.ascontiguousarray(ada_b[:, cs].reshape(2, 12, 128).transpose(0, 2, 1))})
    res = run_bass_kernel_spmd(_prog("mod", build_mod), maps, core_ids=cores).results
    modfull = np.zeros((2, 4, 12288), f32)
    for i in cores:
        m = res[i]["MOD"]
        modfull[:, :, 1536 * i:1536 * i + 1536] = m.transpose(0, 3, 2, 1).reshape(2, 4, 1536)
    def modv(l, r):
        return np.ascontiguousarray(modfull[l, r].reshape(6, NB, 128).transpose(2, 1, 0))
    if _dbg is not None:
        _dbg["mod"] = modfull
    maps = []
    jidx = (np.arange(4, dtype=f32)[None, :] * 128 + np.arange(128, dtype=f32)[:, None]).astype(f32)
    for i in cores:
        b, q = divmod(i, 4)
        t = np.arange(1024 * q, 1024 * q + 1024)
        mvb = modv(0, b); mvc = modv(0, 2)
        mv = np.stack([mvb[:, :, 0], mvb[:, :, 1], mvc[:, :, 0], mvc[:, :, 1]], 2)
        maps.append({"XT": _fm(x[b, 1024 * q:1024 * q + 1024]), "CT": _fm(ctx[b, 64 * q:64 * q + 64]),
                     "MV": np.ascontiguousarray(mv), "G0": _pervec(norm_g[0, 0]),
                     "RIDX": np.broadcast_to((t // 64).astype(f32)[None], (128, 1024)).copy(),
                     "CIDX": np.broadcast_to((t % 64).astype(f32)[None], (128, 1024)).copy(), "JIDX": jidx})
    res = run_bass_kernel_spmd(_prog("prep", build_prep), maps, core_ids=cores).results
    xp = np.zeros((2, L, D), f32); u = np.zeros((2, L, D), ml_dtypes.bfloat16); uc = np.zeros((2, NCTX, D), ml_dtypes.bfloat16)
    for i in cores:
        b, q = divmod(i, 4)
        xp[b, 1024 * q:1024 * q + 1024] = _unfm(res[i]["XPT"])
        u[b, 1024 * q:1024 * q + 1024] = _unfm(res[i]["UT"])
        uc[b, 64 * q:64 * q + 64] = _unfm(res[i]["UCT"])
    if _dbg is not None:
        _dbg["xp"] = xp; _dbg["u"] = u; _dbg["uc"] = uc
    res = run_bass_kernel_spmd(_prog("s5", build_s5), s5_in_maps(u, uc, inp), core_ids=cores).results
    ys5 = s5_gather(res)
    if _dbg is not None:
        _dbg["ys5"] = ys5
    xcur = xp
    for l in range(2):
        h = 1 if l == 0 else 9
        nt = 1024 + 2 * h
        cv = np.zeros((128, 2 * NJ, 4), f32)
        for hh in range(2):
            for j in range(NJ):
                n0 = DFF * hh + 128 * j
                cv[:, NJ * hh + j, 0:3] = ffn_conv[l][:, n0:n0 + 128].T
                cv[:, NJ * hh + j, 3] = ffn_conv_b[l][n0:n0 + 128]
        ngl = np.ascontiguousarray(np.asarray(norm_g[l], f32).reshape(4, NB, 128).transpose(2, 1, 0))
        maps = []
        for i in cores:
            b, q = divmod(i, 4)
            tg = np.arange(1024 * q - h, 1024 * q + 1024 + h)
            valid = ((tg >= 0) & (tg < L)).astype(f32)
            m = {"XIN": _fm(_halo(xcur[b], q, h)), "MV": modv(l, b), "NG": ngl,
                 "VM": np.broadcast_to(valid[None], (128, nt)).copy(),
                 "UP": np.asarray(ffn_up[l], f32), "DOWN": np.asarray(ffn_down[l], f32), "CONV": cv}
            if l == 0:
                m["YT"] = _fm(_halo(ys5[b], q, h)); m["GLUW"] = np.asarray(s5_glu_w[0], f32)
            else:
                m["POOLW"] = np.asarray(pool_w[0], f32); m["PSC"] = _pervec(np.asarray(pool_scale[0], f32))
                invc = np.ones((4, nt), f32)
                for mi, w in enumerate((2, 4, 8, 16)):
                    lo = np.clip(tg - w // 2, 0, L - 1); hi = np.clip(tg + w // 2 - 1, 0, L - 1)
                    invc[mi] = 1.0 / np.maximum(hi - lo + 1, 1)
                m["INVC"] = np.broadcast_to(invc[None], (128, 4, nt)).copy()
            maps.append(m)
        res = run_bass_kernel_spmd(_prog("layer", build_layer, l), maps, core_ids=cores).results
        xn = np.zeros((2, L, D), f32)
        for i in cores:
            b, q = divmod(i, 4)
            xn[b, 1024 * q:1024 * q + 1024] = _unfm(res[i]["XOT"])
        xcur = xn
        if _dbg is not None:
            _dbg[f"xout{l}"] = xn
    return xcur
```

```python
import math
from contextlib import ExitStack
import numpy as np
import ml_dtypes
import concourse.bass as bass
import concourse.mybir as mybir
from concourse.bass_utils import run_bass_kernel_spmd

F32 = mybir.dt.float32
BF16 = mybir.dt.bfloat16
I32 = mybir.dt.int32
AF = mybir.ActivationFunctionType
ALU = mybir.AluOpType

D = 2048
NB = 16
L = 4096
NCTX = 256
DFF = 5632
NJ = 44
MAGIC = 12582912.0
TWO_PI = 2.0 * math.pi
EPS = 1e-6
EPOCH = 3000
NDMA_CH = 12


class KB:
    ENGS = ("tensor", "vector", "scalar", "gpsimd", "sync")

    def __init__(self, nc):
        self.nc = nc
        self.prog = {e: [] for e in self.ENGS}
        self.cnt = {e: 0 for e in self.ENGS}
        self.sems = {e: [] for e in self.ENGS}
        self.waited = {e: {} for e in self.ENGS}
        self.last_w = {}
        self.readers = {}
        self.dma_ch = []
        self.dma_rr = 0
        self._stack = []
        for i in range(NDMA_CH):
            self.dma_ch.append({"sem": self._sem(f"s_dma_{i}"), "n": 0})

    def _sem(self, name):
        cm = self.nc.semaphore(name)
        s = cm.__enter__()
        self._stack.append(cm)
        return s

    def _eng_sem(self, e, epoch):
        while len(self.sems[e]) <= epoch:
            self.sems[e].append(self._sem(f"s_{e}_{len(self.sems[e])}"))
        return self.sems[e][epoch]

    def _deps(self, reads, writes):
        deps = []
        for b in reads:
            t = self.last_w.get(b)
            if t is not None:
                deps.append(t)
        for b in writes:
            t = self.last_w.get(b)
            if t is not None:
                deps.append(t)
            deps.extend(self.readers.get(b, []))
        return deps

    def _record(self, tok, reads, writes):
        for b in reads:
            self.readers.setdefault(b, []).append(tok)
        for b in writes:
            self.last_w[b] = tok
            self.readers[b] = []

    def _emit_waits(self, e, deps):
        need = {}
        for t in deps:
            if t[0] == "eng":
                key = ("eng", t[1], t[2]); v = t[3]
            else:
                key = ("dma", t[1]); v = t[2]
            if need.get(key, 0) < v:
                need[key] = v
        for key, v in need.items():
            if self.waited[e].get(key, 0) >= v:
                continue
            self.waited[e][key] = v
            sem = self._eng_sem(key[1], key[2]) if key[0] == "eng" else self.dma_ch[key[1]]["sem"]
            self.prog[e].append(lambda eng, sem=sem, v=v: eng.wait_ge(sem, v))

    def op(self, e, fn, reads=(), writes=()):
        self._emit_waits(e, self._deps(reads, writes))
        n = self.cnt[e]
        epoch, val = divmod(n, EPOCH)
        sem = self._eng_sem(e, epoch)
        self.cnt[e] = n + 1
        self.prog[e].append(lambda eng, fn=fn, sem=sem: fn(eng).then_inc(sem, 1))
        tok = ("eng", e, epoch, val + 1)
        self._record(tok, reads, writes)
        return tok

    def ops(self, e, fns, reads=(), writes=()):
        self._emit_waits(e, self._deps(reads, writes))
        for fn in fns[:-1]:
            self.prog[e].append(lambda eng, fn=fn: fn(eng))
        n = self.cnt[e]
        epoch, val = divmod(n, EPOCH)
        sem = self._eng_sem(e, epoch)
        self.cnt[e] = n + 1
        self.prog[e].append(lambda eng, fn=fns[-1], sem=sem: fn(eng).then_inc(sem, 1))
        tok = ("eng", e, epoch, val + 1)
        self._record(tok, reads, writes)
        return tok

    def dma(self, q, out, in_, reads=(), writes=(), **kw):
        deps = self._deps(reads, writes)
        ch = self.dma_rr
        self.dma_rr = (self.dma_rr + 1) % len(self.dma_ch)
        c = self.dma_ch[ch]
        if c["n"] > 0:
            deps.append(("dma", ch, 16 * c["n"]))
        self._emit_waits(q, deps)
        c["n"] += 1
        sem = c["sem"]
        self.prog[q].append(
            lambda eng, out=out, in_=in_, sem=sem, kw=kw: eng.dma_start(out=out, in_=in_, **kw).then_inc(sem, 16))
        tok = ("dma", ch, 16 * c["n"])
        self._record(tok, reads, writes)
        return tok

    def barrier(self):
        toks = []
        for e in self.ENGS:
            n = self.cnt[e]
            if n > 0:
                epoch, val = divmod(n - 1, EPOCH)
                toks.append(("eng", e, epoch, val + 1))
        for ch, c in enumerate(self.dma_ch):
            if c["n"] > 0:
                toks.append(("dma", ch, 16 * c["n"]))
        for e in self.ENGS:
            self._emit_waits(e, toks)
        self.last_w = {}
        self.readers = {}

    def finish(self):
        toks = []
        for e in self.ENGS:
            n = self.cnt[e]
            if n > 0:
                epoch, val = divmod(n - 1, EPOCH)
                toks.append(("eng", e, epoch, val + 1))
        for ch, c in enumerate(self.dma_ch):
            if c["n"] > 0:
                toks.append(("dma", ch, 16 * c["n"]))
        self._emit_waits("sync", toks)

    def run(self):
        with self.nc.Block() as block:
            for e in self.ENGS:
                prog = self.prog[e]
                if not prog:
                    continue

                def body(eng, prog=prog):
                    for f in prog:
                        f(eng)
                getattr(block, e)(body)

    def close(self):
        while self._stack:
            self._stack.pop().__exit__(None, None, None)


class Ctx:
    def __init__(self):
        self.nc = bass.Bass("TRN2", target_bir_lowering=False)
        self.es = ExitStack()
        self.k = KB(self.nc)
        self.uid = 0

    def din(self, name, shape, dt=F32):
        return self.nc.dram_tensor(name, list(shape), dt, kind="ExternalInput").ap()

    def dout(self, name, shape, dt=F32):
        return self.nc.dram_tensor(name, list(shape), dt, kind="ExternalOutput").ap()

    def sb(self, name, shape, dt=F32, es=None):
        return (es or self.es).enter_context(self.nc.sbuf_tensor(name, list(shape), dt))

    def ps(self, name, shape, dt=F32, es=None):
        return (es or self.es).enter_context(self.nc.psum_tensor(name, list(shape), dt))

    def done(self):
        self.k.finish()
        self.k.run()
        self.k.close()
        self.es.close()
        return self.nc

    def mm(self, out, lhsT, rhs, start, stop, reads, writes):
        return self.k.op("tensor", lambda e: e.matmul(out, lhsT=lhsT, rhs=rhs, start=start, stop=stop), reads, writes)

    def mmg(self, out, pairs, reads, writes):
        n = len(pairs)
        fns = []
        for i, (lh, rh) in enumerate(pairs):
            fns.append(lambda e, lh=lh, rh=rh, i=i: e.matmul(out, lhsT=lh, rhs=rh, start=(i == 0), stop=(i == n - 1)))
        return self.k.ops("tensor", fns, reads, writes)

    def tt(self, eng, out, in0, in1, op, reads, writes):
        return self.k.op(eng, lambda e: e.tensor_tensor(out=out, in0=in0, in1=in1, op=op), reads, writes)

    def ts(self, eng, out, in0, s1, s2, op0, op1, reads, writes):
        if op1 is None:
            return self.k.op(eng, lambda e: e.tensor_scalar(out=out, in0=in0, scalar1=s1, scalar2=None, op0=op0), reads, writes)
        return self.k.op(eng, lambda e: e.tensor_scalar(out=out, in0=in0, scalar1=s1, scalar2=s2, op0=op0, op1=op1), reads, writes)

    def stt(self, out, in0, scalar, in1, op0, op1, reads, writes):
        return self.k.op("vector", lambda e: e.scalar_tensor_tensor(out=out, in0=in0, scalar=scalar, in1=in1, op0=op0, op1=op1), reads, writes)

    def act(self, out, in_, func, reads, writes, scale=None, bias=None):
        kw = {}
        if scale is not None:
            kw["scale"] = scale
        if bias is not None:
            kw["bias"] = bias
        return self.k.op("scalar", lambda e: e.activation(out=out, in_=in_, func=func, **kw), reads, writes)

    def cp(self, eng, out, in_, reads, writes):
        if eng == "scalar":
            return self.k.op(eng, lambda e: e.copy(out=out, in_=in_), reads, writes)
        return self.k.op(eng, lambda e: e.tensor_copy(out=out, in_=in_), reads, writes)

    def memset(self, eng, out, val, writes):
        return self.k.op(eng, lambda e: e.memset(out, val), (), writes)

    def recip(self, out, in_, reads, writes):
        return self.k.op("vector", lambda e: e.reciprocal(out=out, in_=in_), reads, writes)

    def dma(self, q, out, in_, reads=(), writes=()):
        return self.k.dma(q, out, in_, reads, writes)


def ap_of(t, off, dims):
    a = t[:]
    return bass.AP(a.tensor, a.offset + off, [list(a.ap[0])] + [list(d) for d in dims])


NK = 548
GL = 16


def build_s5(debug=False):
    c = Ctx(); k = c.k; nc = c.nc
    dbg = []
    XF = c.din("XF", [2, GL, 128, NK], BF16)
    XB = c.din("XB", [2, GL, 128, NK], BF16)
    LAM = c.din("LAM", [128, 2, 2, GL])
    LST = c.din("LST", [128, 2, GL])
    B1 = c.din("B1", [128, 2, GL, 16]); B2 = c.din("B2", [128, 2, GL, 16])
    C1 = c.din("C1", [128, 2, GL, 16]); C2 = c.din("C2", [128, 2, GL, 16])
    DCOL = c.din("DCOL", [128, GL])
    CONS = c.din("CONS", [128, 4, 128])
    SGN = c.din("SGN", [128, 2])
    JFAC = c.din("JFAC", [128, 16, GL])
    IOTA = c.din("IOTA", [128, NK])
    YC = c.dout("YC", [2, GL, 128, 512])

    sb = c.sb
    lam = sb("lam", [128, 2, 2, GL]); lst = sb("lst", [128, 2, GL])
    b1 = sb("b1", [128, 2, GL, 16]); b2 = sb("b2", [128, 2, GL, 16])
    c1 = sb("c1", [128, 2, GL, 16]); c2 = sb("c2", [128, 2, GL, 16])
    dcol = sb("dcol", [128, GL]); cons = sb("cons", [128, 4, 128]); consb = sb("consb", [128, 2, 128], BF16)
    sgn = sb("sgn", [128, 2]); jfac = sb("jfac", [128, 16, GL]); iota = sb("iota", [128, NK])
    for dst, src, nm in ((lam, LAM, "lam"), (lst, LST, "lst"), (b1, B1, "b1"), (b2, B2, "b2"), (c1, C1, "c1"),
                         (c2, C2, "c2"), (dcol, DCOL, "dcol"), (cons, CONS, "cons"), (sgn, SGN, "sgn"),
                         (jfac, JFAC, "jfac"), (iota, IOTA, "iota")):
        k.dma("sync", dst[:], src[:], writes=[nm])
    k.op("vector", lambda e: e.tensor_copy(out=consb[:], in_=cons[:, 0:2, :]), reads=["cons"], writes=["consb"])
    identb = consb[:, 0, :]; pswapb = consb[:, 1, :]

    V = "vector"
    dt = sb("dt", [128, 2, GL]); ee = sb("ee", [128, 2, GL]); th = sb("th", [128, 2, GL])
    k.op("scalar", lambda e: e.activation(out=dt[:], in_=lst[:], func=AF.Exp), reads=["lst"], writes=["dt"])
    k.op(V, lambda e: e.tensor_tensor(out=ee[:], in0=lam[:, 0], in1=dt[:], op=ALU.mult), reads=["lam", "dt"], writes=["ee"])
    k.op(V, lambda e: e.tensor_tensor(out=th[:], in0=lam[:, 1], in1=dt[:], op=ALU.mult), reads=["lam", "dt"], writes=["th"])
    shp = [128, 2, 16, GL]
    ej = sb("ej", shp); yj = sb("yj", shp); tmp = sb("tmp", shp); tmp2 = sb("tmp2", shp)
    emag = sb("emag", shp); sn = sb("sn", shp); cs = sb("cs", shp); pr = sb("pr", shp); pi = sb("pi", shp)

    def bc_dg(t):
        return ap_of(t, 0, [[GL, 2], [0, 16], [1, GL]])

    def bc_j(t):
        return ap_of(t, 0, [[0, 2], [GL, 16], [1, GL]])
    k.op(V, lambda e: e.tensor_tensor(out=ej[:], in0=bc_dg(ee), in1=bc_j(jfac), op=ALU.mult), reads=["ee", "jfac"], writes=["ej"])
    k.op("scalar", lambda e: e.activation(out=emag[:], in_=ej[:], func=AF.Exp), reads=["ej"], writes=["emag"])
    k.op(V, lambda e: e.tensor_tensor(out=yj[:], in0=bc_dg(th), in1=bc_j(jfac), op=ALU.mult), reads=["th", "jfac"], writes=["yj"])

    def sin_of(dst, src, shift, nm_dst, nm_src):
        k.op(V, lambda e: e.tensor_scalar(out=tmp[:], in0=src[:], scalar1=1.0 / TWO_PI, scalar2=shift, op0=ALU.mult, op1=ALU.add),
             reads=[nm_src], writes=["tmp"])
        k.op(V, lambda e: e.tensor_scalar(out=tmp2[:], in0=tmp[:], scalar1=MAGIC, scalar2=MAGIC, op0=ALU.add, op1=ALU.subtract),
             reads=["tmp"], writes=["tmp2"])
        k.op(V, lambda e: e.tensor_tensor(out=tmp[:], in0=tmp[:], in1=tmp2[:], op=ALU.subtract), reads=["tmp", "tmp2"], writes=["tmp"])
        k.op("scalar", lambda e: e.activation(out=dst[:], in_=tmp[:], func=AF.Sin, scale=TWO_PI), reads=["tmp"], writes=[nm_dst])
    sin_of(sn, yj, 0.0, "sn", "yj")
    sin_of(cs, yj, 0.25, "cs", "yj")
    k.op(V, lambda e: e.tensor_tensor(out=pr[:], in0=emag[:], in1=cs[:], op=ALU.mult), reads=["emag", "cs"], writes=["pr"])
    k.op(V, lambda e: e.tensor_tensor(out=pi[:], in0=emag[:], in1=sn[:], op=ALU.mult), reads=["emag", "sn"], writes=["pi"])
    s3 = [128, 2, GL]
    nr = sb("nr", s3); den = sb("den", s3); t3 = sb("t3", s3); t4 = sb("t4", s3); fr = sb("fr", s3); fi = sb("fi", s3)
    ar = pr[:, :, 8, :]; ai = pi[:, :, 8, :]
    lr = lam[:, 0]; li = lam[:, 1]
    k.op(V, lambda e: e.tensor_scalar(out=nr[:], in0=ar, scalar1=-1.0, scalar2=None, op0=ALU.add), reads=["pr"], writes=["nr"])
    k.op(V, lambda e: e.tensor_tensor(out=den[:], in0=lr, in1=lr, op=ALU.mult), reads=["lam"], writes=["den"])
    k.op(V, lambda e: e.tensor_tensor(out=t3[:], in0=li, in1=li, op=ALU.mult), reads=["lam"], writes=["t3"])
    k.op(V, lambda e: e.tensor_tensor(out=den[:], in0=den[:], in1=t3[:], op=ALU.add), reads=["den", "t3"], writes=["den"])
    k.op(V, lambda e: e.reciprocal(out=den[:], in_=den[:]), reads=["den"], writes=["den"])
    k.op(V, lambda e: e.tensor_tensor(out=t3[:], in0=nr[:], in1=lr, op=ALU.mult), reads=["nr", "lam"], writes=["t3"])
    k.op(V, lambda e: e.tensor_tensor(out=t4[:], in0=ai, in1=li, op=ALU.mult), reads=["pi", "lam"], writes=["t4"])
    k.op(V, lambda e: e.tensor_tensor(out=t3[:], in0=t3[:], in1=t4[:], op=ALU.add), reads=["t3", "t4"], writes=["t3"])
    k.op(V, lambda e: e.tensor_tensor(out=fr[:], in0=t3[:], in1=den[:], op=ALU.mult), reads=["t3", "den"], writes=["fr"])
    k.op(V, lambda e: e.tensor_tensor(out=t3[:], in0=ai, in1=lr, op=ALU.mult), reads=["pi", "lam"], writes=["t3"])
    k.op(V, lambda e: e.tensor_tensor(out=t4[:], in0=nr[:], in1=li, op=ALU.mult), reads=["nr", "lam"], writes=["t4"])
    k.op(V, lambda e: e.tensor_tensor(out=t3[:], in0=t3[:], in1=t4[:], op=ALU.subtract), reads=["t3", "t4"], writes=["t3"])
    k.op(V, lambda e: e.tensor_tensor(out=fi[:], in0=t3[:], in1=den[:], op=ALU.mult), reads=["t3", "den"], writes=["fi"])
    qa = sb("qa", shp); qb = sb("qb", shp); t1t = sb("t1t", shp); t2t = sb("t2t", shp)
    k.op(V, lambda e: e.tensor_tensor(out=tmp[:], in0=pr[:], in1=bc_dg(fr), op=ALU.mult), reads=["pr", "fr"], writes=["tmp"])
    k.op(V, lambda e: e.tensor_tensor(out=tmp2[:], in0=pi[:], in1=bc_dg(fi), op=ALU.mult), reads=["pi", "fi"], writes=["tmp2"])
    k.op(V, lambda e: e.tensor_tensor(out=qa[:], in0=tmp[:], in1=tmp2[:], op=ALU.subtract), reads=["tmp", "tmp2"], writes=["qa"])
    k.op(V, lambda e: e.tensor_tensor(out=tmp[:], in0=pr[:], in1=bc_dg(fi), op=ALU.mult), reads=["pr", "fi"], writes=["tmp"])
    k.op(V, lambda e: e.tensor_tensor(out=tmp2[:], in0=pi[:], in1=bc_dg(fr), op=ALU.mult), reads=["pi", "fr"], writes=["tmp2"])
    k.op(V, lambda e: e.tensor_tensor(out=tmp[:], in0=tmp[:], in1=tmp2[:], op=ALU.add), reads=["tmp", "tmp2"], writes=["tmp"])
    k.op(V, lambda e: e.tensor_scalar(out=qb[:], in0=tmp[:], scalar1=sgn[:, 1:2], scalar2=None, op0=ALU.mult), reads=["tmp", "sgn"], writes=["qb"])
    k.op(V, lambda e: e.tensor_scalar(out=t1t[:], in0=pr[:], scalar1=sgn[:, 0:1], scalar2=None, op0=ALU.mult), reads=["pr", "sgn"], writes=["t1t"])
    k.op(V, lambda e: e.tensor_scalar(out=t2t[:], in0=pi[:], scalar1=-1.0, scalar2=None, op0=ALU.mult), reads=["pi"], writes=["t2t"])
    ph = sb("ph", s3); phs = sb("phs", s3)
    k.op(V, lambda e: e.tensor_scalar(out=ph[:], in0=th[:], scalar1=8.0 / TWO_PI, scalar2=None, op0=ALU.mult), reads=["th"], writes=["ph"])
    k.op(V, lambda e: e.tensor_scalar(out=phs[:], in0=ph[:], scalar1=sgn[:, 0:1], scalar2=None, op0=ALU.mult), reads=["ph", "sgn"], writes=["phs"])
    rho = emag

    msh = [128, GL, 8, 16]
    mt1 = sb("mt1", msh); mt2 = sb("mt2", msh)
    Lm = [sb(f"Lm{d}", msh, BF16) for d in range(2)]
    Rm = [sb(f"Rm{d}", msh, BF16) for d in range(2)]
    Wo = [sb(f"Wo{d}", msh, BF16) for d in range(2)]
    Wos = [sb(f"Wos{d}", [128, GL, 128], BF16) for d in range(2)]
    Wi = [sb(f"Wi{d}", [128, GL, 128], BF16) for d in range(2)]
    Wis = [sb(f"Wis{d}", [128, GL, 128], BF16) for d in range(2)]
    Mg = sb("Mg", [128, GL, 128], BF16)

    def tab_ap(t, d, jj0, jstep):
        return ap_of(t, d * 16 * GL + jj0 * GL, [[1, GL], [jstep * GL, 8], [0, 16]])

    def par_ap(t, d):
        return ap_of(t, d * GL * 16, [[16, GL], [0, 8], [1, 16]])

    def gen(dst, nm, ta, tb, na, nb_, pa, pb, npa, npb, d, jj0, jstep):
        k.op(V, lambda e: e.tensor_tensor(out=mt1[:], in0=tab_ap(ta, d, jj0, jstep), in1=par_ap(pa, d), op=ALU.mult),
             reads=[na, npa], writes=["mt1"])
        k.op(V, lambda e: e.tensor_tensor(out=mt2[:], in0=tab_ap(tb, d, jj0, jstep), in1=par_ap(pb, d), op=ALU.mult),
             reads=[nb_, npb], writes=["mt2"])
        k.op(V, lambda e: e.tensor_tensor(out=dst[:], in0=mt1[:], in1=mt2[:], op=ALU.add), reads=["mt1", "mt2"], writes=[nm])
    gen(Lm[0], "Lm0", qa, qb, "qa", "qb", b1, b2, "b1", "b2", 0, 14, -1)
    gen(Rm[0], "Rm0", t1t, t2t, "t1t", "t2t", c1, c2, "c1", "c2", 0, 0, 1)
    gen(Wo[0], "Wo0", t1t, t2t, "t1t", "t2t", c1, c2, "c1", "c2", 0, 8, 1)
    gen(Lm[1], "Lm1", qa, qb, "qa", "qb", b1, b2, "b1", "b2", 1, 7, 1)
    gen(Rm[1], "Rm1", t1t, t2t, "t1t", "t2t", c1, c2, "c1", "c2", 1, 7, -1)
    gen(Wo[1], "Wo1", t1t, t2t, "t1t", "t2t", c1, c2, "c1", "c2", 1, 15, -1)

    pg = [c.ps(f"pg{i}", [128, 512]) for i in range(2)]
    gi = 0
    mtmp = sb("mtmp", [128, 128]); mtmp2 = sb("mtmp2", [128, 128])
    for g in range(GL):
        for d in range(2):
            Lg = Lm[d][:, g].rearrange("p s c -> p (s c)")
            Wog = Wo[d][:, g].rearrange("p s c -> p (s c)")
            p = pg[gi % 2]; pn = f"pg{gi % 2}"; gi += 1
            k.op("tensor", lambda e, p=p, Lg=Lg: e.matmul(p[:, 0:128], lhsT=Lg, rhs=identb, start=True, stop=True),
                 reads=[f"Lm{d}", "consb"], writes=[pn])
            k.op("tensor", lambda e, p=p, Lg=Lg: e.matmul(p[:, 128:256], lhsT=Lg, rhs=pswapb, start=True, stop=True),
                 reads=[f"Lm{d}", "consb"], writes=[pn])
            k.op("tensor", lambda e, p=p, Wog=Wog: e.matmul(p[:, 256:384], lhsT=pswapb, rhs=Wog, start=True, stop=True),
                 reads=[f"Wo{d}", "consb"], writes=[pn])
            k.op("scalar", lambda e, p=p, d=d, g=g: e.copy(out=Wi[d][:, g, :], in_=p[:, 0:128]), reads=[pn], writes=[f"Wi{d}"])
            k.op("scalar", lambda e, p=p, d=d, g=g: e.copy(out=Wis[d][:, g, :], in_=p[:, 128:256]), reads=[pn], writes=[f"Wis{d}"])
            k.op("scalar", lambda e, p=p, d=d, g=g: e.copy(out=Wos[d][:, g, :], in_=p[:, 256:384]), reads=[pn], writes=[f"Wos{d}"])
        p = pg[gi % 2]; pn = f"pg{gi % 2}"; gi += 1
        for d in range(2):
            Lg = Lm[d][:, g].rearrange("p s c -> p (s c)")
            Rg = Rm[d][:, g].rearrange("p s c -> p (s c)")
            k.op("tensor", lambda e, p=p, Lg=Lg, Rg=Rg, d=d: e.matmul(p[:, 128 * d:128 * d + 128], lhsT=Lg, rhs=Rg, start=True, stop=True),
                 reads=[f"Lm{d}", f"Rm{d}"], writes=[pn])
        k.op(V, lambda e, p=p: e.tensor_tensor(out=mtmp[:], in0=p[:, 0:128], in1=cons[:, 2, :], op=ALU.mult), reads=[pn, "cons"], writes=["mtmp"])
        k.op(V, lambda e, p=p: e.tensor_tensor(out=mtmp2[:], in0=p[:, 128:256], in1=cons[:, 3, :], op=ALU.mult), reads=[pn, "cons"], writes=["mtmp2"])
        k.op(V, lambda e: e.tensor_tensor(out=mtmp[:], in0=mtmp[:], in1=mtmp2[:], op=ALU.add), reads=["mtmp", "mtmp2"], writes=["mtmp"])
        k.op(V, lambda e, g=g: e.scalar_tensor_tensor(out=Mg[:, g, :], in0=cons[:, 0, :], scalar=dcol[:, g:g + 1], in1=mtmp[:],
                                                      op0=ALU.mult, op1=ALU.add), reads=["cons", "dcol", "mtmp"], writes=["Mg"])

    pS = c.ps("pS", [128, 1024]); pSw = c.ps("pSw", [128, 1024]); pY = c.ps("pY", [128, 512])
    xf = [sb(f"xf{i}", [128, NK], BF16) for i in range(2)]
    xb = [sb(f"xb{i}", [128, NK], BF16) for i in range(2)]
    ctabs = [sb(f"ctab{i}", [128, NK]) for i in range(2)]; stabs = [sb(f"stab{i}", [128, NK]) for i in range(2)]; ty = sb("ty", [128, NK]); tr = sb("tr", [128, NK])
    sp = sb("sp", [128, NK]); sp2 = sb("sp2", [128, NK]); ggs = [sb(f"gg{i}", [128, NK]) for i in range(2)]
    GC = [[sb(f"GC{b}{d}", [128, NK], BF16) for d in range(2)] for b in range(2)]
    GS = [[sb(f"GS{b}{d}", [128, NK], BF16) for d in range(2)] for b in range(2)]
    yo = [sb(f"yo{i}", [128, 512]) for i in range(2)]
    for g in range(GL):
        for b in range(2):
            k.dma("sync", xf[b][:], XF[b, g], writes=[f"xf{b}"])
            k.dma("sync", xb[b][:], XB[b, g], writes=[f"xb{b}"])
        for d in range(2):
            ctab = ctabs[d]; stab = stabs[d]; cn = f"ctab{d}"; sn_ = f"stab{d}"
            def table(dst, nm, phcol, shift):
                k.op(V, lambda e: e.tensor_scalar(out=ty[:], in0=iota[:], scalar1=phcol, scalar2=shift, op0=ALU.mult, op1=ALU.add),
                     reads=["iota", "ph", "phs"], writes=["ty"])
                k.op(V, lambda e: e.tensor_scalar(out=tr[:], in0=ty[:], scalar1=MAGIC, scalar2=MAGIC, op0=ALU.add, op1=ALU.subtract),
                     reads=["ty"], writes=["tr"])
                k.op(V, lambda e: e.tensor_tensor(out=ty[:], in0=ty[:], in1=tr[:], op=ALU.subtract), reads=["ty", "tr"], writes=["ty"])
                k.op("scalar", lambda e: e.activation(out=dst[:], in_=ty[:], func=AF.Sin, scale=TWO_PI), reads=["ty"], writes=[nm])
            table(stab, sn_, phs[:, d, g:g + 1], 0.0)
            table(ctab, cn, ph[:, d, g:g + 1], 0.25)
            rho_bc = ap_of(emag, d * 16 * GL + 15 * GL + g, [[0, NK]])
            for b in range(2):
                gg = ggs[b]; gn = f"gg{b}"
                X = xf[b] if d == 0 else xb[b]
                xn = f"xf{b}" if d == 0 else f"xb{b}"
                for (P, pn, W, wn) in ((pS, "pS", Wi[d], f"Wi{d}"), (pSw, "pSw", Wis[d], f"Wis{d}")):
                    k.op("tensor", lambda e, P=P, W=W, X=X, g=g: e.matmul(P[:, 0:512], lhsT=W[:, g, :], rhs=X[:, 0:512], start=True, stop=True),
                         reads=[wn, xn], writes=[pn])
                    k.op("tensor", lambda e, P=P, W=W, X=X, g=g: e.matmul(P[:, 512:NK], lhsT=W[:, g, :], rhs=X[:, 512:NK], start=True, stop=True),
                         reads=[wn, xn], writes=[pn])
                k.op(V, lambda e, ctab=ctab: e.tensor_tensor(out=sp[:], in0=pS[:, 0:NK], in1=ctab[:], op=ALU.mult), reads=["pS", cn], writes=["sp"])
                k.op(V, lambda e, stab=stab: e.tensor_tensor(out=sp2[:], in0=pSw[:, 0:NK], in1=stab[:], op=ALU.mult), reads=["pSw", sn_], writes=["sp2"])
                k.op(V, lambda e: e.tensor_tensor(out=sp[:], in0=sp[:], in1=sp2[:], op=ALU.add), reads=["sp", "sp2"], writes=["sp"])
                k.op(V, lambda e, rho_bc=rho_bc, gg=gg: e.tensor_tensor_scan(out=gg[:], data0=rho_bc, data1=sp[:], initial=0.0, op0=ALU.mult, op1=ALU.add),
                     reads=["emag", "sp"], writes=[gn])
                k.op(V, lambda e, b=b, d=d, gg=gg, ctab=ctab: e.tensor_tensor(out=GC[b][d][:], in0=gg[:], in1=ctab[:], op=ALU.mult), reads=[gn, cn], writes=[f"GC{b}{d}"])
                k.op("gpsimd", lambda e, b=b, d=d, gg=gg, stab=stab: e.tensor_tensor(out=GS[b][d][:], in0=gg[:], in1=stab[:], op=ALU.mult), reads=[gn, sn_], writes=[f"GS{b}{d}"])
        for b in range(2):
            def rev(t):
                return ap_of(t, 542, [[-1, 512]])
            mm = [(Mg[:, g, :], xf[b][:, 32:544], ["Mg", f"xf{b}"]),
                  (Wo[0][:, g].rearrange("p s c -> p (s c)"), GC[b][0][:, 31:543], ["Wo0", f"GC{b}0"]),
                  (Wos[0][:, g, :], GS[b][0][:, 31:543], ["Wos0", f"GS{b}0"]),
                  (Wo[1][:, g].rearrange("p s c -> p (s c)"), rev(GC[b][1]), ["Wo1", f"GC{b}1"]),
                  (Wos[1][:, g, :], rev(GS[b][1]), ["Wos1", f"GS{b}1"])]
            for i, (lh, rh, rd) in enumerate(mm):
                k.op("tensor", lambda e, lh=lh, rh=rh, i=i: e.matmul(pY[:], lhsT=lh, rhs=rh, start=(i == 0), stop=(i == 4)),
                     reads=rd, writes=["pY"])
            y = yo[b]
            k.op("scalar", lambda e, y=y: e.copy(out=y[:], in_=pY[:]), reads=["pY"], writes=[f"yo{b}"])
            k.dma("sync", YC[b, g], y[:], reads=[f"yo{b}"], writes=[f"YC{b}{g}"])
    if debug:
        for nm, t, shp, dt_ in (("pr", pr, shp, F32), ("pi", pi, shp, F32), ("qa", qa, shp, F32), ("qb", qb, shp, F32),
                                ("fr", fr, s3, F32), ("fi", fi, s3, F32), ("Lm0", Lm[0], msh, BF16), ("Rm0", Rm[0], msh, BF16),
                                ("Wo0", Wo[0], msh, BF16), ("Mg", Mg, [128, GL, 128], BF16), ("Wi0", Wi[0], [128, GL, 128], BF16),
                                ("Wis0", Wis[0], [128, GL, 128], BF16), ("Wos0", Wos[0], [128, GL, 128], BF16),
                                ("ctab1", ctabs[1], [128, NK], F32), ("stab1", stabs[1], [128, NK], F32), ("gg1", ggs[1], [128, NK], F32),
                                ("sp", sp, [128, NK], F32), ("GC00", GC[0][0], [128, NK], BF16)):
            o = c.dout("dbg_" + nm, shp, dt_)
            k.dma("sync", o[:], t[:], reads=[nm], writes=["dbg_" + nm])
    return c.done()


def s5_consts():
    idx = np.arange(128)
    ident = np.eye(128, dtype=np.float32)
    pswap = np.zeros((128, 128), np.float32); pswap[idx, (idx + 64) % 128] = 1
    s_of = idx // 16
    maskF = (s_of[None, :] >= s_of[:, None]).astype(np.float32)
    maskB = (s_of[None, :] <= s_of[:, None]).astype(np.float32)
    cons = np.stack([ident, pswap, maskF, maskB], 1)
    sg = np.where(idx < 64, 1.0, -1.0).astype(np.float32)
    sgn = np.stack([sg, -sg], 1)
    jfac = np.broadcast_to((np.arange(16, dtype=np.float32) - 7)[None, :, None], (128, 16, GL)).copy()
    iota = np.broadcast_to(np.arange(NK, dtype=np.float32)[None], (128, NK)).copy()
    return cons, sgn, jfac, iota


def s5_in_maps(u, uc, inp):
    cons, sgn, jfac, iota = s5_consts()
    bf = u.dtype
    XF = np.zeros((2, 128, 128, NK), bf); XB = np.zeros((2, 128, 128, NK), bf)
    for b in range(2):
        seq = np.concatenate([uc[b], u[b]], 0).reshape(544, 8, 128, 16)
        chb = np.concatenate([seq[:32][::-1], seq[32:][::-1]], 0)
        XF[b, :, :, :544] = seq.transpose(2, 1, 3, 0).reshape(128, 128, 544)
        XB[b, :, :, :544] = chb.transpose(2, 1, 3, 0).reshape(128, 128, 544)
    lre = inp["s5_lam_re"][0]; lim = inp["s5_lam_im"][0]
    def rep(a):
        t = a.transpose(2, 0, 1)
        return np.concatenate([t, t], 0)
    LAM = np.stack([rep(lre), rep(lim)], 1)
    LST = np.broadcast_to(inp["s5_log_step"][0][None], (128, 2, 128)).copy()
    bre = inp["s5_b_re"][0].transpose(2, 0, 1, 3); bim = inp["s5_b_im"][0].transpose(2, 0, 1, 3)
    cre = inp["s5_c_re"][0].transpose(3, 0, 1, 2); cim = inp["s5_c_im"][0].transpose(3, 0, 1, 2)
    B1 = np.concatenate([bre, bim], 0); B2 = np.concatenate([bim, bre], 0)
    C1 = np.concatenate([cre, cim], 0); C2 = np.concatenate([cim, cre], 0)
    dd = inp["s5_d"][0].reshape(128, 16)
    DCOL = np.tile(dd.T, (8, 1))
    maps = []
    for core in range(8):
        gs = slice(16 * core, 16 * core + 16)
        maps.append({
            "XF": np.ascontiguousarray(XF[:, gs]), "XB": np.ascontiguousarray(XB[:, gs]),
            "LAM": np.ascontiguousarray(LAM[..., gs]).astype(np.float32), "LST": np.ascontiguousarray(LST[..., gs]).astype(np.float32),
            "B1": np.ascontiguousarray(B1[:, :, gs]), "B2": np.ascontiguousarray(B2[:, :, gs]),
            "C1": np.ascontiguousarray(C1[:, :, gs]), "C2": np.ascontiguousarray(C2[:, :, gs]),
            "DCOL": np.ascontiguousarray(DCOL[:, gs]).astype(np.float32),
            "CONS": cons, "SGN": sgn, "JFAC": jfac, "IOTA": iota})
    return maps


def s5_gather(res):
    y = np.zeros((2, L, D), np.float32)
    for core in range(8):
        yc = res[core]["YC"].reshape(2, 16, 8, 16, 512)
        y[:, :, 256 * core:256 * core + 256] = yc.transpose(0, 4, 2, 1, 3).reshape(2, L, 256)
    return y


def build_mod():
    c = Ctx(); k = c.k
    CND = c.din("CND", [128, 16, 4])
    AW = c.din("AW", [2, 2048, 1536])
    AB = c.din("AB", [2, 128, 12])
    MOD = c.dout("MOD", [2, 128, 12, 4])
    cnd = c.sb("cnd", [128, 16, 4]); ab = c.sb("ab", [2, 128, 12]) if False else None
    abt = [c.sb(f"abt{l}", [128, 12]) for l in range(2)]
    c.dma("sync", cnd[:], CND[:], writes=["cnd"])
    for l in range(2):
        c.dma("sync", abt[l][:], AB[l], writes=[f"abt{l}"])
    cond = c.sb("cond", [128, 16, 4])
    c.act(cond[:], cnd[:], AF.Silu, ["cnd"], ["cond"])
    wb = [c.sb(f"wb{i}", [128, 16, 512]) for i in range(2)]
    pm = [c.ps(f"pm{i}", [128, 512]) for i in range(2)]
    mo = c.sb("mo", [128, 2, 12, 4])
    n = 0
    for l in range(2):
        for q in range(3):
            w = wb[n % 2]; wn = f"wb{n % 2}"; p = pm[n % 2]; pn = f"pm{n % 2}"; n += 1
            c.dma("sync", w[:], AW[l, :, 512 * q:512 * q + 512].rearrange("(kc p) n -> p kc n", p=128), writes=[wn])
            for j in range(4):
                c.mmg(p[:, 4 * j:4 * j + 4], [(w[:, kc, 128 * j:128 * j + 128], cond[:, kc, :]) for kc in range(16)], [wn, "cond"], [pn])
            for j in range(4):
                blk = 4 * q + j
                c.ts("vector", mo[:, l, blk, :], p[:, 4 * j:4 * j + 4], abt[l][:, blk:blk + 1], None, ALU.add, None, [pn, f"abt{l}"], ["mo"])
    c.dma("sync", MOD.rearrange("l p b r -> p l b r"), mo[:], reads=["mo"], writes=["MOD"])
    return c.done()


def col_tiles(nt):
    n = (nt + 511) // 512
    sz = (nt + n - 1) // n
    return [(i * sz, min(nt, (i + 1) * sz)) for i in range(n)]


def rms_rstd(c, X, xname, nt, rstd, rname, ones, sq, psq):
    cts = col_tiles(nt)
    for blk in range(NB):
        s_ = sq[blk % 2]; sn_ = f"sq{blk % 2}"
        c.act(s_[:, 0:nt], X[:, blk, :], AF.Square, [xname], [sn_])
        for i, (c0, c1) in enumerate(cts):
            c.mm(psq[i][:, 0:c1 - c0], ones[:], s_[:, c0:c1], blk == 0, blk == NB - 1, ["ones", sn_], [f"psq{i}"])
    for i, (c0, c1) in enumerate(cts):
        c.ts("vector", rstd[:, c0:c1], psq[i][:, 0:c1 - c0], 1.0 / D, EPS, ALU.mult, ALU.add, [f"psq{i}"], [rname])
    c.act(rstd[:, 0:nt], rstd[:, 0:nt], AF.Sqrt, [rname], [rname])
    c.recip(rstd[:, 0:nt], rstd[:, 0:nt], [rname], [rname])


def build_prep():
    c = Ctx(); k = c.k
    NT = 1024; NC = 64
    XT = c.din("XT", [NB, 128, NT]); CT = c.din("CT", [NB, 128, NC])
    MV = c.din("MV", [128, NB, 4])
    G0 = c.din("G0", [128, NB])
    RIDX = c.din("RIDX", [128, NT]); CIDX = c.din("CIDX", [128, NT]); JIDX = c.din("JIDX", [128, 4])
    XPT = c.dout("XPT", [NB, 128, NT]); UT = c.dout("UT", [NB, 128, NT], BF16); UCT = c.dout("UCT", [NB, 128, NC], BF16)
    xp = c.sb("xp", [128, NB, NT]); xc = c.sb("xc", [128, NB, NC])
    mv = c.sb("mv", [128, NB, 4]); g0 = c.sb("g0", [128, NB]); ridx = c.sb("ridx", [128, NT]); cidx = c.sb("cidx", [128, NT])
    jidx = c.sb("jidx", [128, 4]); om = c.sb("om", [128, 4]); ones = c.sb("ones", [128, 128])
    c.memset("vector", ones[:], 1.0, ["ones"])
    for blk in range(NB):
        c.dma("sync", xp[:, blk, :], XT[blk], writes=[f"xp{blk}"])
    c.dma("sync", xc[:], CT.rearrange("b p t -> p b t"), writes=["xc"])
    for dst, src, nm in ((mv, MV, "mv"), (g0, G0, "g0"), (ridx, RIDX, "ridx"), (cidx, CIDX, "cidx"), (jidx, JIDX, "jidx")):
        c.dma("sync", dst[:], src[:], writes=[nm])
    c.act(om[:], jidx[:], AF.Exp, ["jidx"], ["om"], scale=-math.log(10000.0) / 512.0)
    c.ts("vector", om[:], om[:], 1.0 / TWO_PI, None, ALU.mult, None, ["om"], ["om"])
    ty = [c.sb(f"ty{i}", [128, NT]) for i in range(2)]; tr = [c.sb(f"tr{i}", [128, NT]) for i in range(2)]
    for blk in range(NB):
        idx, inm = (ridx, "ridx") if blk < 8 else (cidx, "cidx")
        shift = 0.25 if (blk // 4) % 2 == 1 else 0.0
        y = ty[blk % 2]; yn = f"ty{blk % 2}"; r = tr[blk % 2]; rn = f"tr{blk % 2}"
        c.ts("vector", y[:], idx[:], om[:, blk % 4:blk % 4 + 1], shift, ALU.mult, ALU.add, [inm, "om"], [yn])
        c.ts("vector", r[:], y[:], MAGIC, MAGIC, ALU.add, ALU.subtract, [yn], [rn])
        c.tt("vector", y[:], y[:], r[:], ALU.subtract, [yn, rn], [yn])
        c.act(r[:], y[:], AF.Sin, [yn], [rn], scale=TWO_PI)
        c.tt("gpsimd", xp[:, blk, :], xp[:, blk, :], r[:], ALU.add, [f"xp{blk}", rn], [f"xp{blk}"])
        c.dma("sync", XPT[blk], xp[:, blk, :], reads=[f"xp{blk}"], writes=[f"XPT{blk}"])
    sq = [c.sb(f"sq{i}", [128, NT]) for i in range(2)]
    psq = [c.ps(f"psq{i}", [128, 512]) for i in range(2)]
    rstd = c.sb("rstd", [128, NT]); rstc = c.sb("rstc", [128, NC])
    allx = [f"xp{b}" for b in range(NB)]
    cts = col_tiles(NT)
    for blk in range(NB):
        s_ = sq[blk % 2]; sn_ = f"sq{blk % 2}"
        c.act(s_[:], xp[:, blk, :], AF.Square, [f"xp{blk}"], [sn_])
        for i, (c0, c1) in enumerate(cts):
            c.mm(psq[i][:, 0:c1 - c0], ones[:], s_[:, c0:c1], blk == 0, blk == NB - 1, ["ones", sn_], [f"psq{i}"])
    for i, (c0, c1) in enumerate(cts):
        c.ts("vector", rstd[:, c0:c1], psq[i][:, 0:c1 - c0], 1.0 / D, EPS, ALU.mult, ALU.add, [f"psq{i}"], ["rstd"])
    c.act(rstd[:], rstd[:], AF.Sqrt, ["rstd"], ["rstd"])
    c.recip(rstd[:], rstd[:], ["rstd"], ["rstd"])
    pc = c.ps("pc", [128, 512])
    for blk in range(NB):
        s_ = sq[blk % 2]; sn_ = f"sq{blk % 2}"
        c.act(s_[:, 0:NC], xc[:, blk, :], AF.Square, ["xc"], [sn_])
        c.mm(pc[:, 0:NC], ones[:], s_[:, 0:NC], blk == 0, blk == NB - 1, ["ones", sn_], ["pc"])
    c.ts("vector", rstc[:], pc[:, 0:NC], 1.0 / D, EPS, ALU.mult, ALU.add, ["pc"], ["rstc"])
    c.act(rstc[:], rstc[:], AF.Sqrt, ["rstc"], ["rstc"])
    c.recip(rstc[:], rstc[:], ["rstc"], ["rstc"])
    gm = c.sb("gm", [128, NB, 2])
    for r_ in range(2):
        c.ts("vector", gm[:, :, r_], mv[:, :, 2 * r_ + 1], 1.0, None, ALU.add, None, ["mv"], ["gm"])
        c.tt("vector", gm[:, :, r_], gm[:, :, r_], g0[:], ALU.mult, ["gm", "g0"], ["gm"])
    ub = [c.sb(f"ub{i}", [128, NT], BF16) for i in range(2)]
    ucb = c.sb("ucb", [128, NB, NC], BF16); tcx = c.sb("tcx", [128, NC])
    for blk in range(NB):
        y = ty[blk % 2]; yn = f"ty{blk % 2}"; u = ub[blk % 2]; un = f"ub{blk % 2}"
        c.tt("vector", y[:], xp[:, blk, :], rstd[:], ALU.mult, [f"xp{blk}", "rstd"], [yn])
        c.ts("gpsimd", u[:], y[:], gm[:, blk, 0:1], mv[:, blk, 0:1], ALU.mult, ALU.add, [yn, "gm", "mv"], [un])
        c.dma("sync", UT[blk], u[:], reads=[un], writes=[f"UT{blk}"])
        c.tt("vector", tcx[:], xc[:, blk, :], rstc[:], ALU.mult, ["xc", "rstc"], ["tcx"])
        c.ts("vector", ucb[:, blk, :], tcx[:], gm[:, blk, 1:2], mv[:, blk, 2:3], ALU.mult, ALU.add, ["tcx", "gm", "mv"], ["ucb"])
    c.dma("sync", UCT.rearrange("b p t -> p b t"), ucb[:], reads=["ucb"], writes=["UCT"])
    return c.done()


def build_layer(kind, stop=0):
    c = Ctx(); k = c.k
    H = 1 if kind == 0 else 9
    NT = 1024 + 2 * H
    cts = col_tiles(NT)
    XIN = c.din("XIN", [NB, 128, NT])
    MV = c.din("MV", [128, NB, 6]); NG = c.din("NG", [128, NB, 4])
    VM = c.din("VM", [128, NT])
    UP = c.din("UP", [D, 2 * DFF]); DOWN = c.din("DOWN", [DFF, D]); CONV = c.din("CONV", [128, 2 * NJ, 4])
    if kind == 0:
        YT = c.din("YT", [NB, 128, NT]); GLUW = c.din("GLUW", [D, 2 * D])
    else:
        POOLW = c.din("POOLW", [4, 512, 512]); PSC = c.din("PSC", [128, NB]); INVC = c.din("INVC", [128, 4, NT])
    XOT = c.dout("XOT", [NB, 128, 1024])

    XS = c.nc.dram_tensor("XS", [NB, 128, NT], F32, kind="Internal").ap()
    mv = c.sb("mv", [128, NB, 6]); ng = c.sb("ng", [128, NB, 4]); vm = c.sb("vm", [128, NT])
    conv = c.sb("conv", [128, 2 * NJ, 4]); ones = c.sb("ones", [128, 128])
    zu = c.sb("zu", [128, NB, NT], BF16)
    rstd = c.sb("rstd", [128, NT]); coef = c.sb("coef", [128, NB, 4])
    sq = [c.sb(f"sq{i}", [128, NT]) for i in range(2)]
    es1 = ExitStack()
    xp = c.sb("xp", [128, NB, NT], es=es1)
    c.memset("vector", ones[:], 1.0, ["ones"])
    for blk in range(NB):
        c.dma("sync", xp[:, blk, :], XIN[blk], writes=["xp"])
    for dst, src, nm in ((mv, MV, "mv"), (ng, NG, "ng"), (vm, VM, "vm"), (conv, CONV, "conv")):
        c.dma("sync", dst[:], src[:], writes=[nm])
    c.tt("vector", coef[:, :, 0], mv[:, :, 2], ng[:, :, 1], ALU.mult, ["mv", "ng"], ["coef"])
    c.ts("vector", coef[:, :, 1], mv[:, :, 4], 1.0, None, ALU.add, None, ["mv"], ["coef"])
    c.tt("vector", coef[:, :, 1], coef[:, :, 1], ng[:, :, 2], ALU.mult, ["coef", "ng"], ["coef"])
    c.tt("vector", coef[:, :, 2], mv[:, :, 5], ng[:, :, 3], ALU.mult, ["mv", "ng"], ["coef"])
    c.ts("vector", coef[:, :, 3], mv[:, :, 1], 1.0, None, ALU.add, None, ["mv"], ["coef"])
    c.tt("vector", coef[:, :, 3], coef[:, :, 3], ng[:, :, 0], ALU.mult, ["coef", "ng"], ["coef"])

    psq = [c.ps(f"psq{i}", [128, 512]) for i in range(3)]
    pa = [c.ps(f"pa{i}", [128, 512]) for i in range(4)]
    yv = c.sb("yv", [128, NB, NT], BF16, es=es1)
    if kind == 0:
        zf = zu
        yst = [c.sb(f"yst{i}", [128, NT], es=es1) for i in range(2)]
        for blk in range(NB):
            y = yst[blk % 2]; yn = f"yst{blk % 2}"
            c.dma("sync", y[:], YT[blk], writes=[yn])
            c.act(zf[:, blk, :], y[:], AF.Gelu_apprx_tanh, [yn], ["zu"])
        wg = [c.sb(f"wg{i}", [128, 16, 2, 256], BF16, es=es1) for i in range(2)]
        sg = [c.sb(f"sg{i}", [128, 512], es=es1) for i in range(2)]
        n = 0
        for i in range(NB):
            w = wg[(i // 2) % 2]; wn = f"wg{(i // 2) % 2}"; wc0 = 128 * (i % 2)
            if i % 2 == 0:
                for h in range(2):
                    c.dma("gpsimd", w[:, :, h, :], GLUW[:, D * h + 128 * i:D * h + 128 * i + 256].rearrange("(kc p) n -> p kc n", p=128), writes=[wn])
            for (c0, c1) in cts:
                pv = pa[(2 * n) % 4]; pvn = f"pa{(2 * n) % 4}"; pg_ = pa[(2 * n + 1) % 4]; pgn = f"pa{(2 * n + 1) % 4}"
                s_ = sg[n % 2]; sn_ = f"sg{n % 2}"; n += 1
                c.mmg(pv[:, 0:c1 - c0], [(w[:, kc, 0, wc0:wc0 + 128], zf[:, kc, c0:c1]) for kc in range(16)], [wn, "zu"], [pvn])
                c.mmg(pg_[:, 0:c1 - c0], [(w[:, kc, 1, wc0:wc0 + 128], zf[:, kc, c0:c1]) for kc in range(16)], [wn, "zu"], [pgn])
                c.act(s_[:, 0:c1 - c0], pg_[:, 0:c1 - c0], AF.Sigmoid, [pgn], [sn_])
                c.tt("vector", yv[:, i, c0:c1], pv[:, 0:c1 - c0], s_[:, 0:c1 - c0], ALU.mult, [pvn, sn_], ["yv"])
    else:
        pp = zu
        invc = c.sb("invc", [128, NT], es=es1); psc = c.sb("psc", [128, NB], es=es1)
        c.dma("sync", psc[:], PSC[:], writes=["psc"])
        rms_rstd(c, xp, "xp", NT, rstd, "rstd", ones, sq, psq)
        W = NT + 32
        ua = [c.sb(f"ua{i}", [128, W], es=es1) for i in range(3)]
        for i in range(3):
            c.memset("vector", ua[i][:], 0.0, [f"ua{i}"])
        ut = c.sb("ut", [128, NT], es=es1)
        for blk in range(NB):
            m = 1 + blk // 4
            if blk % 4 == 0:
                c.dma("sync", invc[:], INVC[:, m - 1, :], writes=["invc"])
            c.tt("vector", ut[:], xp[:, blk, :], rstd[:, 0:NT], ALU.mult, ["xp", "rstd"], ["ut"])
            c.ts("vector", ut[:], ut[:], coef[:, blk, 3:4], mv[:, blk, 0:1], ALU.mult, ALU.add, ["ut", "coef", "mv"], ["ut"])
            c.tt("vector", ua[0][:, 16:16 + NT], ut[:], vm[:], ALU.mult, ["ut", "vm"], ["ua0"])
            cur = 0
            for lvl in range(m):
                sh = 1 << lvl
                nxt = 1 + (lvl % 2)
                c.tt("vector", ua[nxt][:, 16:W], ua[cur][:, 16:W], ua[cur][:, 16 - sh:W - sh], ALU.add, [f"ua{cur}"], [f"ua{nxt}"])
                cur = nxt
            w2 = (1 << m) // 2
            off = 16 + w2 - 1
            c.tt("vector", ut[:], ua[cur][:, off:off + NT], invc[:], ALU.mult, [f"ua{cur}", "invc"], ["ut"])
            c.tt("vector", pp[:, blk, :], ut[:], ua[0][:, 16:16 + NT], ALU.subtract, ["ut", "ua0"], ["zu"])
        wp = [c.sb(f"wp{i}", [128, 4, 128], BF16, es=es1) for i in range(2)]
        n = 0
        for gi in range(4):
            for bo in range(4):
                w = wp[(4 * gi + bo) % 2]; wn = f"wp{(4 * gi + bo) % 2}"
                c.dma("gpsimd", w[:], POOLW[gi, :, 128 * bo:128 * bo + 128].rearrange("(kc p) n -> p kc n", p=128), writes=[wn])
                for (c0, c1) in cts:
                    p = pa[n % 4]; pn = f"pa{n % 4}"; n += 1
                    c.mmg(p[:, 0:c1 - c0], [(w[:, kc, :], pp[:, 4 * gi + kc, c0:c1]) for kc in range(4)], [wn, "zu"], [pn])
                    c.ts("vector", yv[:, 4 * gi + bo, c0:c1], p[:, 0:c1 - c0], psc[:, 4 * gi + bo:4 * gi + bo + 1], None, ALU.mult, None, [pn, "psc"], ["yv"])
    rms_rstd(c, yv, "yv", NT, rstd, "rstd", ones, sq, psq)
    for blk in range(NB):
        s_ = sq[blk % 2]; sn_ = f"sq{blk % 2}"
        c.stt(s_[:, 0:NT], yv[:, blk, :], coef[:, blk, 0:1], rstd[:, 0:NT], ALU.mult, ALU.mult, ["yv", "coef", "rstd"], [sn_])
        c.tt("gpsimd", xp[:, blk, :], xp[:, blk, :], s_[:, 0:NT], ALU.add, ["xp", sn_], ["xp"])

    if stop == 1:
        for blk in range(NB):
            c.dma("sync", XOT[blk], xp[:, blk, H:H + 1024], reads=["xp"], writes=[f"XOT{blk}"])
        k.barrier()
        es1.close()
        return c.done()
    u2 = zu
    rms_rstd(c, xp, "xp", NT, rstd, "rstd", ones, sq, psq)
    for blk in range(NB):
        s_ = sq[blk % 2]; sn_ = f"sq{blk % 2}"
        c.tt("vector", s_[:, 0:NT], xp[:, blk, :], rstd[:, 0:NT], ALU.mult, ["xp", "rstd"], [sn_])
        c.ts("vector", s_[:, 0:NT], s_[:, 0:NT], coef[:, blk, 1:2], mv[:, blk, 3:4], ALU.mult, ALU.add, [sn_, "coef", "mv"], [sn_])
        c.tt("gpsimd", u2[:, blk, :], s_[:, 0:NT], vm[:], ALU.mult, [sn_, "vm"], ["zu"])
        c.dma("sync", XS[blk], xp[:, blk, :], reads=["xp"], writes=[f"XS{blk}"])
    k.barrier()
    es1.close()
    es2 = ExitStack()
    A = c.sb("A", [128, NJ, NT], BF16, es=es2)
    es3 = ExitStack()
    wu = [c.sb(f"wu{i}", [128, 16, 2, 256], BF16, es=es3) for i in range(2)]
    hs = [[c.sb(f"hs{i}{h}", [128, NT + 2], es=es3) for h in range(2)] for i in range(2)]
    hc = [[c.sb(f"hc{i}{h}", [128, NT], es=es3) for h in range(2)] for i in range(2)]
    for i in range(2):
        for h in range(2):
            c.memset("vector", hs[i][h][:], 0.0, [f"hs{i}{h}"])
    n = 0
    for j in range(NJ):
        w = wu[(j // 2) % 2]; wn = f"wu{(j // 2) % 2}"; wc0 = 128 * (j % 2)
        if j % 2 == 0:
            for h in range(2):
                c.dma("gpsimd", w[:, :, h, :], UP[:, DFF * h + 128 * j:DFF * h + 128 * j + 256].rearrange("(kc p) n -> p kc n", p=128), writes=[wn])
        for h in range(2):
            hsb = hs[j % 2][h]; hsn = f"hs{j % 2}{h}"; hcb = hc[j % 2][h]; hcn = f"hc{j % 2}{h}"
            cb = NJ * h + j
            for (c0, c1) in cts:
                p = pa[n % 4]; pn = f"pa{n % 4}"; n += 1
                c.mmg(p[:, 0:c1 - c0], [(w[:, kc, h, wc0:wc0 + 128], u2[:, kc, c0:c1]) for kc in range(16)], [wn, "zu"], [pn])
                c.cp("scalar", hsb[:, 1 + c0:1 + c1], p[:, 0:c1 - c0], [pn], [hsn])
            c.act(hcb[:], hsb[:, 1:NT + 1], AF.Identity, [hsn, "conv"], [hcn], scale=conv[:, cb, 1:2], bias=conv[:, cb, 3:4])
            c.stt(hcb[:], hsb[:, 0:NT], conv[:, cb, 0:1], hcb[:], ALU.mult, ALU.add, [hsn, "conv", hcn], [hcn])
            c.stt(hcb[:], hsb[:, 2:NT + 2], conv[:, cb, 2:3], hcb[:], ALU.mult, ALU.add, [hsn, "conv", hcn], [hcn])
        hv = hc[j % 2][0]; hg = hc[j % 2][1]
        c.act(hg[:], hg[:], AF.Silu, [f"hc{j % 2}1"], [f"hc{j % 2}1"])
        c.tt("vector", A[:, j, :], hv[:], hg[:], ALU.mult, [f"hc{j % 2}0", f"hc{j % 2}1"], ["A"])
    k.barrier()
    es3.close()
    if stop == 3:
        xl = [c.sb(f"xl{i}", [128, NT]) for i in range(2)]
        for blk in range(NB):
            c.cp("vector", xl[blk % 2][:], A[:, blk, :], ["A"], [f"xl{blk % 2}"])
            c.dma("sync", XOT[blk], xl[blk % 2][:, H:H + 1024], reads=[f"xl{blk % 2}"], writes=[f"XOT{blk}"])
        k.barrier()
        c.es.pop_all().close() if False else None
        nc_ = c.k
        c.k.finish(); c.k.run(); c.k.close()
        return c.nc
    es4 = ExitStack()
    wd = [c.sb(f"wd{i}", [128, NJ, 256], BF16, es=es4) for i in range(2)]
    F = zu
    xl = [c.sb(f"xl{i}", [128, NT], es=es4) for i in range(2)]
    n = 0
    ctd = [(H, H + 512), (H + 512, H + 1024)]
    for blk in range(NB):
        w = wd[(blk // 2) % 2]; wn = f"wd{(blk // 2) % 2}"; wc0 = 128 * (blk % 2)
        if blk % 2 == 0:
            for jq in range(4):
                c.dma("gpsimd", w[:, 11 * jq:11 * jq + 11, :],
                      DOWN[1408 * jq:1408 * jq + 1408, 128 * blk:128 * blk + 256].rearrange("(j p) n -> p j n", p=128), writes=[wn])
        s_ = sq[blk % 2]; sn_ = f"sq{blk % 2}"
        for i, (c0, c1) in enumerate(ctd):
            p = pa[n % 4]; pn = f"pa{n % 4}"; n += 1
            c.mmg(p[:, 0:c1 - c0], [(w[:, j, wc0:wc0 + 128], A[:, j, c0:c1]) for j in range(NJ)], [wn, "A"], [pn])
            c.cp("vector", F[:, blk, c0:c1], p[:, 0:c1 - c0], [pn], ["zu"])
            c.act(s_[:, c0:c1], F[:, blk, c0:c1], AF.Square, ["zu"], [sn_])
        for i, (c0, c1) in enumerate(ctd):
            c.mm(psq[i][:, 0:c1 - c0], ones[:], s_[:, c0:c1], blk == 0, blk == NB - 1, ["ones", sn_], [f"psq{i}"])
    for i, (c0, c1) in enumerate(ctd):
        c.ts("vector", rstd[:, c0:c1], psq[i][:, 0:c1 - c0], 1.0 / D, EPS, ALU.mult, ALU.add, [f"psq{i}"], ["rstd"])
    c.act(rstd[:, H:H + 1024], rstd[:, H:H + 1024], AF.Sqrt, ["rstd"], ["rstd"])
    c.recip(rstd[:, H:H + 1024], rstd[:, H:H + 1024], ["rstd"], ["rstd"])
    for blk in range(NB):
        s_ = sq[blk % 2]; sn_ = f"sq{blk % 2}"
        x_ = xl[blk % 2]; xn_ = f"xl{blk % 2}"
        c.dma("sync", x_[:], XS[blk], reads=[f"XS{blk}"], writes=[xn_])
        c.stt(s_[:, H:H + 1024], F[:, blk, H:H + 1024], coef[:, blk, 2:3], rstd[:, H:H + 1024], ALU.mult, ALU.mult, ["zu", "coef", "rstd"], [sn_])
        c.tt("vector", s_[:, H:H + 1024], x_[:, H:H + 1024], s_[:, H:H + 1024], ALU.add, [xn_, sn_], [sn_])
        c.dma("sync", XOT[blk], s_[:, H:H + 1024], reads=[sn_], writes=[f"XOT{blk}"])
    k.barrier()
    es4.close(); es2.close()
    return c.done()


_PROGS = {}


def _prog(name, fn, *a):
    key = (name,) + a
    if key not in _PROGS:
        _PROGS[key] = fn(*a)
    return _PROGS[key]


def _fm(a):
    return np.ascontiguousarray(a.T.reshape(NB, 128, a.shape[0]))


def _unfm(a):
    return a.reshape(D, a.shape[2]).T


def _pervec(v):
    return np.ascontiguousarray(v.reshape(NB, 128).T)


def _halo(a, q, h):
    out = np.zeros((1024 + 2 * h, a.shape[1]), a.dtype)
    lo = 1024 * q - h; hi = 1024 * q + 1024 + h
    s0 = max(lo, 0); s1 = min(hi, L)
    out[s0 - lo:s1 - lo] = a[s0:s1]
    return out


def kernel(x, c, ctx, c_ctx, ada_w, ada_b, norm_g, s5_lam_re, s5_lam_im, s5_log_step,
           s5_b_re, s5_b_im, s5_c_re, s5_c_im, s5_d, s5_glu_w, pool_w, pool_scale,
           ffn_up, ffn_conv, ffn_conv_b, ffn_down, _dbg=None):
    f32 = np.float32
    inp = dict(s5_lam_re=np.asarray(s5_lam_re, f32), s5_lam_im=np.asarray(s5_lam_im, f32), s5_log_step=np.asarray(s5_log_step, f32),
               s5_b_re=np.asarray(s5_b_re, f32), s5_b_im=np.asarray(s5_b_im, f32), s5_c_re=np.asarray(s5_c_re, f32),
               s5_c_im=np.asarray(s5_c_im, f32), s5_d=np.asarray(s5_d, f32))
    x = np.asarray(x, f32); c = np.asarray(c, f32); ctx = np.asarray(ctx, f32); c_ctx = np.asarray(c_ctx, f32)
    ada_w = np.asarray(ada_w, f32); ada_b = np.asarray(ada_b, f32); norm_g = np.asarray(norm_g, f32)
    cores = list(range(8))
    cnd = np.zeros((128, 16, 4), f32)
    for r, v in enumerate((c[0], c[1], c_ctx)):
        cnd[:, :, r] = v.reshape(16, 128).T
    maps = []
    for i in cores:
        cs = slice(1536 * i, 1536 * i + 1536)
        maps.append({"CND": cnd, "AW": np.ascontiguousarray(ada_w[:, :, cs]),
                     "AB": np.ascontiguousarray(ada_b[:, cs].reshape(2, 12, 128).transpose(0, 2, 1))})
    res = run_bass_kernel_spmd(_prog("mod", build_mod), maps, core_ids=cores).results
    modfull = np.zeros((2, 4, 12288), f32)
    for i in cores:
        m = res[i]["MOD"]
        modfull[:, :, 1536 * i:1536 * i + 1536] = m.transpose(0, 3, 2, 1).reshape(2, 4, 1536)
    def modv(l, r):
        return np.ascontiguousarray(modfull[l, r].reshape(6, NB, 128).transpose(2, 1, 0))
    if _dbg is not None:
        _dbg["mod"] = modfull
    maps = []
    jidx = (np.arange(4, dtype=f32)[None, :] * 128 + np.arange(128, dtype=f32)[:, None]).astype(f32)
    for i in cores:
        b, q = divmod(i, 4)
        t = np.arange(1024 * q, 1024 * q + 1024)
        mvb = modv(0, b); mvc = modv(0, 2)
        mv = np.stack([mvb[:, :, 0], mvb[:, :, 1], mvc[:, :, 0], mvc[:, :, 1]], 2)
        maps.append({"XT": _fm(x[b, 1024 * q:1024 * q + 1024]), "CT": _fm(ctx[b, 64 * q:64 * q + 64]),
                     "MV": np.ascontiguousarray(mv), "G0": _pervec(norm_g[0, 0]),
                     "RIDX": np.broadcast_to((t // 64).astype(f32)[None], (128, 1024)).copy(),
                     "CIDX": np.broadcast_to((t % 64).astype(f32)[None], (128, 1024)).copy(), "JIDX": jidx})
    res = run_bass_kernel_spmd(_prog("prep", build_prep), maps, core_ids=cores).results
    xp = np.zeros((2, L, D), f32); u = np.zeros((2, L, D), ml_dtypes.bfloat16); uc = np.zeros((2, NCTX, D), ml_dtypes.bfloat16)
    for i in cores:
        b, q = divmod(i, 4)
        xp[b, 1024 * q:1024 * q + 1024] = _unfm(res[i]["XPT"])
        u[b, 1024 * q:1024 * q + 1024] = _unfm(res[i]["UT"])
        uc[b, 64 * q:64 * q + 64] = _unfm(res[i]["UCT"])
    if _dbg is not None:
        _dbg["xp"] = xp; _dbg["u"] = u; _dbg["uc"] = uc
    res = run_bass_kernel_spmd(_prog("s5", build_s5), s5_in_maps(u, uc, inp), core_ids=cores).results
    ys5 = s5_gather(res)
    if _dbg is not None:
        _dbg["ys5"] = ys5
    xcur = xp
    for l in range(2):
        h = 1 if l == 0 else 9
        nt = 1024 + 2 * h
        cv = np.zeros((128, 2 * NJ, 4), f32)
        for hh in range(2):
            for j in range(NJ):
                n0 = DFF * hh + 128 * j
                cv[:, NJ * hh + j, 0:3] = ffn_conv[l][:, n0:n0 + 128].T
                cv[:, NJ * hh + j, 3] = ffn_conv_b[l][n0:n0 + 128]
        ngl = np.ascontiguousarray(np.asarray(norm_g[l], f32).reshape(4, NB, 128).transpose(2, 1, 0))
        maps = []
        for i in cores:
            b, q = divmod(i, 4)
            tg = np.arange(1024 * q - h, 1024 * q + 1024 + h)
            valid = ((tg >= 0) & (tg < L)).astype(f32)
            m = {"XIN": _fm(_halo(xcur[b], q, h)), "MV": modv(l, b), "NG": ngl,
                 "VM": np.broadcast_to(valid[None], (128, nt)).copy(),
                 "UP": np.asarray(ffn_up[l], f32), "DOWN": np.asarray(ffn_down[l], f32), "CONV": cv}
            if l == 0:
                m["YT"] = _fm(_halo(ys5[b], q, h)); m["GLUW"] = np.asarray(s5_glu_w[0], f32)
            else:
                m["POOLW"] = np.asarray(pool_w[0], f32); m["PSC"] = _pervec(np.asarray(pool_scale[0], f32))
                invc = np.ones((4, nt), f32)
                for mi, w in enumerate((2, 4, 8, 16)):
                    lo = np.clip(tg - w // 2, 0, L - 1); hi = np.clip(tg + w // 2 - 1, 0, L - 1)
                    invc[mi] = 1.0 / np.maximum(hi - lo + 1, 1)
                m["INVC"] = np.broadcast_to(invc[None], (128, 4, nt)).copy()
            maps.append(m)
        res = run_bass_kernel_spmd(_prog("layer", build_layer, l), maps, core_ids=cores).results
        xn = np.zeros((2, L, D), f32)
        for i in cores:
            b, q = divmod(i, 4)
            xn[b, 1024 * q:1024 * q + 1024] = _unfm(res[i]["XOT"])
        xcur = xn
        if _dbg is not None:
            _dbg[f"xout{l}"] = xn
    return xcur
```

```python
import math
from contextlib import ExitStack
import numpy as np
import ml_dtypes
import concourse.bass as bass
import concourse.mybir as mybir
from concourse.bass_utils import run_bass_kernel_spmd

F32 = mybir.dt.float32
BF16 = mybir.dt.bfloat16
I32 = mybir.dt.int32
AF = mybir.ActivationFunctionType
ALU = mybir.AluOpType

D = 2048
NB = 16
L = 4096
NCTX = 256
DFF = 5632
NJ = 44
MAGIC = 12582912.0
TWO_PI = 2.0 * math.pi
EPS = 1e-6
EPOCH = 3000
NDMA_CH = 12


class KB:
    ENGS = ("tensor", "vector", "scalar", "gpsimd", "sync")

    def __init__(self, nc):
        self.nc = nc
        self.prog = {e: [] for e in self.ENGS}
        self.cnt = {e: 0 for e in self.ENGS}
        self.sems = {e: [] for e in self.ENGS}
        self.waited = {e: {} for e in self.ENGS}
        self.last_w = {}
        self.readers = {}
        self.dma_ch = []
        self.dma_rr = 0
        self._stack = []
        for i in range(NDMA_CH):
            self.dma_ch.append({"sem": self._sem(f"s_dma_{i}"), "n": 0})

    def _sem(self, name):
        cm = self.nc.semaphore(name)
        s = cm.__enter__()
        self._stack.append(cm)
        return s

    def _eng_sem(self, e, epoch):
        while len(self.sems[e]) <= epoch:
            self.sems[e].append(self._sem(f"s_{e}_{len(self.sems[e])}"))
        return self.sems[e][epoch]

    def _deps(self, reads, writes):
        deps = []
        for b in reads:
            t = self.last_w.get(b)
            if t is not None:
                deps.append(t)
        for b in writes:
            t = self.last_w.get(b)
            if t is not None:
                deps.append(t)
            deps.extend(self.readers.get(b, []))
        return deps

    def _record(self, tok, reads, writes):
        for b in reads:
            self.readers.setdefault(b, []).append(tok)
        for b in writes:
            self.last_w[b] = tok
            self.readers[b] = []

    def _emit_waits(self, e, deps):
        need = {}
        for t in deps:
            if t[0] == "eng":
                key = ("eng", t[1], t[2]); v = t[3]
            else:
                key = ("dma", t[1]); v = t[2]
            if need.get(key, 0) < v:
                need[key] = v
        for key, v in need.items():
            if self.waited[e].get(key, 0) >= v:
                continue
            self.waited[e][key] = v
            sem = self._eng_sem(key[1], key[2]) if key[0] == "eng" else self.dma_ch[key[1]]["sem"]
            self.prog[e].append(lambda eng, sem=sem, v=v: eng.wait_ge(sem, v))

    def op(self, e, fn, reads=(), writes=()):
        self._emit_waits(e, self._deps(reads, writes))
        n = self.cnt[e]
        epoch, val = divmod(n, EPOCH)
        sem = self._eng_sem(e, epoch)
        self.cnt[e] = n + 1
        self.prog[e].append(lambda eng, fn=fn, sem=sem: fn(eng).then_inc(sem, 1))
        tok = ("eng", e, epoch, val + 1)
        self._record(tok, reads, writes)
        return tok

    def ops(self, e, fns, reads=(), writes=()):
        self._emit_waits(e, self._deps(reads, writes))
        for fn in fns[:-1]:
            self.prog[e].append(lambda eng, fn=fn: fn(eng))
        n = self.cnt[e]
        epoch, val = divmod(n, EPOCH)
        sem = self._eng_sem(e, epoch)
        self.cnt[e] = n + 1
        self.prog[e].append(lambda eng, fn=fns[-1], sem=sem: fn(eng).then_inc(sem, 1))
        tok = ("eng", e, epoch, val + 1)
        self._record(tok, reads, writes)
        return tok

    def dma(self, q, out, in_, reads=(), writes=(), **kw):
        deps = self._deps(reads, writes)
        ch = self.dma_rr
        self.dma_rr = (self.dma_rr + 1) % len(self.dma_ch)
        c = self.dma_ch[ch]
        if c["n"] > 0:
            deps.append(("dma", ch, 16 * c["n"]))
        self._emit_waits(q, deps)
        c["n"] += 1
        sem = c["sem"]
        self.prog[q].append(
            lambda eng, out=out, in_=in_, sem=sem, kw=kw: eng.dma_start(out=out, in_=in_, **kw).then_inc(sem, 16))
        tok = ("dma", ch, 16 * c["n"])
        self._record(tok, reads, writes)
        return tok

    def barrier(self):
        toks = []
        for e in self.ENGS:
            n = self.cnt[e]
            if n > 0:
                epoch, val = divmod(n - 1, EPOCH)
                toks.append(("eng", e, epoch, val + 1))
        for ch, c in enumerate(self.dma_ch):
            if c["n"] > 0:
                toks.append(("dma", ch, 16 * c["n"]))
        for e in self.ENGS:
            self._emit_waits(e, toks)
        self.last_w = {}
        self.readers = {}

    def finish(self):
        toks = []
        for e in self.ENGS:
            n = self.cnt[e]
            if n > 0:
                epoch, val = divmod(n - 1, EPOCH)
                toks.append(("eng", e, epoch, val + 1))
        for ch, c in enumerate(self.dma_ch):
            if c["n"] > 0:
                toks.append(("dma", ch, 16 * c["n"]))
        self._emit_waits("sync", toks)

    def run(self):
        with self.nc.Block() as block:
            for e in self.ENGS:
                prog = self.prog[e]
                if not prog:
                    continue

                def body(eng, prog=prog):
                    for f in prog:
                        f(eng)
                getattr(block, e)(body)

    def close(self):
        while self._stack:
            self._stack.pop().__exit__(None, None, None)


class Ctx:
    def __init__(self):
        self.nc = bass.Bass("TRN2", target_bir_lowering=False)
        self.es = ExitStack()
        self.k = KB(self.nc)
        self.uid = 0

    def din(self, name, shape, dt=F32):
        return self.nc.dram_tensor(name, list(shape), dt, kind="ExternalInput").ap()

    def dout(self, name, shape, dt=F32):
        return self.nc.dram_tensor(name, list(shape), dt, kind="ExternalOutput").ap()

    def sb(self, name, shape, dt=F32, es=None):
        return (es or self.es).enter_context(self.nc.sbuf_tensor(name, list(shape), dt))

    def ps(self, name, shape, dt=F32, es=None):
        return (es or self.es).enter_context(self.nc.psum_tensor(name, list(shape), dt))

    def done(self):
        self.k.finish()
        self.k.run()
        self.k.close()
        self.es.close()
        return self.nc

    def mm(self, out, lhsT, rhs, start, stop, reads, writes):
        return self.k.op("tensor", lambda e: e.matmul(out, lhsT=lhsT, rhs=rhs, start=start, stop=stop), reads, writes)

    def mmg(self, out, pairs, reads, writes):
        n = len(pairs)
        fns = []
        for i, (lh, rh) in enumerate(pairs):
            fns.append(lambda e, lh=lh, rh=rh, i=i: e.matmul(out, lhsT=lh, rhs=rh, start=(i == 0), stop=(i == n - 1)))
        return self.k.ops("tensor", fns, reads, writes)

    def tt(self, eng, out, in0, in1, op, reads, writes):
        return self.k.op(eng, lambda e: e.tensor_tensor(out=out, in0=in0, in1=in1, op=op), reads, writes)

    def ts(self, eng, out, in0, s1, s2, op0, op1, reads, writes):
        if op1 is None:
            return self.k.op(eng, lambda e: e.tensor_scalar(out=out, in0=in0, scalar1=s1, scalar2=None, op0=op0), reads, writes)
        return self.k.op(eng, lambda e: e.tensor_scalar(out=out, in0=in0, scalar1=s1, scalar2=s2, op0=op0, op1=op1), reads, writes)

    def stt(self, out, in0, scalar, in1, op0, op1, reads, writes):
        return self.k.op("vector", lambda e: e.scalar_tensor_tensor(out=out, in0=in0, scalar=scalar, in1=in1, op0=op0, op1=op1), reads, writes)

    def act(self, out, in_, func, reads, writes, scale=None, bias=None):
        kw = {}
        if scale is not None:
            kw["scale"] = scale
        if bias is not None:
            kw["bias"] = bias
        return self.k.op("scalar", lambda e: e.activation(out=out, in_=in_, func=func, **kw), reads, writes)

    def cp(self, eng, out, in_, reads, writes):
        if eng == "scalar":
            return self.k.op(eng, lambda e: e.copy(out=out, in_=in_), reads, writes)
        return self.k.op(eng, lambda e: e.tensor_copy(out=out, in_=in_), reads, writes)

    def memset(self, eng, out, val, writes):
        return self.k.op(eng, lambda e: e.memset(out, val), (), writes)

    def recip(self, out, in_, reads, writes):
        return self.k.op("vector", lambda e: e.reciprocal(out=out, in_=in_), reads, writes)

    def dma(self, q, out, in_, reads=(), writes=()):
        return self.k.dma(q, out, in_, reads, writes)


def ap_of(t, off, dims):
    a = t[:]
    return bass.AP(a.tensor, a.offset + off, [list(a.ap[0])] + [list(d) for d in dims])


NK = 548
GL = 16


def build_s5(debug=False):
    c = Ctx(); k = c.k; nc = c.nc
    dbg = []
    XF = c.din("XF", [2, GL, 128, NK], BF16)
    XB = c.din("XB", [2, GL, 128, NK], BF16)
    LAM = c.din("LAM", [128, 2, 2, GL])
    LST = c.din("LST", [128, 2, GL])
    B1 = c.din("B1", [128, 2, GL, 16]); B2 = c.din("B2", [128, 2, GL, 16])
    C1 = c.din("C1", [128, 2, GL, 16]); C2 = c.din("C2", [128, 2, GL, 16])
    DCOL = c.din("DCOL", [128, GL])
    CONS = c.din("CONS", [128, 4, 128])
    SGN = c.din("SGN", [128, 2])
    JFAC = c.din("JFAC", [128, 16, GL])
    IOTA = c.din("IOTA", [128, NK])
    YC = c.dout("YC", [2, GL, 128, 512])

    sb = c.sb
    lam = sb("lam", [128, 2, 2, GL]); lst = sb("lst", [128, 2, GL])
    b1 = sb("b1", [128, 2, GL, 16]); b2 = sb("b2", [128, 2, GL, 16])
    c1 = sb("c1", [128, 2, GL, 16]); c2 = sb("c2", [128, 2, GL, 16])
    dcol = sb("dcol", [128, GL]); cons = sb("cons", [128, 4, 128]); consb = sb("consb", [128, 2, 128], BF16)
    sgn = sb("sgn", [128, 2]); jfac = sb("jfac", [128, 16, GL]); iota = sb("iota", [128, NK])
    for dst, src, nm in ((lam, LAM, "lam"), (lst, LST, "lst"), (b1, B1, "b1"), (b2, B2, "b2"), (c1, C1, "c1"),
                         (c2, C2, "c2"), (dcol, DCOL, "dcol"), (cons, CONS, "cons"), (sgn, SGN, "sgn"),
                         (jfac, JFAC, "jfac"), (iota, IOTA, "iota")):
        k.dma("sync", dst[:], src[:], writes=[nm])
    k.op("vector", lambda e: e.tensor_copy(out=consb[:], in_=cons[:, 0:2, :]), reads=["cons"], writes=["consb"])
    identb = consb[:, 0, :]; pswapb = consb[:, 1, :]

    V = "vector"
    dt = sb("dt", [128, 2, GL]); ee = sb("ee", [128, 2, GL]); th = sb("th", [128, 2, GL])
    k.op("scalar", lambda e: e.activation(out=dt[:], in_=lst[:], func=AF.Exp), reads=["lst"], writes=["dt"])
    k.op(V, lambda e: e.tensor_tensor(out=ee[:], in0=lam[:, 0], in1=dt[:], op=ALU.mult), reads=["lam", "dt"], writes=["ee"])
    k.op(V, lambda e: e.tensor_tensor(out=th[:], in0=lam[:, 1], in1=dt[:], op=ALU.mult), reads=["lam", "dt"], writes=["th"])
    shp = [128, 2, 16, GL]
    ej = sb("ej", shp); yj = sb("yj", shp); tmp = sb("tmp", shp); tmp2 = sb("tmp2", shp)
    emag = sb("emag", shp); sn = sb("sn", shp); cs = sb("cs", shp); pr = sb("pr", shp); pi = sb("pi", shp)

    def bc_dg(t):
        return ap_of(t, 0, [[GL, 2], [0, 16], [1, GL]])

    def bc_j(t):
        return ap_of(t, 0, [[0, 2], [GL, 16], [1, GL]])
    k.op(V, lambda e: e.tensor_tensor(out=ej[:], in0=bc_dg(ee), in1=bc_j(jfac), op=ALU.mult), reads=["ee", "jfac"], writes=["ej"])
    k.op("scalar", lambda e: e.activation(out=emag[:], in_=ej[:], func=AF.Exp), reads=["ej"], writes=["emag"])
    k.op(V, lambda e: e.tensor_tensor(out=yj[:], in0=bc_dg(th), in1=bc_j(jfac), op=ALU.mult), reads=["th", "jfac"], writes=["yj"])

    def sin_of(dst, src, shift, nm_dst, nm_src):
        k.op(V, lambda e: e.tensor_scalar(out=tmp[:], in0=src[:], scalar1=1.0 / TWO_PI, scalar2=shift, op0=ALU.mult, op1=ALU.add),
             reads=[nm_src], writes=["tmp"])
        k.op(V, lambda e: e.tensor_scalar(out=tmp2[:], in0=tmp[:], scalar1=MAGIC, scalar2=MAGIC, op0=ALU.add, op1=ALU.subtract),
             reads=["tmp"], writes=["tmp2"])
        k.op(V, lambda e: e.tensor_tensor(out=tmp[:], in0=tmp[:], in1=tmp2[:], op=ALU.subtract), reads=["tmp", "tmp2"], writes=["tmp"])
        k.op("scalar", lambda e: e.activation(out=dst[:], in_=tmp[:], func=AF.Sin, scale=TWO_PI), reads=["tmp"], writes=[nm_dst])
    sin_of(sn, yj, 0.0, "sn", "yj")
    sin_of(cs, yj, 0.25, "cs", "yj")
    k.op(V, lambda e: e.tensor_tensor(out=pr[:], in0=emag[:], in1=cs[:], op=ALU.mult), reads=["emag", "cs"], writes=["pr"])
    k.op(V, lambda e: e.tensor_tensor(out=pi[:], in0=emag[:], in1=sn[:], op=ALU.mult), reads=["emag", "sn"], writes=["pi"])
    s3 = [128, 2, GL]
    nr = sb("nr", s3); den = sb("den", s3); t3 = sb("t3", s3); t4 = sb("t4", s3); fr = sb("fr", s3); fi = sb("fi", s3)
    ar = pr[:, :, 8, :]; ai = pi[:, :, 8, :]
    lr = lam[:, 0]; li = lam[:, 1]
    k.op(V, lambda e: e.tensor_scalar(out=nr[:], in0=ar, scalar1=-1.0, scalar2=None, op0=ALU.add), reads=["pr"], writes=["nr"])
    k.op(V, lambda e: e.tensor_tensor(out=den[:], in0=lr, in1=lr, op=ALU.mult), reads=["lam"], writes=["den"])
    k.op(V, lambda e: e.tensor_tensor(out=t3[:], in0=li, in1=li, op=ALU.mult), reads=["lam"], writes=["t3"])
    k.op(V, lambda e: e.tensor_tensor(out=den[:], in0=den[:], in1=t3[:], op=ALU.add), reads=["den", "t3"], writes=["den"])
    k.op(V, lambda e: e.reciprocal(out=den[:], in_=den[:]), reads=["den"], writes=["den"])
    k.op(V, lambda e: e.tensor_tensor(out=t3[:], in0=nr[:], in1=lr, op=ALU.mult), reads=["nr", "lam"], writes=["t3"])
    k.op(V, lambda e: e.tensor_tensor(out=t4[:], in0=ai, in1=li, op=ALU.mult), reads=["pi", "lam"], writes=["t4"])
    k.op(V, lambda e: e.tensor_tensor(out=t3[:], in0=t3[:], in1=t4[:], op=ALU.add), reads=["t3", "t4"], writes=["t3"])
    k.op(V, lambda e: e.tensor_tensor(out=fr[:], in0=t3[:], in1=den[:], op=ALU.mult), reads=["t3", "den"], writes=["fr"])
    k.op(V, lambda e: e.tensor_tensor(out=t3[:], in0=ai, in1=lr, op=ALU.mult), reads=["pi", "lam"], writes=["t3"])
    k.op(V, lambda e: e.tensor_tensor(out=t4[:], in0=nr[:], in1=li, op=ALU.mult), reads=["nr", "lam"], writes=["t4"])
    k.op(V, lambda e: e.tensor_tensor(out=t3[:], in0=t3[:], in1=t4[:], op=ALU.subtract), reads=["t3", "t4"], writes=["t3"])
    k.op(V, lambda e: e.tensor_tensor(out=fi[:], in0=t3[:], in1=den[:], op=ALU.mult), reads=["t3", "den"], writes=["fi"])
    qa = sb("qa", shp); qb = sb("qb", shp); t1t = sb("t1t", shp); t2t = sb("t2t", shp)
    k.op(V, lambda e: e.tensor_tensor(out=tmp[:], in0=pr[:], in1=bc_dg(fr), op=ALU.mult), reads=["pr", "fr"], writes=["tmp"])
    k.op(V, lambda e: e.tensor_tensor(out=tmp2[:], in0=pi[:], in1=bc_dg(fi), op=ALU.mult), reads=["pi", "fi"], writes=["tmp2"])
    k.op(V, lambda e: e.tensor_tensor(out=qa[:], in0=tmp[:], in1=tmp2[:], op=ALU.subtract), reads=["tmp", "tmp2"], writes=["qa"])
    k.op(V, lambda e: e.tensor_tensor(out=tmp[:], in0=pr[:], in1=bc_dg(fi), op=ALU.mult), reads=["pr", "fi"], writes=["tmp"])
    k.op(V, lambda e: e.tensor_tensor(out=tmp2[:], in0=pi[:], in1=bc_dg(fr), op=ALU.mult), reads=["pi", "fr"], writes=["tmp2"])
    k.op(V, lambda e: e.tensor_tensor(out=tmp[:], in0=tmp[:], in1=tmp2[:], op=ALU.add), reads=["tmp", "tmp2"], writes=["tmp"])
    k.op(V, lambda e: e.tensor_scalar(out=qb[:], in0=tmp[:], scalar1=sgn[:, 1:2], scalar2=None, op0=ALU.mult), reads=["tmp", "sgn"], writes=["qb"])
    k.op(V, lambda e: e.tensor_scalar(out=t1t[:], in0=pr[:], scalar1=sgn[:, 0:1], scalar2=None, op0=ALU.mult), reads=["pr", "sgn"], writes=["t1t"])
    k.op(V, lambda e: e.tensor_scalar(out=t2t[:], in0=pi[:], scalar1=-1.0, scalar2=None, op0=ALU.mult), reads=["pi"], writes=["t2t"])
    ph = sb("ph", s3); phs = sb("phs", s3)
    k.op(V, lambda e: e.tensor_scalar(out=ph[:], in0=th[:], scalar1=8.0 / TWO_PI, scalar2=None, op0=ALU.mult), reads=["th"], writes=["ph"])
    k.op(V, lambda e: e.tensor_scalar(out=phs[:], in0=ph[:], scalar1=sgn[:, 0:1], scalar2=None, op0=ALU.mult), reads=["ph", "sgn"], writes=["phs"])
    rho = emag

    msh = [128, GL, 8, 16]
    mt1 = sb("mt1", msh); mt2 = sb("mt2", msh)
    Lm = [sb(f"Lm{d}", msh, BF16) for d in range(2)]
    Rm = [sb(f"Rm{d}", msh, BF16) for d in range(2)]
    Wo = [sb(f"Wo{d}", msh, BF16) for d in range(2)]
    Wos = [sb(f"Wos{d}", [128, GL, 128], BF16) for d in range(2)]
    Wi = [sb(f"Wi{d}", [128, GL, 128], BF16) for d in range(2)]
    Wis = [sb(f"Wis{d}", [128, GL, 128], BF16) for d in range(2)]
    Mg = sb("Mg", [128, GL, 128], BF16)

    def tab_ap(t, d, jj0, jstep):
        return ap_of(t, d * 16 * GL + jj0 * GL, [[1, GL], [jstep * GL, 8], [0, 16]])

    def par_ap(t, d):
        return ap_of(t, d * GL * 16, [[16, GL], [0, 8], [1, 16]])

    def gen(dst, nm, ta, tb, na, nb_, pa, pb, npa, npb, d, jj0, jstep):
        k.op(V, lambda e: e.tensor_tensor(out=mt1[:], in0=tab_ap(ta, d, jj0, jstep), in1=par_ap(pa, d), op=ALU.mult),
             reads=[na, npa], writes=["mt1"])
        k.op(V, lambda e: e.tensor_tensor(out=mt2[:], in0=tab_ap(tb, d, jj0, jstep), in1=par_ap(pb, d), op=ALU.mult),
             reads=[nb_, npb], writes=["mt2"])
        k.op(V, lambda e: e.tensor_tensor(out=dst[:], in0=mt1[:], in1=mt2[:], op=ALU.add), reads=["mt1", "mt2"], writes=[nm])
    gen(Lm[0], "Lm0", qa, qb, "qa", "qb", b1, b2, "b1", "b2", 0, 14, -1)
    gen(Rm[0], "Rm0", t1t, t2t, "t1t", "t2t", c1, c2, "c1", "c2", 0, 0, 1)
    gen(Wo[0], "Wo0", t1t, t2t, "t1t", "t2t", c1, c2, "c1", "c2", 0, 8, 1)
    gen(Lm[1], "Lm1", qa, qb, "qa", "qb", b1, b2, "b1", "b2", 1, 7, 1)
    gen(Rm[1], "Rm1", t1t, t2t, "t1t", "t2t", c1, c2, "c1", "c2", 1, 7, -1)
    gen(Wo[1], "Wo1", t1t, t2t, "t1t", "t2t", c1, c2, "c1", "c2", 1, 15, -1)

    pg = [c.ps(f"pg{i}", [128, 512]) for i in range(2)]
    gi = 0
    mtmp = sb("mtmp", [128, 128]); mtmp2 = sb("mtmp2", [128, 128])
    for g in range(GL):
        for d in range(2):
            Lg = Lm[d][:, g].rearrange("p s c -> p (s c)")
            Wog = Wo[d][:, g].rearrange("p s c -> p (s c)")
            p = pg[gi % 2]; pn = f"pg{gi % 2}"; gi += 1
            k.op("tensor", lambda e, p=p, Lg=Lg: e.matmul(p[:, 0:128], lhsT=Lg, rhs=identb, start=True, stop=True),
                 reads=[f"Lm{d}", "consb"], writes=[pn])
            k.op("tensor", lambda e, p=p, Lg=Lg: e.matmul(p[:, 128:256], lhsT=Lg, rhs=pswapb, start=True, stop=True),
                 reads=[f"Lm{d}", "consb"], writes=[pn])
            k.op("tensor", lambda e, p=p, Wog=Wog: e.matmul(p[:, 256:384], lhsT=pswapb, rhs=Wog, start=True, stop=True),
                 reads=[f"Wo{d}", "consb"], writes=[pn])
            k.op("scalar", lambda e, p=p, d=d, g=g: e.copy(out=Wi[d][:, g, :], in_=p[:, 0:128]), reads=[pn], writes=[f"Wi{d}"])
            k.op("scalar", lambda e, p=p, d=d, g=g: e.copy(out=Wis[d][:, g, :], in_=p[:, 128:256]), reads=[pn], writes=[f"Wis{d}"])
            k.op("scalar", lambda e, p=p, d=d, g=g: e.copy(out=Wos[d][:, g, :], in_=p[:, 256:384]), reads=[pn], writes=[f"Wos{d}"])
        p = pg[gi % 2]; pn = f"pg{gi % 2}"; gi += 1
        for d in range(2):
            Lg = Lm[d][:, g].rearrange("p s c -> p (s c)")
            Rg = Rm[d][:, g].rearrange("p s c -> p (s c)")
            k.op("tensor", lambda e, p=p, Lg=Lg, Rg=Rg, d=d: e.matmul(p[:, 128 * d:128 * d + 128], lhsT=Lg, rhs=Rg, start=True, stop=True),
                 reads=[f"Lm{d}", f"Rm{d}"], writes=[pn])
        k.op(V, lambda e, p=p: e.tensor_tensor(out=mtmp[:], in0=p[:, 0:128], in1=cons[:, 2, :], op=ALU.mult), reads=[pn, "cons"], writes=["mtmp"])
        k.op(V, lambda e, p=p: e.tensor_tensor(out=mtmp2[:], in0=p[:, 128:256], in1=cons[:, 3, :], op=ALU.mult), reads=[pn, "cons"], writes=["mtmp2"])
        k.op(V, lambda e: e.tensor_tensor(out=mtmp[:], in0=mtmp[:], in1=mtmp2[:], op=ALU.add), reads=["mtmp", "mtmp2"], writes=["mtmp"])
        k.op(V, lambda e, g=g: e.scalar_tensor_tensor(out=Mg[:, g, :], in0=cons[:, 0, :], scalar=dcol[:, g:g + 1], in1=mtmp[:],
                                                      op0=ALU.mult, op1=ALU.add), reads=["cons", "dcol", "mtmp"], writes=["Mg"])

    pS = c.ps("pS", [128, 1024]); pSw = c.ps("pSw", [128, 1024]); pY = c.ps("pY", [128, 512])
    xf = [sb(f"xf{i}", [128, NK], BF16) for i in range(2)]
    xb = [sb(f"xb{i}", [128, NK], BF16) for i in range(2)]
    ctabs = [sb(f"ctab{i}", [128, NK]) for i in range(2)]; stabs = [sb(f"stab{i}", [128, NK]) for i in range(2)]; ty = sb("ty", [128, NK]); tr = sb("tr", [128, NK])
    sp = sb("sp", [128, NK]); sp2 = sb("sp2", [128, NK]); ggs = [sb(f"gg{i}", [128, NK]) for i in range(2)]
    GC = [[sb(f"GC{b}{d}", [128, NK], BF16) for d in range(2)] for b in range(2)]
    GS = [[sb(f"GS{b}{d}", [128, NK], BF16) for d in range(2)] for b in range(2)]
    yo = [sb(f"yo{i}", [128, 512]) for i in range(2)]
    for g in range(GL):
        for b in range(2):
            k.dma("sync", xf[b][:], XF[b, g], writes=[f"xf{b}"])
            k.dma("sync", xb[b][:], XB[b, g], writes=[f"xb{b}"])
        for d in range(2):
            ctab = ctabs[d]; stab = stabs[d]; cn = f"ctab{d}"; sn_ = f"stab{d}"
            def table(dst, nm, phcol, shift):
                k.op(V, lambda e: e.tensor_scalar(out=ty[:], in0=iota[:], scalar1=phcol, scalar2=shift, op0=ALU.mult, op1=ALU.add),
                     reads=["iota", "ph", "phs"], writes=["ty"])
                k.op(V, lambda e: e.tensor_scalar(out=tr[:], in0=ty[:], scalar1=MAGIC, scalar2=MAGIC, op0=ALU.add, op1=ALU.subtract),
                     reads=["ty"], writes=["tr"])
                k.op(V, lambda e: e.tensor_tensor(out=ty[:], in0=ty[:], in1=tr[:], op=ALU.subtract), reads=["ty", "tr"], writes=["ty"])
                k.op("scalar", lambda e: e.activation(out=dst[:], in_=ty[:], func=AF.Sin, scale=TWO_PI), reads=["ty"], writes=[nm])
            table(stab, sn_, phs[:, d, g:g + 1], 0.0)
            table(ctab, cn, ph[:, d, g:g + 1], 0.25)
            rho_bc = ap_of(emag, d * 16 * GL + 15 * GL + g, [[0, NK]])
            for b in range(2):
                gg = ggs[b]; gn = f"gg{b}"
                X = xf[b] if d == 0 else xb[b]
                xn = f"xf{b}" if d == 0 else f"xb{b}"
                for (P, pn, W, wn) in ((pS, "pS", Wi[d], f"Wi{d}"), (pSw, "pSw", Wis[d], f"Wis{d}")):
                    k.op("tensor", lambda e, P=P, W=W, X=X, g=g: e.matmul(P[:, 0:512], lhsT=W[:, g, :], rhs=X[:, 0:512], start=True, stop=True),
                         reads=[wn, xn], writes=[pn])
                    k.op("tensor", lambda e, P=P, W=W, X=X, g=g: e.matmul(P[:, 512:NK], lhsT=W[:, g, :], rhs=X[:, 512:NK], start=True, stop=True),
                         reads=[wn, xn], writes=[pn])
                k.op(V, lambda e, ctab=ctab: e.tensor_tensor(out=sp[:], in0=pS[:, 0:NK], in1=ctab[:], op=ALU.mult), reads=["pS", cn], writes=["sp"])
                k.op(V, lambda e, stab=stab: e.tensor_tensor(out=sp2[:], in0=pSw[:, 0:NK], in1=stab[:], op=ALU.mult), reads=["pSw", sn_], writes=["sp2"])
                k.op(V, lambda e: e.tensor_tensor(out=sp[:], in0=sp[:], in1=sp2[:], op=ALU.add), reads=["sp", "sp2"], writes=["sp"])
                k.op(V, lambda e, rho_bc=rho_bc, gg=gg: e.tensor_tensor_scan(out=gg[:], data0=rho_bc, data1=sp[:], initial=0.0, op0=ALU.mult, op1=ALU.add),
                     reads=["emag", "sp"], writes=[gn])
                k.op(V, lambda e, b=b, d=d, gg=gg, ctab=ctab: e.tensor_tensor(out=GC[b][d][:], in0=gg[:], in1=ctab[:], op=ALU.mult), reads=[gn, cn], writes=[f"GC{b}{d}"])
                k.op("gpsimd", lambda e, b=b, d=d, gg=gg, stab=stab: e.tensor_tensor(out=GS[b][d][:], in0=gg[:], in1=stab[:], op=ALU.mult), reads=[gn, sn_], writes=[f"GS{b}{d}"])
        for b in range(2):
            def rev(t):
                return ap_of(t, 542, [[-1, 512]])
            mm = [(Mg[:, g, :], xf[b][:, 32:544], ["Mg", f"xf{b}"]),
                  (Wo[0][:, g].rearrange("p s c -> p (s c)"), GC[b][0][:, 31:543], ["Wo0", f"GC{b}0"]),
                  (Wos[0][:, g, :], GS[b][0][:, 31:543], ["Wos0", f"GS{b}0"]),
                  (Wo[1][:, g].rearrange("p s c -> p (s c)"), rev(GC[b][1]), ["Wo1", f"GC{b}1"]),
                  (Wos[1][:, g, :], rev(GS[b][1]), ["Wos1", f"GS{b}1"])]
            for i, (lh, rh, rd) in enumerate(mm):
                k.op("tensor", lambda e, lh=lh, rh=rh, i=i: e.matmul(pY[:], lhsT=lh, rhs=rh, start=(i == 0), stop=(i == 4)),
                     reads=rd, writes=["pY"])
            y = yo[b]
            k.op("scalar", lambda e, y=y: e.copy(out=y[:], in_=pY[:]), reads=["pY"], writes=[f"yo{b}"])
            k.dma("sync", YC[b, g], y[:], reads=[f"yo{b}"], writes=[f"YC{b}{g}"])
    if debug:
        for nm, t, shp, dt_ in (("pr", pr, shp, F32), ("pi", pi, shp, F32), ("qa", qa, shp, F32), ("qb", qb, shp, F32),
                                ("fr", fr, s3, F32), ("fi", fi, s3, F32), ("Lm0", Lm[0], msh, BF16), ("Rm0", Rm[0], msh, BF16),
                                ("Wo0", Wo[0], msh, BF16), ("Mg", Mg, [128, GL, 128], BF16), ("Wi0", Wi[0], [128, GL, 128], BF16),
                                ("Wis0", Wis[0], [128, GL, 128], BF16), ("Wos0", Wos[0], [128, GL, 128], BF16),
                                ("ctab1", ctabs[1], [128, NK], F32), ("stab1", stabs[1], [128, NK], F32), ("gg1", ggs[1], [128, NK], F32),
                                ("sp", sp, [128, NK], F32), ("GC00", GC[0][0], [128, NK], BF16)):
            o = c.dout("dbg_" + nm, shp, dt_)
            k.dma("sync", o[:], t[:], reads=[nm], writes=["dbg_" + nm])
    return c.done()


def s5_consts():
    idx = np.arange(128)
    ident = np.eye(128, dtype=np.float32)
    pswap = np.zeros((128, 128), np.float32); pswap[idx, (idx + 64) % 128] = 1
    s_of = idx // 16
    maskF = (s_of[None, :] >= s_of[:, None]).astype(np.float32)
    maskB = (s_of[None, :] <= s_of[:, None]).astype(np.float32)
    cons = np.stack([ident, pswap, maskF, maskB], 1)
    sg = np.where(idx < 64, 1.0, -1.0).astype(np.float32)
    sgn = np.stack([sg, -sg], 1)
    jfac = np.broadcast_to((np.arange(16, dtype=np.float32) - 7)[None, :, None], (128, 16, GL)).copy()
    iota = np.broadcast_to(np.arange(NK, dtype=np.float32)[None], (128, NK)).copy()
    return cons, sgn, jfac, iota


def s5_in_maps(u, uc, inp):
    cons, sgn, jfac, iota = s5_consts()
    bf = u.dtype
    XF = np.zeros((2, 128, 128, NK), bf); XB = np.zeros((2, 128, 128, NK), bf)
    for b in range(2):
        seq = np.concatenate([uc[b], u[b]], 0).reshape(544, 8, 128, 16)
        chb = np.concatenate([seq[:32][::-1], seq[32:][::-1]], 0)
        XF[b, :, :, :544] = seq.transpose(2, 1, 3, 0).reshape(128, 128, 544)
        XB[b, :, :, :544] = chb.transpose(2, 1, 3, 0).reshape(128, 128, 544)
    lre = inp["s5_lam_re"][0]; lim = inp["s5_lam_im"][0]
    def rep(a):
        t = a.transpose(2, 0, 1)
        return np.concatenate([t, t], 0)
    LAM = np.stack([rep(lre), rep(lim)], 1)
    LST = np.broadcast_to(inp["s5_log_step"][0][None], (128, 2, 128)).copy()
    bre = inp["s5_b_re"][0].transpose(2, 0, 1, 3); bim = inp["s5_b_im"][0].transpose(2, 0, 1, 3)
    cre = inp["s5_c_re"][0].transpose(3, 0, 1, 2); cim = inp["s5_c_im"][0].transpose(3, 0, 1, 2)
    B1 = np.concatenate([bre, bim], 0); B2 = np.concatenate([bim, bre], 0)
    C1 = np.concatenate([cre, cim], 0); C2 = np.concatenate([cim, cre], 0)
    dd = inp["s5_d"][0].reshape(128, 16)
    DCOL = np.tile(dd.T, (8, 1))
    maps = []
    for core in range(8):
        gs = slice(16 * core, 16 * core + 16)
        maps.append({
            "XF": np.ascontiguousarray(XF[:, gs]), "XB": np.ascontiguousarray(XB[:, gs]),
            "LAM": np.ascontiguousarray(LAM[..., gs]).astype(np.float32), "LST": np.ascontiguousarray(LST[..., gs]).astype(np.float32),
            "B1": np.ascontiguousarray(B1[:, :, gs]), "B2": np.ascontiguousarray(B2[:, :, gs]),
            "C1": np.ascontiguousarray(C1[:, :, gs]), "C2": np.ascontiguousarray(C2[:, :, gs]),
            "DCOL": np.ascontiguousarray(DCOL[:, gs]).astype(np.float32),
            "CONS": cons, "SGN": sgn, "JFAC": jfac, "IOTA": iota})
    return maps


def s5_gather(res):
    y = np.zeros((2, L, D), np.float32)
    for core in range(8):
        yc = res[core]["YC"].reshape(2, 16, 8, 16, 512)
        y[:, :, 256 * core:256 * core + 256] = yc.transpose(0, 4, 2, 1, 3).reshape(2, L, 256)
    return y


def build_mod():
    c = Ctx(); k = c.k
    CND = c.din("CND", [128, 16, 4])
    AW = c.din("AW", [2, 2048, 1536])
    AB = c.din("AB", [2, 128, 12])
    MOD = c.dout("MOD", [2, 128, 12, 4])
    cnd = c.sb("cnd", [128, 16, 4]); ab = c.sb("ab", [2, 128, 12]) if False else None
    abt = [c.sb(f"abt{l}", [128, 12]) for l in range(2)]
    c.dma("sync", cnd[:], CND[:], writes=["cnd"])
    for l in range(2):
        c.dma("sync", abt[l][:], AB[l], writes=[f"abt{l}"])
    cond = c.sb("cond", [128, 16, 4])
    c.act(cond[:], cnd[:], AF.Silu, ["cnd"], ["cond"])
    wb = [c.sb(f"wb{i}", [128, 16, 512]) for i in range(2)]
    pm = [c.ps(f"pm{i}", [128, 512]) for i in range(2)]
    mo = c.sb("mo", [128, 2, 12, 4])
    n = 0
    for l in range(2):
        for q in range(3):
            w = wb[n % 2]; wn = f"wb{n % 2}"; p = pm[n % 2]; pn = f"pm{n % 2}"; n += 1
            c.dma("sync", w[:], AW[l, :, 512 * q:512 * q + 512].rearrange("(kc p) n -> p kc n", p=128), writes=[wn])
            for j in range(4):
                c.mmg(p[:, 4 * j:4 * j + 4], [(w[:, kc, 128 * j:128 * j + 128], cond[:, kc, :]) for kc in range(16)], [wn, "cond"], [pn])
            for j in range(4):
                blk = 4 * q + j
                c.ts("vector", mo[:, l, blk, :], p[:, 4 * j:4 * j + 4], abt[l][:, blk:blk + 1], None, ALU.add, None, [pn, f"abt{l}"], ["mo"])
    c.dma("sync", MOD.rearrange("l p b r -> p l b r"), mo[:], reads=["mo"], writes=["MOD"])
    return c.done()


def col_tiles(nt):
    n = (nt + 511) // 512
    sz = (nt + n - 1) // n
    return [(i * sz, min(nt, (i + 1) * sz)) for i in range(n)]


def rms_rstd(c, X, xname, nt, rstd, rname, ones, sq, psq):
    cts = col_tiles(nt)
    for blk in range(NB):
        s_ = sq[blk % 2]; sn_ = f"sq{blk % 2}"
        c.act(s_[:, 0:nt], X[:, blk, :], AF.Square, [xname], [sn_])
        for i, (c0, c1) in enumerate(cts):
            c.mm(psq[i][:, 0:c1 - c0], ones[:], s_[:, c0:c1], blk == 0, blk == NB - 1, ["ones", sn_], [f"psq{i}"])
    for i, (c0, c1) in enumerate(cts):
        c.ts("vector", rstd[:, c0:c1], psq[i][:, 0:c1 - c0], 1.0 / D, EPS, ALU.mult, ALU.add, [f"psq{i}"], [rname])
    c.act(rstd[:, 0:nt], rstd[:, 0:nt], AF.Sqrt, [rname], [rname])
    c.recip(rstd[:, 0:nt], rstd[:, 0:nt], [rname], [rname])


def build_prep():
    c = Ctx(); k = c.k
    NT = 1024; NC = 64
    XT = c.din("XT", [NB, 128, NT]); CT = c.din("CT", [NB, 128, NC])
    MV = c.din("MV", [128, NB, 4])
    G0 = c.din("G0", [128, NB])
    RIDX = c.din("RIDX", [128, NT]); CIDX = c.din("CIDX", [128, NT]); JIDX = c.din("JIDX", [128, 4])
    XPT = c.dout("XPT", [NB, 128, NT]); UT = c.dout("UT", [NB, 128, NT], BF16); UCT = c.dout("UCT", [NB, 128, NC], BF16)
    xp = c.sb("xp", [128, NB, NT]); xc = c.sb("xc", [128, NB, NC])
    mv = c.sb("mv", [128, NB, 4]); g0 = c.sb("g0", [128, NB]); ridx = c.sb("ridx", [128, NT]); cidx = c.sb("cidx", [128, NT])
    jidx = c.sb("jidx", [128, 4]); om = c.sb("om", [128, 4]); ones = c.sb("ones", [128, 128])
    c.memset("vector", ones[:], 1.0, ["ones"])
    for blk in range(NB):
        c.dma("sync", xp[:, blk, :], XT[blk], writes=[f"xp{blk}"])
    c.dma("sync", xc[:], CT.rearrange("b p t -> p b t"), writes=["xc"])
    for dst, src, nm in ((mv, MV, "mv"), (g0, G0, "g0"), (ridx, RIDX, "ridx"), (cidx, CIDX, "cidx"), (jidx, JIDX, "jidx")):
        c.dma("sync", dst[:], src[:], writes=[nm])
    c.act(om[:], jidx[:], AF.Exp, ["jidx"], ["om"], scale=-math.log(10000.0) / 512.0)
    c.ts("vector", om[:], om[:], 1.0 / TWO_PI, None, ALU.mult, None, ["om"], ["om"])
    ty = [c.sb(f"ty{i}", [128, NT]) for i in range(2)]; tr = [c.sb(f"tr{i}", [128, NT]) for i in range(2)]
    for blk in range(NB):
        idx, inm = (ridx, "ridx") if blk < 8 else (cidx, "cidx")
        shift = 0.25 if (blk // 4) % 2 == 1 else 0.0
        y = ty[blk % 2]; yn = f"ty{blk % 2}"; r = tr[blk % 2]; rn = f"tr{blk % 2}"
        c.ts("vector", y[:], idx[:], om[:, blk % 4:blk % 4 + 1], shift, ALU.mult, ALU.add, [inm, "om"], [yn])
        c.ts("vector", r[:], y[:], MAGIC, MAGIC, ALU.add, ALU.subtract, [yn], [rn])
        c.tt("vector", y[:], y[:], r[:], ALU.subtract, [yn, rn], [yn])
        c.act(r[:], y[:], AF.Sin, [yn], [rn], scale=TWO_PI)
        c.tt("gpsimd", xp[:, blk, :], xp[:, blk, :], r[:], ALU.add, [f"xp{blk}", rn], [f"xp{blk}"])
        c.dma("sync", XPT[blk], xp[:, blk, :], reads=[f"xp{blk}"], writes=[f"XPT{blk}"])
    sq = [c.sb(f"sq{i}", [128, NT]) for i in range(2)]
    psq = [c.ps(f"psq{i}", [128, 512]) for i in range(2)]
    rstd = c.sb("rstd", [128, NT]); rstc = c.sb("rstc", [128, NC])
    allx = [f"xp{b}" for b in range(NB)]
    cts = col_tiles(NT)
    for blk in range(NB):
        s_ = sq[blk % 2]; sn_ = f"sq{blk % 2}"
        c.act(s_[:], xp[:, blk, :], AF.Square, [f"xp{blk}"], [sn_])
        for i, (c0, c1) in enumerate(cts):
            c.mm(psq[i][:, 0:c1 - c0], ones[:], s_[:, c0:c1], blk == 0, blk == NB - 1, ["ones", sn_], [f"psq{i}"])
    for i, (c0, c1) in enumerate(cts):
        c.ts("vector", rstd[:, c0:c1], psq[i][:, 0:c1 - c0], 1.0 / D, EPS, ALU.mult, ALU.add, [f"psq{i}"], ["rstd"])
    c.act(rstd[:], rstd[:], AF.Sqrt, ["rstd"], ["rstd"])
    c.recip(rstd[:], rstd[:], ["rstd"], ["rstd"])
    pc = c.ps("pc", [128, 512])
    for blk in range(NB):
        s_ = sq[blk % 2]; sn_ = f"sq{blk % 2}"
        c.act(s_[:, 0:NC], xc[:, blk, :], AF.Square, ["xc"], [sn_])
        c.mm(pc[:, 0:NC], ones[:], s_[:, 0:NC], blk == 0, blk == NB - 1, ["ones", sn_], ["pc"])
    c.ts("vector", rstc[:], pc[:, 0:NC], 1.0 / D, EPS, ALU.mult, ALU.add, ["pc"], ["rstc"])
    c.act(rstc[:], rstc[:], AF.Sqrt, ["rstc"], ["rstc"])
    c.recip(rstc[:], rstc[:], ["rstc"], ["rstc"])
    gm = c.sb("gm", [128, NB, 2])
    for r_ in range(2):
        c.ts("vector", gm[:, :, r_], mv[:, :, 2 * r_ + 1], 1.0, None, ALU.add, None, ["mv"], ["gm"])
        c.tt("vector", gm[:, :, r_], gm[:, :, r_], g0[:], ALU.mult, ["gm", "g0"], ["gm"])
    ub = [c.sb(f"ub{i}", [128, NT], BF16) for i in range(2)]
    ucb = c.sb("ucb", [128, NB, NC], BF16); tcx = c.sb("tcx", [128, NC])
    for blk in range(NB):
        y = ty[blk % 2]; yn = f"ty{blk % 2}"; u = ub[blk % 2]; un = f"ub{blk % 2}"
        c.tt("vector", y[:], xp[:, blk, :], rstd[:], ALU.mult, [f"xp{blk}", "rstd"], [yn])
        c.ts("gpsimd", u[:], y[:], gm[:, blk, 0:1], mv[:, blk, 0:1], ALU.mult, ALU.add, [yn, "gm", "mv"], [un])
        c.dma("sync", UT[blk], u[:], reads=[un], writes=[f"UT{blk}"])
        c.tt("vector", tcx[:], xc[:, blk, :], rstc[:], ALU.mult, ["xc", "rstc"], ["tcx"])
        c.ts("vector", ucb[:, blk, :], tcx[:], gm[:, blk, 1:2], mv[:, blk, 2:3], ALU.mult, ALU.add, ["tcx", "gm", "mv"], ["ucb"])
    c.dma("sync", UCT.rearrange("b p t -> p b t"), ucb[:], reads=["ucb"], writes=["UCT"])
    return c.done()


def build_layer(kind, stop=0):
    c = Ctx(); k = c.k
    H = 1 if kind == 0 else 9
    NT = 1024 + 2 * H
    cts = col_tiles(NT)
    XIN = c.din("XIN", [NB, 128, NT])
    MV = c.din("MV", [128, NB, 6]); NG = c.din("NG", [128, NB, 4])
    VM = c.din("VM", [128, NT])
    UP = c.din("UP", [D, 2 * DFF]); DOWN = c.din("DOWN", [DFF, D]); CONV = c.din("CONV", [128, 2 * NJ, 4])
    if kind == 0:
        YT = c.din("YT", [NB, 128, NT]); GLUW = c.din("GLUW", [D, 2 * D])
    else:
        POOLW = c.din("POOLW", [4, 512, 512]); PSC = c.din("PSC", [128, NB]); INVC = c.din("INVC", [128, 4, NT])
    XOT = c.dout("XOT", [NB, 128, 1024])

    XS = c.nc.dram_tensor("XS", [NB, 128, NT], F32, kind="Internal").ap()
    mv = c.sb("mv", [128, NB, 6]); ng = c.sb("ng", [128, NB, 4]); vm = c.sb("vm", [128, NT])
    conv = c.sb("conv", [128, 2 * NJ, 4]); ones = c.sb("ones", [128, 128])
    zu = c.sb("zu", [128, NB, NT], BF16)
    rstd = c.sb("rstd", [128, NT]); coef = c.sb("coef", [128, NB, 4])
    sq = [c.sb(f"sq{i}", [128, NT]) for i in range(2)]
    es1 = ExitStack()
    xp = c.sb("xp", [128, NB, NT], es=es1)
    c.memset("vector", ones[:], 1.0, ["ones"])
    for blk in range(NB):
        c.dma("sync", xp[:, blk, :], XIN[blk], writes=["xp"])
    for dst, src, nm in ((mv, MV, "mv"), (ng, NG, "ng"), (vm, VM, "vm"), (conv, CONV, "conv")):
        c.dma("sync", dst[:], src[:], writes=[nm])
    c.tt("vector", coef[:, :, 0], mv[:, :, 2], ng[:, :, 1], ALU.mult, ["mv", "ng"], ["coef"])
    c.ts("vector", coef[:, :, 1], mv[:, :, 4], 1.0, None, ALU.add, None, ["mv"], ["coef"])
    c.tt("vector", coef[:, :, 1], coef[:, :, 1], ng[:, :, 2], ALU.mult, ["coef", "ng"], ["coef"])
    c.tt("vector", coef[:, :, 2], mv[:, :, 5], ng[:, :, 3], ALU.mult, ["mv", "ng"], ["coef"])
    c.ts("vector", coef[:, :, 3], mv[:, :, 1], 1.0, None, ALU.add, None, ["mv"], ["coef"])
    c.tt("vector", coef[:, :, 3], coef[:, :, 3], ng[:, :, 0], ALU.mult, ["coef", "ng"], ["coef"])

    psq = [c.ps(f"psq{i}", [128, 512]) for i in range(3)]
    pa = [c.ps(f"pa{i}", [128, 512]) for i in range(4)]
    yv = c.sb("yv", [128, NB, NT], BF16, es=es1)
    if kind == 0:
        zf = zu
        yst = [c.sb(f"yst{i}", [128, NT], es=es1) for i in range(2)]
        for blk in range(NB):
            y = yst[blk % 2]; yn = f"yst{blk % 2}"
            c.dma("sync", y[:], YT[blk], writes=[yn])
            c.act(zf[:, blk, :], y[:], AF.Gelu_apprx_tanh, [yn], ["zu"])
        wg = [c.sb(f"wg{i}", [128, 16, 2, 256], BF16, es=es1) for i in range(2)]
        sg = [c.sb(f"sg{i}", [128, 512], es=es1) for i in range(2)]
        n = 0
        for i in range(NB):
            w = wg[(i // 2) % 2]; wn = f"wg{(i // 2) % 2}"; wc0 = 128 * (i % 2)
            if i % 2 == 0:
                for h in range(2):
                    c.dma("gpsimd", w[:, :, h, :], GLUW[:, D * h + 128 * i:D * h + 128 * i + 256].rearrange("(kc p) n -> p kc n", p=128), writes=[wn])
            for (c0, c1) in cts:
                pv = pa[(2 * n) % 4]; pvn = f"pa{(2 * n) % 4}"; pg_ = pa[(2 * n + 1) % 4]; pgn = f"pa{(2 * n + 1) % 4}"
                s_ = sg[n % 2]; sn_ = f"sg{n % 2}"; n += 1
                c.mmg(pv[:, 0:c1 - c0], [(w[:, kc, 0, wc0:wc0 + 128], zf[:, kc, c0:c1]) for kc in range(16)], [wn, "zu"], [pvn])
                c.mmg(pg_[:, 0:c1 - c0], [(w[:, kc, 1, wc0:wc0 + 128], zf[:, kc, c0:c1]) for kc in range(16)], [wn, "zu"], [pgn])
                c.act(s_[:, 0:c1 - c0], pg_[:, 0:c1 - c0], AF.Sigmoid, [pgn], [sn_])
                c.tt("vector", yv[:, i, c0:c1], pv[:, 0:c1 - c0], s_[:, 0:c1 - c0], ALU.mult, [pvn, sn_], ["yv"])
    else:
        pp = zu
        invc = c.sb("invc", [128, NT], es=es1); psc = c.sb("psc", [128, NB], es=es1)
        c.dma("sync", psc[:], PSC[:], writes=["psc"])
        rms_rstd(c, xp, "xp", NT, rstd, "rstd", ones, sq, psq)
        W = NT + 32
        ua = [c.sb(f"ua{i}", [128, W], es=es1) for i in range(3)]
        for i in range(3):
            c.memset("vector", ua[i][:], 0.0, [f"ua{i}"])
        ut = c.sb("ut", [128, NT], es=es1)
        for blk in range(NB):
            m = 1 + blk // 4
            if blk % 4 == 0:
                c.dma("sync", invc[:], INVC[:, m - 1, :], writes=["invc"])
            c.tt("vector", ut[:], xp[:, blk, :], rstd[:, 0:NT], ALU.mult, ["xp", "rstd"], ["ut"])
            c.ts("vector", ut[:], ut[:], coef[:, blk, 3:4], mv[:, blk, 0:1], ALU.mult, ALU.add, ["ut", "coef", "mv"], ["ut"])
            c.tt("vector", ua[0][:, 16:16 + NT], ut[:], vm[:], ALU.mult, ["ut", "vm"], ["ua0"])
            cur = 0
            for lvl in range(m):
                sh = 1 << lvl
                nxt = 1 + (lvl % 2)
                c.tt("vector", ua[nxt][:, 16:W], ua[cur][:, 16:W], ua[cur][:, 16 - sh:W - sh], ALU.add, [f"ua{cur}"], [f"ua{nxt}"])
                cur = nxt
            w2 = (1 << m) // 2
            off = 16 + w2 - 1
            c.tt("vector", ut[:], ua[cur][:, off:off + NT], invc[:], ALU.mult, [f"ua{cur}", "invc"], ["ut"])
            c.tt("vector", pp[:, blk, :], ut[:], ua[0][:, 16:16 + NT], ALU.subtract, ["ut", "ua0"], ["zu"])
        wp = [c.sb(f"wp{i}", [128, 4, 128], BF16, es=es1) for i in range(2)]
        n = 0
        for gi in range(4):
            for bo in range(4):
                w = wp[(4 * gi + bo) % 2]; wn = f"wp{(4 * gi + bo) % 2}"
                c.dma("gpsimd", w[:], POOLW[gi, :, 128 * bo:128 * bo + 128].rearrange("(kc p) n -> p kc n", p=128), writes=[wn])
                for (c0, c1) in cts:
                    p = pa[n % 4]; pn = f"pa{n % 4}"; n += 1
                    c.mmg(p[:, 0:c1 - c0], [(w[:, kc, :], pp[:, 4 * gi + kc, c0:c1]) for kc in range(4)], [wn, "zu"], [pn])
                    c.ts("vector", yv[:, 4 * gi + bo, c0:c1], p[:, 0:c1 - c0], psc[:, 4 * gi + bo:4 * gi + bo + 1], None, ALU.mult, None, [pn, "psc"], ["yv"])
    rms_rstd(c, yv, "yv", NT, rstd, "rstd", ones, sq, psq)
    for blk in range(NB):
        s_ = sq[blk % 2]; sn_ = f"sq{blk % 2}"
        c.stt(s_[:, 0:NT], yv[:, blk, :], coef[:, blk, 0:1], rstd[:, 0:NT], ALU.mult, ALU.mult, ["yv", "coef", "rstd"], [sn_])
        c.tt("vector", xp[:, blk, :], xp[:, blk, :], s_[:, 0:NT], ALU.add, ["xp", sn_], ["xp"])

    if stop == 1:
        for blk in range(NB):
            c.dma("sync", XOT[blk], xp[:, blk, H:H + 1024], reads=["xp"], writes=[f"XOT{blk}"])
        k.barrier()
        es1.close()
        return c.done()
    u2 = zu
    rms_rstd(c, xp, "xp", NT, rstd, "rstd", ones, sq, psq)
    for blk in range(NB):
        s_ = sq[blk % 2]; sn_ = f"sq{blk % 2}"
        c.tt("vector", s_[:, 0:NT], xp[:, blk, :], rstd[:, 0:NT], ALU.mult, ["xp", "rstd"], [sn_])
        c.ts("vector", s_[:, 0:NT], s_[:, 0:NT], coef[:, blk, 1:2], mv[:, blk, 3:4], ALU.mult, ALU.add, [sn_, "coef", "mv"], [sn_])
        c.tt("vector", u2[:, blk, :], s_[:, 0:NT], vm[:], ALU.mult, [sn_, "vm"], ["zu"])
        c.dma("sync", XS[blk], xp[:, blk, :], reads=["xp"], writes=[f"XS{blk}"])
    k.barrier()
    es1.close()
    es2 = ExitStack()
    A = c.sb("A", [128, NJ, NT], BF16, es=es2)
    es3 = ExitStack()
    wu = [c.sb(f"wu{i}", [128, 16, 2, 256], BF16, es=es3) for i in range(2)]
    hs = [[c.sb(f"hs{i}{h}", [128, NT + 2], es=es3) for h in range(2)] for i in range(2)]
    hc = [[c.sb(f"hc{i}{h}", [128, NT], es=es3) for h in range(2)] for i in range(2)]
    for i in range(2):
        for h in range(2):
            c.memset("vector", hs[i][h][:], 0.0, [f"hs{i}{h}"])
    n = 0
    for j in range(NJ):
        w = wu[(j // 2) % 2]; wn = f"wu{(j // 2) % 2}"; wc0 = 128 * (j % 2)
        if j % 2 == 0:
            for h in range(2):
                c.dma("gpsimd", w[:, :, h, :], UP[:, DFF * h + 128 * j:DFF * h + 128 * j + 256].rearrange("(kc p) n -> p kc n", p=128), writes=[wn])
        for h in range(2):
            hsb = hs[j % 2][h]; hsn = f"hs{j % 2}{h}"; hcb = hc[j % 2][h]; hcn = f"hc{j % 2}{h}"
            cb = NJ * h + j
            for (c0, c1) in cts:
                p = pa[n % 4]; pn = f"pa{n % 4}"; n += 1
                c.mmg(p[:, 0:c1 - c0], [(w[:, kc, h, wc0:wc0 + 128], u2[:, kc, c0:c1]) for kc in range(16)], [wn, "zu"], [pn])
                c.cp("scalar", hsb[:, 1 + c0:1 + c1], p[:, 0:c1 - c0], [pn], [hsn])
            c.act(hcb[:], hsb[:, 1:NT + 1], AF.Identity, [hsn, "conv"], [hcn], scale=conv[:, cb, 1:2], bias=conv[:, cb, 3:4])
            c.stt(hcb[:], hsb[:, 0:NT], conv[:, cb, 0:1], hcb[:], ALU.mult, ALU.add, [hsn, "conv", hcn], [hcn])
            c.stt(hcb[:], hsb[:, 2:NT + 2], conv[:, cb, 2:3], hcb[:], ALU.mult, ALU.add, [hsn, "conv", hcn], [hcn])
        hv = hc[j % 2][0]; hg = hc[j % 2][1]
        c.act(hg[:], hg[:], AF.Silu, [f"hc{j % 2}1"], [f"hc{j % 2}1"])
        c.tt("vector", A[:, j, :], hv[:], hg[:], ALU.mult, [f"hc{j % 2}0", f"hc{j % 2}1"], ["A"])
    k.barrier()
    es3.close()
    if stop == 3:
        xl = [c.sb(f"xl{i}", [128, NT]) for i in range(2)]
        for blk in range(NB):
            c.cp("vector", xl[blk % 2][:], A[:, blk, :], ["A"], [f"xl{blk % 2}"])
            c.dma("sync", XOT[blk], xl[blk % 2][:, H:H + 1024], reads=[f"xl{blk % 2}"], writes=[f"XOT{blk}"])
        k.barrier()
        c.es.pop_all().close() if False else None
        nc_ = c.k
        c.k.finish(); c.k.run(); c.k.close()
        return c.nc
    es4 = ExitStack()
    wd = [c.sb(f"wd{i}", [128, NJ, 256], BF16, es=es4) for i in range(2)]
    F = zu
    xl = [c.sb(f"xl{i}", [128, NT], es=es4) for i in range(2)]
    n = 0
    ctd = [(H, H + 512), (H + 512, H + 1024)]
    for blk in range(NB):
        w = wd[(blk // 2) % 2]; wn = f"wd{(blk // 2) % 2}"; wc0 = 128 * (blk % 2)
        if blk % 2 == 0:
            for jq in range(4):
                c.dma("gpsimd", w[:, 11 * jq:11 * jq + 11, :],
                      DOWN[1408 * jq:1408 * jq + 1408, 128 * blk:128 * blk + 256].rearrange("(j p) n -> p j n", p=128), writes=[wn])
        s_ = sq[blk % 2]; sn_ = f"sq{blk % 2}"
        for i, (c0, c1) in enumerate(ctd):
            p = pa[n % 4]; pn = f"pa{n % 4}"; n += 1
            c.mmg(p[:, 0:c1 - c0], [(w[:, j, wc0:wc0 + 128], A[:, j, c0:c1]) for j in range(NJ)], [wn, "A"], [pn])
            c.cp("vector", F[:, blk, c0:c1], p[:, 0:c1 - c0], [pn], ["zu"])
            c.act(s_[:, c0:c1], F[:, blk, c0:c1], AF.Square, ["zu"], [sn_])
        for i, (c0, c1) in enumerate(ctd):
            c.mm(psq[i][:, 0:c1 - c0], ones[:], s_[:, c0:c1], blk == 0, blk == NB - 1, ["ones", sn_], [f"psq{i}"])
    for i, (c0, c1) in enumerate(ctd):
        c.ts("vector", rstd[:, c0:c1], psq[i][:, 0:c1 - c0], 1.0 / D, EPS, ALU.mult, ALU.add, [f"psq{i}"], ["rstd"])
    c.act(rstd[:, H:H + 1024], rstd[:, H:H + 1024], AF.Sqrt, ["rstd"], ["rstd"])
    c.recip(rstd[:, H:H + 1024], rstd[:, H:H + 1024], ["rstd"], ["rstd"])
    for blk in range(NB):
        s_ = sq[blk % 2]; sn_ = f"sq{blk % 2}"
        x_ = xl[blk % 2]; xn_ = f"xl{blk % 2}"
        c.dma("sync", x_[:], XS[blk], reads=[f"XS{blk}"], writes=[xn_])
        c.stt(s_[:, H:H + 1024], F[:, blk, H:H + 1024], coef[:, blk, 2:3], rstd[:, H:H + 1024], ALU.mult, ALU.mult, ["zu", "coef", "rstd"], [sn_])
        c.tt("vector", s_[:, H:H + 1024], x_[:, H:H + 1024], s_[:, H:H + 1024], ALU.add, [xn_, sn_], [sn_])
        c.dma("sync", XOT[blk], s_[:, H:H + 1024], reads=[sn_], writes=[f"XOT{blk}"])
    k.barrier()
    es4.close(); es2.close()
    return c.done()


_PROGS = {}


def _prog(name, fn, *a):
    key = (name,) + a
    if key not in _PROGS:
        _PROGS[key] = fn(*a)
    return _PROGS[key]


def _fm(a):
    return np.ascontiguousarray(a.T.reshape(NB, 128, a.shape[0]))


def _unfm(a):
    return a.reshape(D, a.shape[2]).T


def _pervec(v):
    return np.ascontiguousarray(v.reshape(NB, 128).T)


def _halo(a, q, h):
    out = np.zeros((1024 + 2 * h, a.shape[1]), a.dtype)
    lo = 1024 * q - h; hi = 1024 * q + 1024 + h
    s0 = max(lo, 0); s1 = min(hi, L)
    out[s0 - lo:s1 - lo] = a[s0:s1]
    return out


def kernel(x, c, ctx, c_ctx, ada_w, ada_b, norm_g, s5_lam_re, s5_lam_im, s5_log_step,
           s5_b_re, s5_b_im, s5_c_re, s5_c_im, s5_d, s5_glu_w, pool_w, pool_scale,
           ffn_up, ffn_conv, ffn_conv_b, ffn_down, _dbg=None):
    f32 = np.float32
    inp = dict(s5_lam_re=np.asarray(s5_lam_re, f32), s5_lam_im=np.asarray(s5_lam_im, f32), s5_log_step=np.asarray(s5_log_step, f32),
               s5_b_re=np.asarray(s5_b_re, f32), s5_b_im=np.asarray(s5_b_im, f32), s5_c_re=np.asarray(s5_c_re, f32),
               s5_c_im=np.asarray(s5_c_im, f32), s5_d=np.asarray(s5_d, f32))
    x = np.asarray(x, f32); c = np.asarray(c, f32); ctx = np.asarray(ctx, f32); c_ctx = np.asarray(c_ctx, f32)
    ada_w = np.asarray(ada_w, f32); ada_b = np.asarray(ada_b, f32); norm_g = np.asarray(norm_g, f32)
    cores = list(range(8))
    cnd = np.zeros((128, 16, 4), f32)
    for r, v in enumerate((c[0], c[1], c_ctx)):
        cnd[:, :, r] = v.reshape(16, 128).T
    maps = []
    for i in cores:
        cs = slice(1536 * i, 1536 * i + 1536)
        maps.append({"CND": cnd, "AW": np.ascontiguousarray(ada_w[:, :, cs]),
                     "AB": np.ascontiguousarray(ada_b[:, cs].reshape(2, 12, 128).transpose(0, 2, 1))})
    res = run_bass_kernel_spmd(_prog("mod", build_mod), maps, core_ids=cores).results
    modfull = np.zeros((2, 4, 12288), f32)
    for i in cores:
        m = res[i]["MOD"]
        modfull[:, :, 1536 * i:1536 * i + 1536] = m.transpose(0, 3, 2, 1).reshape(2, 4, 1536)
    def modv(l, r):
        return np.ascontiguousarray(modfull[l, r].reshape(6, NB, 128).transpose(2, 1, 0))
    if _dbg is not None:
        _dbg["mod"] = modfull
    maps = []
    jidx = (np.arange(4, dtype=f32)[None, :] * 128 + np.arange(128, dtype=f32)[:, None]).astype(f32)
    for i in cores:
        b, q = divmod(i, 4)
        t = np.arange(1024 * q, 1024 * q + 1024)
        mvb = modv(0, b); mvc = modv(0, 2)
        mv = np.stack([mvb[:, :, 0], mvb[:, :, 1], mvc[:, :, 0], mvc[:, :, 1]], 2)
        maps.append({"XT": _fm(x[b, 1024 * q:1024 * q + 1024]), "CT": _fm(ctx[b, 64 * q:64 * q + 64]),
                     "MV": np.ascontiguousarray(mv), "G0": _pervec(norm_g[0, 0]),
                     "RIDX": np.broadcast_to((t // 64).astype(f32)[None], (128, 1024)).copy(),
                     "CIDX": np.broadcast_to((t % 64).astype(f32)[None], (128, 1024)).copy(), "JIDX": jidx})
    res = run_bass_kernel_spmd(_prog("prep", build_prep), maps, core_ids=cores).results
    xp = np.zeros((2, L, D), f32); u = np.zeros((2, L, D), ml_dtypes.bfloat16); uc = np.zeros((2, NCTX, D), ml_dtypes.bfloat16)
    for i in cores:
        b, q = divmod(i, 4)
        xp[b, 1024 * q:1024 * q + 1024] = _unfm(res[i]["XPT"])
        u[b, 1024 * q:1024 * q + 1024] = _unfm(res[i]["UT"])
        uc[b, 64 * q:64 * q + 64] = _unfm(res[i]["UCT"])
    if _dbg is not None:
        _dbg["xp"] = xp; _dbg["u"] = u; _dbg["uc"] = uc
    res = run_bass_kernel_spmd(_prog("s5", build_s5), s5_in_maps(u, uc, inp), core_ids=cores).results
    ys5 = s5_gather(res)
    if _dbg is not None:
        _dbg["ys5"] = ys5
    xcur = xp
    for l in range(2):
        h = 1 if l == 0 else 9
        nt = 1024 + 2 * h
        cv = np.zeros((128, 2 * NJ, 4), f32)
        for hh in range(2):
            for j in range(NJ):
                n0 = DFF * hh + 128 * j
                cv[:, NJ * hh + j, 0:3] = ffn_conv[l][:, n0:n0 + 128].T
                cv[:, NJ * hh + j, 3] = ffn_conv_b[l][n0:n0 + 128]
        ngl = np.ascontiguousarray(np.asarray(norm_g[l], f32).reshape(4, NB, 128).transpose(2, 1, 0))
        maps = []
        for i in cores:
            b, q = divmod(i, 4)
            tg = np.arange(1024 * q - h, 1024 * q + 1024 + h)
            valid = ((tg >= 0) & (tg < L)).astype(f32)
            m = {"XIN": _fm(_halo(xcur[b], q, h)), "MV": modv(l, b), "NG": ngl,
                 "VM": np.broadcast_to(valid[None], (128, nt)).copy(),
                 "UP": np.asarray(ffn_up[l], f32), "DOWN": np.asarray(ffn_down[l], f32), "CONV": cv}
            if l == 0:
                m["YT"] = _fm(_halo(ys5[b], q, h)); m["GLUW"] = np.asarray(s5_glu_w[0], f32)
            else:
                m["POOLW"] = np.asarray(pool_w[0], f32); m["PSC"] = _pervec(np.asarray(pool_scale[0], f32))
                invc = np.ones((4, nt), f32)
                for mi, w in enumerate((2, 4, 8, 16)):
                    lo = np.clip(tg - w // 2, 0, L - 1); hi = np.clip(tg + w // 2 - 1, 0, L - 1)
                    invc[mi] = 1.0 / np.maximum(hi - lo + 1, 1)
                m["INVC"] = np.broadcast_to(invc[None], (128, 4, nt)).copy()
            maps.append(m)
        res = run_bass_kernel_spmd(_prog("layer", build_layer, l), maps, core_ids=cores).results
        xn = np.zeros((2, L, D), f32)
        for i in cores:
            b, q = divmod(i, 4)
            xn[b, 1024 * q:1024 * q + 1024] = _unfm(res[i]["XOT"])
        xcur = xn
        if _dbg is not None:
            _dbg[f"xout{l}"] = xn
    return xcur
```
